# Optimizing a Trainium2 kernel written in Bass

```python
import jax, jax.numpy as jnp
from jax import lax
import numpy as np

D_MODEL = 1024
BATCH = 4
SEQ = 4096
DEPTH = 2

N_MIXERS = 2
NORM_EPS = 1e-6
D_RNN = 128 * round(4 * D_MODEL / (3 * 128))
LRU_BLOCKS = 16
LRU_BLOCK = D_RNN // LRU_BLOCKS
LRU_C = 8.0
CONV_WIDTH = 4
FOX_HEADS = 16
FOX_HEAD_DIM = D_MODEL // FOX_HEADS
QUERY_BLOCK = 128
PEER_HEADS = 8
PEER_KEYS = 128
PEER_EXPERTS = PEER_KEYS * PEER_KEYS
PEER_TOPK = 16
PEER_QDIM = 256
PEER_HALF = PEER_QDIM // 2
TOKEN_CHUNK = 128

kernel_name = 'hybrid_rglru_fox_peer_adaln'


def rms_norm(x, g):
    xf = x.astype(jnp.float32)
    y = xf * lax.rsqrt(jnp.mean(xf * xf, axis=-1, keepdims=True) + NORM_EPS)
    return (y * g.astype(jnp.float32)).astype(x.dtype)


def ada_modulation(c, w, b):
    mod = jax.nn.silu(c) @ w + b
    shift, scale, gate = jnp.split(mod, 3, axis=-1)
    return shift[:, None, :], scale[:, None, :], gate[:, None, :]


def causal_depthwise_conv(x, w, b):
    y = lax.conv_general_dilated(x, w[:, None, :], window_strides=(1,), padding=[(CONV_WIDTH - 1, 0)], dimension_numbers=('NWC', 'WIO', 'NWC'), feature_group_count=x.shape[-1])
    return y + b


def _linear_recurrence_combine(left, right):
    a_l, b_l = left
    a_r, b_r = right
    return a_l * a_r, a_r * b_l + b_r


def rglru_mixer(h, in_w, conv_w, conv_b, ra_w, ra_b, ri_w, ri_b, lam, out_w):
    B, S, _ = h.shape
    xb, gb = jnp.split(h @ in_w, 2, axis=-1)
    xb = causal_depthwise_conv(xb, conv_w, conv_b)
    xblk = xb.reshape(B, S, LRU_BLOCKS, LRU_BLOCK)
    r = jax.nn.sigmoid(jnp.einsum('bsnc,ncd->bsnd', xblk, ra_w).reshape(B, S, D_RNN) + ra_b)
    i = jax.nn.sigmoid(jnp.einsum('bsnc,ncd->bsnd', xblk, ri_w).reshape(B, S, D_RNN) + ri_b)
    log_a = -LRU_C * jax.nn.softplus(-lam.astype(jnp.float32)) * r.astype(jnp.float32)
    a = jnp.exp(log_a)
    u = jnp.sqrt(-jnp.expm1(2.0 * log_a)) * (i * xb).astype(jnp.float32)
    _, hs = lax.associative_scan(_linear_recurrence_combine, (a, u), axis=1)
    y = hs.astype(h.dtype) * jax.nn.gelu(gb, approximate=False)
    return y @ out_w


def fox_mixer(h, in_w, f_b, q_norm_g, k_norm_g, out_w):
    B, S, D = h.shape
    H, E = FOX_HEADS, FOX_HEAD_DIM
    n_blk = S // QUERY_BLOCK
    q, k, v, f_logit, og = jnp.split(h @ in_w, [D, 2 * D, 3 * D, 3 * D + H], axis=-1)
    q = rms_norm(q.reshape(B, S, H, E), q_norm_g)
    k = rms_norm(k.reshape(B, S, H, E), k_norm_g)
    v = v.reshape(B, S, H, E)
    log_f = jax.nn.log_sigmoid(f_logit.astype(jnp.float32) + f_b.astype(jnp.float32))
    cum = jnp.cumsum(log_f, axis=1).transpose(0, 2, 1)
    q_blocks = q.reshape(B, n_blk, QUERY_BLOCK, H, E).transpose(1, 0, 3, 2, 4)
    cum_blocks = cum.reshape(B, H, n_blk, QUERY_BLOCK).transpose(2, 0, 1, 3)
    pos_blocks = jnp.arange(S, dtype=jnp.int32).reshape(n_blk, QUERY_BLOCK)
    key_pos = jnp.arange(S, dtype=jnp.int32)
    scale = E ** -0.5

    def attend_block(args):
        qb, cq, qpos = args
        s = jnp.einsum('bhqe,bkhe->bhqk', qb, k).astype(jnp.float32) * scale
        s = s + cq[..., None] - cum[:, :, None, :]
        s = jnp.where(qpos[:, None] >= key_pos[None, :], s, -jnp.inf)
        p = jax.nn.softmax(s, axis=-1)
        return jnp.einsum('bhqk,bkhe->bqhe', p.astype(v.dtype), v)

    o = lax.map(attend_block, (q_blocks, cum_blocks, pos_blocks))
    o = o.transpose(1, 0, 2, 3, 4).reshape(B, S, H, E)
    o = o * jax.nn.sigmoid(og).reshape(B, S, H, E)
    return o.reshape(B, S, D) @ out_w


def peer_ffn(h, q_w, subkey1, subkey2, expert_u, expert_v):
    B, S, D = h.shape
    chunks = h.reshape(B * S // TOKEN_CHUNK, TOKEN_CHUNK, D)

    def retrieve_chunk(xc):
        q = (xc @ q_w).reshape(TOKEN_CHUNK, PEER_HEADS, 2, PEER_HALF)
        s1 = jnp.einsum('thd,hnd->thn', q[:, :, 0], subkey1).astype(jnp.float32)
        s2 = jnp.einsum('thd,hnd->thn', q[:, :, 1], subkey2).astype(jnp.float32)
        v1, i1 = lax.top_k(s1, PEER_TOPK)
        v2, i2 = lax.top_k(s2, PEER_TOPK)
        cand = (v1[..., :, None] + v2[..., None, :]).reshape(TOKEN_CHUNK, PEER_HEADS, PEER_TOPK * PEER_TOPK)
        top_s, top_c = lax.top_k(cand, PEER_TOPK)
        e1 = jnp.take_along_axis(i1, top_c // PEER_TOPK, axis=-1)
        e2 = jnp.take_along_axis(i2, top_c % PEER_TOPK, axis=-1)
        expert = e1 * PEER_KEYS + e2
        g = jax.nn.softmax(top_s, axis=-1)
        u = jnp.take(expert_u, expert, axis=0)
        act = jax.nn.gelu(jnp.einsum('thkd,td->thk', u, xc).astype(jnp.float32), approximate=False)
        w = (g * act).astype(xc.dtype)
        return jnp.einsum('thk,thkd->td', w, jnp.take(expert_v, expert, axis=0))

    return lax.map(retrieve_chunk, chunks).reshape(B, S, D)


def _mod_params(nrm):
    return nrm((D_MODEL, 3 * D_MODEL), 0.5 * D_MODEL ** -0.5), nrm((3 * D_MODEL,), 0.02)


def setup_inputs(seed: int = 0) -> dict:
    key = jax.random.key(seed)
    ks = iter(jax.random.split(key, 64))

    def nrm(shape, scale):
        return jax.random.normal(next(ks), shape, jnp.float32) * scale

    def gain(n):
        return 1.0 + nrm((n,), 0.05)

    def peer_params():
        return (nrm((D_MODEL, PEER_HEADS * PEER_QDIM), D_MODEL ** -0.5),
                nrm((PEER_HEADS, PEER_KEYS, PEER_HALF), PEER_HALF ** -0.5),
                nrm((PEER_HEADS, PEER_KEYS, PEER_HALF), PEER_HALF ** -0.5),
                nrm((PEER_EXPERTS, D_MODEL), D_MODEL ** -0.5),
                nrm((PEER_EXPERTS, D_MODEL), PEER_HEADS ** -0.5))

    x = nrm((BATCH, SEQ, D_MODEL), 1.0)
    c = nrm((BATCH, D_MODEL), 1.0)
    l0_mix_norm_g = gain(D_MODEL)
    l0_mix_mod_w, l0_mix_mod_b = _mod_params(nrm)
    l0_lru_in_w = nrm((D_MODEL, 2 * D_RNN), D_MODEL ** -0.5)
    l0_lru_conv_w = nrm((CONV_WIDTH, D_RNN), CONV_WIDTH ** -0.5)
    l0_lru_conv_b = nrm((D_RNN,), 0.02)
    l0_lru_ra_w = nrm((LRU_BLOCKS, LRU_BLOCK, LRU_BLOCK), LRU_BLOCK ** -0.5)
    l0_lru_ra_b = nrm((D_RNN,), 0.02)
    l0_lru_ri_w = nrm((LRU_BLOCKS, LRU_BLOCK, LRU_BLOCK), LRU_BLOCK ** -0.5)
    l0_lru_ri_b = nrm((D_RNN,), 0.02)
    a_pow_c = jax.random.uniform(next(ks), (D_RNN,), jnp.float32, minval=0.9, maxval=0.999)
    a0 = a_pow_c ** (1.0 / LRU_C)
    l0_lru_lambda = jnp.log(a0) - jnp.log1p(-a0)
    l0_lru_out_w = nrm((D_RNN, D_MODEL), D_RNN ** -0.5)
    l0_ffn_norm_g = gain(D_MODEL)
    l0_ffn_mod_w, l0_ffn_mod_b = _mod_params(nrm)
    l0_peer_q_w, l0_peer_subkey1, l0_peer_subkey2, l0_peer_u, l0_peer_v = peer_params()
    l1_mix_norm_g = gain(D_MODEL)
    l1_mix_mod_w, l1_mix_mod_b = _mod_params(nrm)
    l1_fox_in_w = nrm((D_MODEL, 4 * D_MODEL + FOX_HEADS), D_MODEL ** -0.5)
    l1_fox_f_b = jax.random.uniform(next(ks), (FOX_HEADS,), jnp.float32, minval=1.0, maxval=6.0)
    l1_fox_q_norm_g = gain(FOX_HEAD_DIM)
    l1_fox_k_norm_g = gain(FOX_HEAD_DIM)
    l1_fox_out_w = nrm((D_MODEL, D_MODEL), D_MODEL ** -0.5)
    l1_ffn_norm_g = gain(D_MODEL)
    l1_ffn_mod_w, l1_ffn_mod_b = _mod_params(nrm)
    l1_peer_q_w, l1_peer_subkey1, l1_peer_subkey2, l1_peer_u, l1_peer_v = peer_params()
    return {'x': x, 'c': c,
            'l0_mix_norm_g': l0_mix_norm_g, 'l0_mix_mod_w': l0_mix_mod_w, 'l0_mix_mod_b': l0_mix_mod_b,
            'l0_lru_in_w': l0_lru_in_w, 'l0_lru_conv_w': l0_lru_conv_w, 'l0_lru_conv_b': l0_lru_conv_b,
            'l0_lru_ra_w': l0_lru_ra_w, 'l0_lru_ra_b': l0_lru_ra_b, 'l0_lru_ri_w': l0_lru_ri_w, 'l0_lru_ri_b': l0_lru_ri_b,
            'l0_lru_lambda': l0_lru_lambda, 'l0_lru_out_w': l0_lru_out_w,
            'l0_ffn_norm_g': l0_ffn_norm_g, 'l0_ffn_mod_w': l0_ffn_mod_w, 'l0_ffn_mod_b': l0_ffn_mod_b,
            'l0_peer_q_w': l0_peer_q_w, 'l0_peer_subkey1': l0_peer_subkey1, 'l0_peer_subkey2': l0_peer_subkey2,
            'l0_peer_u': l0_peer_u, 'l0_peer_v': l0_peer_v,
            'l1_mix_norm_g': l1_mix_norm_g, 'l1_mix_mod_w': l1_mix_mod_w, 'l1_mix_mod_b': l1_mix_mod_b,
            'l1_fox_in_w': l1_fox_in_w, 'l1_fox_f_b': l1_fox_f_b, 'l1_fox_q_norm_g': l1_fox_q_norm_g,
            'l1_fox_k_norm_g': l1_fox_k_norm_g, 'l1_fox_out_w': l1_fox_out_w,
            'l1_ffn_norm_g': l1_ffn_norm_g, 'l1_ffn_mod_w': l1_ffn_mod_w, 'l1_ffn_mod_b': l1_ffn_mod_b,
            'l1_peer_q_w': l1_peer_q_w, 'l1_peer_subkey1': l1_peer_subkey1, 'l1_peer_subkey2': l1_peer_subkey2,
            'l1_peer_u': l1_peer_u, 'l1_peer_v': l1_peer_v}


def reference(x, c,
              l0_mix_norm_g, l0_mix_mod_w, l0_mix_mod_b,
              l0_lru_in_w, l0_lru_conv_w, l0_lru_conv_b,
              l0_lru_ra_w, l0_lru_ra_b, l0_lru_ri_w, l0_lru_ri_b,
              l0_lru_lambda, l0_lru_out_w,
              l0_ffn_norm_g, l0_ffn_mod_w, l0_ffn_mod_b,
              l0_peer_q_w, l0_peer_subkey1, l0_peer_subkey2, l0_peer_u, l0_peer_v,
              l1_mix_norm_g, l1_mix_mod_w, l1_mix_mod_b,
              l1_fox_in_w, l1_fox_f_b, l1_fox_q_norm_g, l1_fox_k_norm_g, l1_fox_out_w,
              l1_ffn_norm_g, l1_ffn_mod_w, l1_ffn_mod_b,
              l1_peer_q_w, l1_peer_subkey1, l1_peer_subkey2, l1_peer_u, l1_peer_v):
    mix_norms = [(l0_mix_norm_g, l0_mix_mod_w, l0_mix_mod_b), (l1_mix_norm_g, l1_mix_mod_w, l1_mix_mod_b)]
    mixer_params = [(l0_lru_in_w, l0_lru_conv_w, l0_lru_conv_b, l0_lru_ra_w, l0_lru_ra_b, l0_lru_ri_w, l0_lru_ri_b, l0_lru_lambda, l0_lru_out_w),
                    (l1_fox_in_w, l1_fox_f_b, l1_fox_q_norm_g, l1_fox_k_norm_g, l1_fox_out_w)]
    ffn_norms = [(l0_ffn_norm_g, l0_ffn_mod_w, l0_ffn_mod_b), (l1_ffn_norm_g, l1_ffn_mod_w, l1_ffn_mod_b)]
    peer_params = [(l0_peer_q_w, l0_peer_subkey1, l0_peer_subkey2, l0_peer_u, l0_peer_v),
                   (l1_peer_q_w, l1_peer_subkey1, l1_peer_subkey2, l1_peer_u, l1_peer_v)]
    for layer in range(DEPTH):
        g, mw, mb = mix_norms[layer]
        shift, scale, gate = ada_modulation(c, mw, mb)
        h = rms_norm(x, g) * (1.0 + scale) + shift
        mixer = rglru_mixer if layer % N_MIXERS == 0 else fox_mixer
        x = x + gate * mixer(h, *mixer_params[layer])
        g, mw, mb = ffn_norms[layer]
        shift, scale, gate = ada_modulation(c, mw, mb)
        h = rms_norm(x, g) * (1.0 + scale) + shift
        x = x + gate * peer_ffn(h, *peer_params[layer])
    return x
```

```python
import numpy as np
from contextlib import ExitStack
import concourse.bass as bass
import concourse.mybir as mybir
from concourse.bass_utils import run_bass_kernel_spmd

F32 = mybir.dt.float32
BF16 = mybir.dt.bfloat16
AF = mybir.ActivationFunctionType
ALU = mybir.AluOpType
AX = mybir.AxisListType

NCORES = 8
T = 2048
G = 512
NG = T // G
D = 1024
DR = 1408
NB = 16
BS = 88
NE = 16384


class Buf:
    __slots__ = ("t", "name", "last_w", "readers", "dsem", "dcnt")

    def __init__(self, t, name):
        self.t = t
        self.name = name
        self.last_w = None
        self.readers = {}
        self.dsem = None
        self.dcnt = 0

    def __getitem__(self, k):
        return self.t[k]


class Prog:
    def __init__(self, nc, stack):
        self.nc = nc
        self.stack = stack
        self.eng = {"pe": nc.tensor, "act": nc.scalar, "dve": nc.vector,
                    "pool": nc.gpsimd, "sp": nc.sync}
        self.sem = {}
        self.cnt = {}
        for e in self.eng:
            self.sem[e] = stack.enter_context(nc.semaphore("s_" + e))
            self.cnt[e] = 0
        self.waited = {}
        self.nbuf = 0
        self.q = {e: [] for e in self.eng}
        self.nblk = 0
        self.live = []
        self.sem_pool = []
        self.ccsem = stack.enter_context(nc.semaphore("s_cc"))
        self.cccnt = 0

    def sb(self, shape, dt=F32, name=None, stack=None):
        self.nbuf += 1
        name = (name or "b") + f"_{self.nbuf}"
        t = (stack or self.stack).enter_context(self.nc.sbuf_tensor(name, list(shape), dt))
        b = Buf(t, name)
        self.live.append(b)
        return b

    def recycle(self):
        for b in self.live:
            if b.dsem is not None:
                self.sem_pool.append((b.dsem, b.dcnt))
                b.dsem = None
        self.live = []

    def coll(self, kind, groups, src, dst):
        self._waits("pool", [src], [dst])
        self.q["pool"].append(("op", "collective_compute", (kind, ALU.bypass),
                               dict(replica_groups=groups, ins=[src.t.opt()], outs=[dst.t.opt()]),
                               (self.ccsem, None)))
        self.cccnt += 1
        tok = (self.ccsem, self.cccnt)
        src.readers[tok[0]] = max(src.readers.get(tok[0], 0), tok[1])
        dst.last_w = tok
        dst.readers = {}

    def ps(self, shape, dt=F32, name=None, stack=None):
        self.nbuf += 1
        name = (name or "p") + f"_{self.nbuf}"
        t = (stack or self.stack).enter_context(self.nc.psum_tensor(name, list(shape), dt))
        return Buf(t, name)

    def dram(self, name, shape, dt=F32, kind="Internal"):
        t = self.nc.dram_tensor(name, list(shape), dt, kind=kind)
        return Buf(t.ap(), name)

    def _waits(self, e, reads, writes):
        need = {}
        for b in reads:
            if b.last_w is not None:
                s, v = b.last_w
                need[s] = max(need.get(s, 0), v)
        for b in writes:
            if b.last_w is not None:
                s, v = b.last_w
                need[s] = max(need.get(s, 0), v)
            for s, v in b.readers.items():
                if s is self.sem.get(e):
                    continue
                need[s] = max(need.get(s, 0), v)
        for s, v in need.items():
            key = (e, id(s))
            if s is self.sem.get(e) and (e == "pe" or v > self.cnt[e]):
                continue
            if self.waited.get(key, 0) < v:
                self.q[e].append(("wait", s, v))
                self.waited[key] = v

    def op(self, e, meth, reads, writes, *args, inc=True, **kw):
        self._waits(e, reads, writes)
        self.q[e].append(("op", meth, args, kw, (self.sem[e], 1) if inc else None))
        if inc:
            self.cnt[e] += 1
            tok = (self.sem[e], self.cnt[e])
        else:
            tok = (self.sem[e], self.cnt[e] + 1)
        for b in reads:
            b.readers[tok[0]] = max(b.readers.get(tok[0], 0), tok[1])
        for b in writes:
            b.last_w = tok
            b.readers = {}

    def dma(self, q, out_ap, in_ap, dst, src, **kw):
        self._waits(q, [src], [dst])
        if dst.dsem is None:
            if self.sem_pool:
                dst.dsem, dst.dcnt = self.sem_pool.pop()
            else:
                dst.dsem = self.stack.enter_context(self.nc.semaphore("d_" + dst.name))
        kw = dict(kw, out=out_ap, in_=in_ap)
        self.q[q].append(("op", "dma_start", (), kw, (dst.dsem, 16)))
        dst.dcnt += 16
        tok = (dst.dsem, dst.dcnt)
        src.readers[tok[0]] = max(src.readers.get(tok[0], 0), tok[1])
        dst.last_w = tok
        dst.readers = {}

    def flush(self, final_bufs=()):
        for b in final_bufs:
            if b.last_w is not None:
                s, v = b.last_w
                self.q["sp"].append(("wait", s, v))
        nc = self.nc
        if not any(self.q.values()):
            return
        self.nblk += 1
        with nc.Block() as block:
            for e, starter in (("sp", block.sync), ("pe", block.tensor), ("act", block.scalar),
                               ("dve", block.vector), ("pool", block.gpsimd)):
                items = self.q[e]
                if not items:
                    continue

                def body(eng, items=items):
                    for it in items:
                        if it[0] == "wait":
                            eng.wait_ge(it[1], it[2])
                        else:
                            _, meth, args, kw, inc = it
                            ins = getattr(eng, meth)(*args, **kw)
                            if inc is not None:
                                if inc[1] is None:
                                    ins.then_inc(inc[0])
                                else:
                                    ins.then_inc(inc[0], inc[1])
                starter(body)
        self.q = {e: [] for e in self.eng}


class Rot:
    def __init__(self, bufs):
        self.bufs = bufs
        self.i = 0

    def next(self):
        b = self.bufs[self.i % len(self.bufs)]
        self.i += 1
        return b


def din(nc, name, shape, dt=F32):
    return Buf(nc.dram_tensor(name, list(shape), dt, kind="ExternalInput").ap(), name)


def dout(nc, name, shape, dt=F32):
    return Buf(nc.dram_tensor(name, list(shape), dt, kind="ExternalOutput").ap(), name)


def emit_consts(P):
    C = {}
    C["ones"] = P.sb([128, 128], F32, "ones")
    P.op("pool", "memset", [], [C["ones"]], C["ones"][:], 1.0)
    C["zb"] = P.sb([128, 512], BF16, "zb")
    P.op("pool", "memset", [], [C["zb"]], C["zb"][:], 0.0)
    return C


def emit_mod(P, ps_small, c_d, g_d, w_d, b_d, pfx, pst=None):
    A = P.sb([128, 8], F32, pfx + "A", pst)
    mod = P.sb([128, 24], F32, pfx + "mod", pst)
    with ExitStack() as st:
        cs = P.sb([128, 8], F32, "cs", st)
        gs = P.sb([128, 8], F32, "gs", st)
        bs = P.sb([128, 24], F32, "bs", st)
        sc = P.sb([128, 8], F32, "sc", st)
        P.dma("sp", cs[:], c_d[:], cs, c_d)
        P.dma("sp", gs[:], g_d[:], gs, g_d)
        P.dma("sp", bs[:], b_d[:], bs, b_d)
        P.op("act", "activation", [cs], [sc], out=sc[:], in_=cs[:], func=AF.Silu)
        wrot = Rot([P.sb([128, 8, 512], F32, f"wb{i}", st) for i in range(2)])
        modp = ps_small
        for s in range(6):
            wb = wrot.next()
            P.dma("sp", wb[:], w_d[:, :, s * 512:(s + 1) * 512], wb, w_d)
            for j in range(4):
                col = s * 4 + j
                for k in range(8):
                    P.op("pe", "matmul", [wb, sc], [modp], modp[:, col:col + 1],
                         lhsT=wb[:, k, j * 128:(j + 1) * 128], rhs=sc[:, k:k + 1],
                         start=(k == 0), stop=(k == 7), inc=(k == 7))
        P.op("dve", "tensor_tensor", [modp, bs], [mod], out=mod[:], in0=modp[:, 0:24], in1=bs[:], op=ALU.add)
        P.op("dve", "scalar_tensor_tensor", [mod, gs], [A], out=A[:], in0=mod[:, 8:16], scalar=1.0,
             in1=gs[:], op0=ALU.add, op1=ALU.mult)
        P.flush()
    return A, mod


def emit_norm(P, C, xg, A, mod, hT, hT_bf, sqrot, ssp, rstd):
    for k in range(8):
        s_ = sqrot.next()
        P.op("act", "activation", [xg], [s_], out=s_[:], in_=xg[:, k, :], func=AF.Square)
        P.op("pe", "matmul", [C["ones"], s_], [ssp], ssp[:], lhsT=C["ones"][:], rhs=s_[:],
             start=(k == 0), stop=(k == 7))
    P.op("dve", "tensor_scalar", [ssp], [rstd], out=rstd[:], in0=ssp[:], scalar1=1.0 / D, scalar2=1e-6,
         op0=ALU.mult, op1=ALU.add)
    P.op("act", "activation", [rstd], [rstd], out=rstd[:], in_=rstd[:], func=AF.Sqrt)
    P.op("dve", "reciprocal", [rstd], [rstd], out=rstd[:], in_=rstd[:])
    for k in range(8):
        P.op("dve", "tensor_tensor", [xg, rstd], [hT], out=hT[:, k, :], in0=xg[:, k, :], in1=rstd[:], op=ALU.mult)
        P.op("act", "activation", [hT, A, mod], [hT], out=hT[:, k, :], in_=hT[:, k, :], func=AF.Identity,
             scale=A[:, k:k + 1], bias=mod[:, k:k + 1])
        if hT_bf is not None:
            P.op("pool", "tensor_copy", [hT], [hT_bf], out=hT_bf[:, k, :], in_=hT[:, k, :])


def xview(x_d, g):
    return x_d.t.rearrange("(k p) t -> p k t", p=128)[:, :, g * G:(g + 1) * G]


def emit_lru(P, C, PS, W, xin_d, xprev_d, xout_d):
    with ExitStack() as st:
        A, mod = emit_mod(P, PS["A"][0], W["c"], W["mix_g"], W["mix_mod_w"], W["mix_mod_b"], "lm", st)
        cw = P.sb([128, NB, 4], F32, "cw", st)
        tabs = {}
        for nm in ("conv_b", "ra_b", "ri_b", "lam"):
            tabs[nm] = P.sb([128, NB], F32, nm, st)
            P.dma("sp", tabs[nm][0:BS, :], W[nm][:], tabs[nm], W[nm])
        P.dma("sp", cw[0:BS, :, :], W["conv_w"][:], cw, W["conv_w"])
        flag = P.sb([128, 1], F32, "flag", st)
        P.dma("sp", flag[:], W["flag"][:], flag, W["flag"])
        raw = P.sb([128, NB, BS], F32, "raw", st)
        riw = P.sb([128, NB, BS], F32, "riw", st)
        P.dma("sp", raw[0:BS, :, :], W["ra_w"].t.rearrange("n c d -> c n d"), raw, W["ra_w"])
        P.dma("sp", riw[0:BS, :, :], W["ri_w"].t.rearrange("n c d -> c n d"), riw, W["ri_w"])
        nls = P.sb([128, NB], F32, "nls", st)
        nls2 = P.sb([128, NB], F32, "nls2", st)
        lam = tabs["lam"]
        P.op("act", "activation", [lam], [nls], out=nls[0:BS, :], in_=lam[0:BS, :], func=AF.Exp, scale=-1.0)
        P.op("act", "activation", [nls], [nls], out=nls[0:BS, :], in_=nls[0:BS, :], func=AF.Ln, bias=1.0, scale=1.0)
        P.op("dve", "tensor_scalar", [nls], [nls2], out=nls2[0:BS, :], in0=nls[0:BS, :], scalar1=-16.0, scalar2=None,
             op0=ALU.mult)
        P.op("dve", "tensor_scalar", [nls], [nls], out=nls[0:BS, :], in0=nls[0:BS, :], scalar1=-8.0, scalar2=None,
             op0=ALU.mult)
        hist = P.sb([128, NB, 3], F32, "hist", st)
        carry = P.sb([128, NB], F32, "carry", st)
        P.op("pool", "memset", [], [hist], hist[:], 0.0)
        P.op("pool", "memset", [], [carry], carry[:], 0.0)

        xg = P.sb([128, 8, G], F32, "xg", st)
        hT = P.sb([128, 8, G], F32, "hT", st)
        hTb = P.sb([128, 8, G], BF16, "hTb", st)
        sqrot = Rot([P.sb([128, G], F32, f"sq{i}", st) for i in range(2)])
        rstd = P.sb([128, G], F32, "rstd", st)
        wxr = Rot([P.sb([128, 8, BS], BF16, f"wx{i}", st) for i in range(3)])
        wgr = Rot([P.sb([128, 8, BS], BF16, f"wg{i}", st) for i in range(3)])
        xbr = Rot([P.sb([128, G + 3], F32, f"xbuf{i}", st) for i in range(2)])

        def tmp(nm, n=2):
            return Rot([P.sb([128, G], F32, f"{nm}{i}", st) for i in range(n)])
        cvr, rr, ir, ar, a2r, ur, hsr, glr = [tmp(n) for n in ("cv", "r", "i", "a", "a2", "u", "hs", "gl")]
        yT = P.sb([128, NB, G], BF16, "yT", st)
        owr = Rot([P.sb([128, NB, 128], BF16, f"ow{i}", st) for i in range(3)])
        psA, psO, psW = Rot(PS["A"]), Rot(PS["O"]), Rot(PS["W"])

        for step in range(2 * NG):
            main = step >= NG
            g = step % NG
            src = xin_d if main else xprev_d
            P.dma("sp", xg[:], xview(src, g), xg, src)
            emit_norm(P, C, xg, A, mod, hT, hTb, sqrot, PS["S"], rstd)
            for n in range(NB):
                wx = wxr.next()
                P.dma("pool", wx[:], W["in_w"][n], wx, W["in_w"])
                xb_ps = psA.next()
                for k in range(8):
                    P.op("pe", "matmul", [wx, hTb], [xb_ps], xb_ps[0:BS, :], lhsT=wx[:, k, :], rhs=hTb[:, k, :],
                         start=(k == 0), stop=(k == 7), inc=(k == 7))
                if main:
                    wg = wgr.next()
                    P.dma("pool", wg[:], W["in_w"][NB + n], wg, W["in_w"])
                    gb_ps = psO.next()
                    for k in range(8):
                        P.op("pe", "matmul", [wg, hTb], [gb_ps], gb_ps[0:BS, :], lhsT=wg[:, k, :], rhs=hTb[:, k, :],
                             start=(k == 0), stop=(k == 7), inc=(k == 7))
                xb = xbr.next()
                P.op("act", "activation", [xb_ps], [xb], out=xb[0:BS, 3:G + 3], in_=xb_ps[0:BS, :], func=AF.Copy)
                P.op("dve", "tensor_copy", [hist], [xb], out=xb[0:BS, 0:3], in_=hist[0:BS, n, :])
                P.op("dve", "tensor_copy", [xb], [hist], out=hist[0:BS, n, :], in_=xb[0:BS, G:G + 3])
                cv = cvr.next()
                P.op("act", "activation", [xb, cw, tabs["conv_b"]], [cv], out=cv[0:BS, :], in_=xb[0:BS, 3:G + 3],
                     func=AF.Identity, scale=cw[0:BS, n, 3:4], bias=tabs["conv_b"][0:BS, n:n + 1])
                for k in range(3):
                    P.op("dve", "scalar_tensor_tensor", [xb, cw, cv], [cv], out=cv[0:BS, :], in0=xb[0:BS, k:k + G],
                         scalar=cw[0:BS, n, k:k + 1], in1=cv[0:BS, :], op0=ALU.mult, op1=ALU.add)
                pw = psW.next()
                r_ps, i_ps = pw[0:BS, 0:G], pw[0:BS, G:2 * G]
                P.op("pe", "matmul", [raw, cv], [pw], r_ps, lhsT=raw[0:BS, n, :], rhs=cv[0:BS, :], start=True, stop=True,
                     inc=False)
                P.op("pe", "matmul", [riw, cv], [pw], i_ps, lhsT=riw[0:BS, n, :], rhs=cv[0:BS, :], start=True, stop=True)
                r, i_, a, a2, u, hs = rr.next(), ir.next(), ar.next(), a2r.next(), ur.next(), hsr.next()
                P.op("act", "activation", [pw, tabs["ra_b"]], [r], out=r[0:BS, :], in_=r_ps, func=AF.Sigmoid,
                     bias=tabs["ra_b"][0:BS, n:n + 1], scale=1.0)
                P.op("act", "activation", [pw, tabs["ri_b"]], [i_], out=i_[0:BS, :], in_=i_ps, func=AF.Sigmoid,
                     bias=tabs["ri_b"][0:BS, n:n + 1], scale=1.0)
                P.op("act", "activation", [r, nls], [a], out=a[0:BS, :], in_=r[0:BS, :], func=AF.Exp,
                     scale=nls[0:BS, n:n + 1])
                P.op("act", "activation", [r, nls2], [a2], out=a2[0:BS, :], in_=r[0:BS, :], func=AF.Exp,
                     scale=nls2[0:BS, n:n + 1])
                P.op("dve", "tensor_scalar", [a2], [a2], out=a2[0:BS, :], in0=a2[0:BS, :], scalar1=1.0, scalar2=-1.0,
                     op0=ALU.min, op1=ALU.mult)
                P.op("act", "activation", [a2], [a2], out=a2[0:BS, :], in_=a2[0:BS, :], func=AF.Sqrt, bias=1.0,
                     scale=1.0)
                P.op("dve", "tensor_tensor", [i_, cv], [u], out=u[0:BS, :], in0=i_[0:BS, :], in1=cv[0:BS, :], op=ALU.mult)
                P.op("dve", "tensor_tensor", [u, a2], [u], out=u[0:BS, :], in0=u[0:BS, :], in1=a2[0:BS, :], op=ALU.mult)
                P.op("dve", "tensor_tensor_scan", [a, u, carry], [hs], out=hs[0:BS, :], data0=a[0:BS, :],
                     data1=u[0:BS, :], initial=carry[0:BS, n:n + 1], op0=ALU.mult, op1=ALU.add)
                P.op("dve", "tensor_copy", [hs], [carry], out=carry[0:BS, n:n + 1], in_=hs[0:BS, G - 1:G])
                if main:
                    gl = glr.next()
                    P.op("act", "activation", [gb_ps], [gl], out=gl[0:BS, :], in_=gb_ps[0:BS, :], func=AF.Gelu)
                    P.op("pool", "tensor_tensor", [hs, gl], [yT], out=yT[0:BS, n, :], in0=hs[0:BS, :], in1=gl[0:BS, :],
                         op=ALU.mult)
            if not main and g == NG - 1:
                P.op("dve", "tensor_scalar", [hist, flag], [hist], out=hist[0:BS, :, :], in0=hist[0:BS, :, :],
                     scalar1=flag[0:BS, 0:1], scalar2=None, op0=ALU.mult)
                P.op("dve", "tensor_scalar", [carry, flag], [carry], out=carry[0:BS, :], in0=carry[0:BS, :],
                     scalar1=flag[0:BS, 0:1], scalar2=None, op0=ALU.mult)
            if main:
                for dc in range(8):
                    ow = owr.next()
                    P.dma("pool", ow[0:BS, :, :], W["out_w"][dc], ow, W["out_w"])
                    o_ps = psO.next()
                    for n in range(NB):
                        P.op("pe", "matmul", [ow, yT], [o_ps], o_ps[:], lhsT=ow[0:BS, n, :], rhs=yT[0:BS, n, :],
                             start=(n == 0), stop=(n == NB - 1), inc=(n == NB - 1))
                    P.op("dve", "scalar_tensor_tensor", [o_ps, mod, xg], [xg], out=xg[:, dc, :], in0=o_ps[:],
                         scalar=mod[:, 16 + dc:17 + dc], in1=xg[:, dc, :], op0=ALU.mult, op1=ALU.add)
                P.dma("sp", xview(xout_d, g), xg[:], xout_d, xg)
        P.flush()
        P.recycle()


def emit_peer(P, C, PS, W, xin_d, xout_d, pfx):
    nc = P.nc
    with ExitStack() as st:
        A, mod = emit_mod(P, PS["A"][0], W["c"], W["ffn_g"], W["ffn_mod_w"], W["ffn_mod_b"], pfx + "pm", st)
        wf_d = P.dram(pfx + "wf", [128, 8, 2048], F32)
        with ExitStack() as s2:
            qbr = Rot([P.sb([128, D], F32, f"qb{i}", s2) for i in range(2)])
            skr = Rot([P.sb([128, 128], F32, f"sk{i}", s2) for i in range(2)])
            wfr = Rot([P.sb([128, 8, 128], F32, f"wfs{i}", s2) for i in range(2)])
            psW = Rot(PS["W"])
            for blk in range(16):
                qb, sk, wfs, pw = qbr.next(), skr.next(), wfr.next(), psW.next()
                P.dma("sp", qb[:], W["qwT"][blk], qb, W["qwT"])
                P.dma("sp", sk[:], W["skT"][blk], sk, W["skT"])
                for dc in range(8):
                    P.op("pe", "matmul", [qb, sk], [pw], pw[:, dc * 128:(dc + 1) * 128],
                         lhsT=qb[:, dc * 128:(dc + 1) * 128], rhs=sk[:], start=True, stop=True, inc=(dc == 7))
                P.op("act", "activation", [pw], [wfs], out=wfs[:].rearrange("p a b -> p (a b)"), in_=pw[:], func=AF.Copy)
                P.dma("sp", wf_d[:, :, blk * 128:(blk + 1) * 128], wfs[:], wf_d, wfs)
            P.flush()

        identf = P.sb([128, 128], F32, "identf", st)
        ident = P.sb([128, 128], BF16, "ident", st)
        P.dma("sp", identf[:], W["ident"][:], identf, W["ident"])
        P.op("dve", "tensor_copy", [identf], [ident], out=ident[:], in_=identf[:])

        xg = P.sb([128, 8, G], F32, "xg", st)
        hT = P.sb([128, 8, G], F32, "hT", st)
        hTb = P.sb([128, 8, G], BF16, "hTb", st)
        sqrot = Rot([P.sb([128, G], F32, f"sq{i}", st) for i in range(2)])
        rstd = P.sb([128, G], F32, "rstd", st)
        s_sb = [P.sb([128, 16, 128], F32, f"s_sb{i}", st) for i in range(4)]
        wfpr = Rot([P.sb([128, 8, 128], F32, f"wfp{i}", st) for i in range(2)])
        vtop = P.sb([128, 16, 16], F32, "vtop", st)
        tmp128 = P.sb([128, 128], F32, "tmp128", st)
        candr = Rot([P.sb([128, 256], F32, f"cand{i}", st) for i in range(2)])
        ctmp = P.sb([128, 256], F32, "ctmp", st)
        ctmp2 = P.sb([128, 256], F32, "ctmp2", st)
        ttop = P.sb([128, 8, 24], F32, "ttop", st)
        etop = P.sb([128, 16, 16], F32, "etop", st)
        mneg = P.sb([128, 8], F32, "mneg", st)
        Zs = P.sb([128, 8], F32, "Zs", st)
        tmid = P.sb([128, 8], F32, "tmid", st)
        thr = [P.sb([128, 8], F32, f"thr{i}", st) for i in range(4)]
        Er = Rot([P.sb([128, 8, 128], F32, f"E{i}", st) for i in range(3)])
        Mr = Rot([P.sb([128, 8, 128], BF16, f"M{i}", st) for i in range(3)])
        Kr = Rot([P.sb([128, 8, 128], BF16, f"K{i}", st) for i in range(3)])
        dg = P.sb([128, 4, 8, 128], BF16, "dg", st)
        Wtr = Rot([P.sb([128, 8, 128], BF16, f"Wt{i}", st) for i in range(2)])
        wTs = [P.sb([128, 8, G], BF16, f"wT{i}", st) for i in range(2)]
        Ur = Rot([P.sb([128, 8, 128], BF16, f"U{i}", st) for i in range(3)])
        NV = 16
        Vr = [P.sb([128, D], BF16, f"V{i}", st) for i in range(NV)]
        WAr = [P.sb([128, G], BF16, f"WA{i}", st) for i in range(NV)]
        Gr = Rot([P.sb([128, G], BF16, f"G{i}", st) for i in range(2)])
        otr = Rot([P.sb([128, G], F32, f"ot{i}", st) for i in range(2)])
        psA, psO = Rot(PS["A"]), Rot(PS["O"])
        wacc, wtp = PS["W"][0], PS["W"][1]
        ev = 0
        ACT_HEADS = ()
        RELU_HEADS = (1, 5)
        DVE_HEADS = (0, 2, 4, 6)

        for g in range(NG):
            P.dma("sp", xg[:], xview(xin_d, g), xg, xin_d)
            emit_norm(P, C, xg, A, mod, hT, hTb, sqrot, PS["S"], rstd)
            for ns in range(16):
                wfp = wfpr.next()
                P.dma("sp", wfp[:], wf_d[:, :, ns * 128:(ns + 1) * 128], wfp, wf_d)
                for tt in range(4):
                    ps = psA.next()
                    for k in range(8):
                        P.op("pe", "matmul", [hT, wfp], [ps], ps[:, 0:128], lhsT=hT[:, k, tt * 128:(tt + 1) * 128],
                             rhs=wfp[:, k, :], start=(k == 0), stop=(k == 7), inc=(k == 7))
                    dst = s_sb[tt][:, ns, :]
                    if ev % 2 == 0:
                        P.op("act", "activation", [ps], [s_sb[tt]], out=dst, in_=ps[:, 0:128], func=AF.Copy)
                    else:
                        P.op("dve", "tensor_copy", [ps], [s_sb[tt]], out=dst, in_=ps[:, 0:128])
                    ev += 1
            for tt in range(4):
                s_ = s_sb[tt]
                for blk in range(16):
                    P.op("dve", "max", [s_], [vtop], out=vtop[:, blk, 0:8], in_=s_[:, blk, :])
                    P.op("dve", "match_replace", [vtop, s_], [tmp128], out=tmp128[:], in_to_replace=vtop[:, blk, 0:8],
                         in_values=s_[:, blk, :], imm_value=-1e30)
                    P.op("dve", "max", [tmp128], [vtop], out=vtop[:, blk, 8:16], in_=tmp128[:])
                P.op("dve", "scalar_tensor_tensor", [vtop], [mneg], out=mneg[:], in0=vtop[:, 0:16:2, 0], scalar=-1.0,
                     in1=vtop[:, 1:16:2, 0], op0=ALU.mult, op1=ALU.subtract)
                for h in range(8):
                    P.op("act", "activation", [vtop, mneg], [etop], out=etop[:, 2 * h, :], in_=vtop[:, 2 * h, :],
                         func=AF.Exp, bias=mneg[:, h:h + 1], scale=1.0)
                    P.op("act", "activation", [s_, mneg], [s_], out=s_[:, 2 * h, :], in_=s_[:, 2 * h, :], func=AF.Exp,
                         bias=mneg[:, h:h + 1], scale=1.0)
                P.op("act", "activation", [vtop], [etop], out=etop[:, 1:16:2, :], in_=vtop[:, 1:16:2, :], func=AF.Exp)
                P.op("act", "activation", [s_], [s_], out=s_[:, 1:16:2, :], in_=s_[:, 1:16:2, :], func=AF.Exp)
                for h in range(8):
                    in0 = etop[:, 2 * h, :].unsqueeze(2).to_broadcast([128, 16, 16])
                    in1 = etop[:, 2 * h + 1, :].unsqueeze(1).to_broadcast([128, 16, 16])
                    cand = candr.next()
                    P.op("dve", "tensor_tensor", [etop], [cand],
                         out=cand[:].rearrange("p (a b) -> p a b", a=16), in0=in0, in1=in1, op=ALU.mult)
                    P.op("dve", "max", [cand], [ttop], out=ttop[:, h, 0:8], in_=cand[:])
                    P.op("dve", "match_replace", [ttop, cand], [ctmp], out=ctmp[:], in_to_replace=ttop[:, h, 0:8],
                         in_values=cand[:], imm_value=-1.0)
                    P.op("dve", "max", [ctmp], [ttop], out=ttop[:, h, 8:16], in_=ctmp[:])
                P.op("dve", "tensor_reduce", [ttop], [Zs], out=Zs[:], in_=ttop[:, :, 0:16], axis=AX.X, op=ALU.add)
                P.op("dve", "reciprocal", [Zs], [Zs], out=Zs[:], in_=Zs[:])
                P.op("dve", "scalar_tensor_tensor", [ttop, Zs], [thr[tt]], out=thr[tt][:], in0=ttop[:, :, 15],
                     scalar=1.0 - 2e-6, in1=Zs[:], op0=ALU.mult, op1=ALU.mult)
                for h in range(8):
                    P.op("pool", "tensor_scalar", [s_, Zs], [s_], out=s_[:, 2 * h, :], in0=s_[:, 2 * h, :],
                         scalar1=Zs[:, h:h + 1], scalar2=None, op0=ALU.mult)
                    P.op("pool", "tensor_scalar", [ident, thr[tt]], [dg], out=dg[:, tt, h, :], in0=ident[:],
                         scalar1=thr[tt][:, h:h + 1], scalar2=None, op0=ALU.mult)

            sched = []

            def at(t, fn):
                sched.append((t, len(sched), fn))

            NIC = 16
            for step in range(NIC + 2):
                for tt in range(4):
                    n0 = (step * 4 + tt) * 8
                    if step < NIC:
                        ic = step
                        s_ = s_sb[tt]
                        for h in range(8):
                            E, M = Er.next(), Mr.next()
                            in0 = s_[:, 2 * h, ic * 8:(ic + 1) * 8].unsqueeze(2).to_broadcast([128, 8, 128])
                            in1 = s_[:, 2 * h + 1, :].unsqueeze(1).to_broadcast([128, 8, 128])
                            eng = "dve" if h in DVE_HEADS else "pool"
                            at(n0 + h - 2, lambda eng=eng, s_=s_, E=E, in0=in0, in1=in1: P.op(
                                eng, "tensor_tensor", [s_], [E], out=E[:], in0=in0, in1=in1, op=ALU.mult))
                            if h in RELU_HEADS:
                                Kb = Kr.next()
                                at(n0 + h - 1, lambda E=E, M=M, tt=tt, h=h: P.op(
                                    "dve", "tensor_scalar", [E, thr[tt]], [M], out=M[:], in0=E[:],
                                    scalar1=thr[tt][:, h:h + 1], scalar2=0.0, op0=ALU.subtract, op1=ALU.max))
                                at(n0 + h, lambda M=M, Kb=Kb: P.op("act", "activation", [M], [Kb], out=Kb[:], in_=M[:],
                                                                    func=AF.Sign))
                            else:
                                Kb = None
                                at(n0 + h - 1, lambda E=E, M=M, tt=tt, h=h: P.op(
                                    "dve", "scalar_tensor_tensor", [E, thr[tt]], [M], out=M[:], in0=E[:],
                                    scalar=thr[tt][:, h:h + 1], in1=E[:], op0=ALU.is_ge, op1=ALU.mult))

                            def acc(M=M, Kb=Kb, h=h, tt=tt):
                                for hb in range(2):
                                    P.op("pe", "matmul", [ident, M], [wacc], wacc[:, hb * 512:(hb + 1) * 512], lhsT=ident[:],
                                         rhs=M[:, hb * 4:(hb + 1) * 4, :].rearrange("p a b -> p (a b)"), start=(h == 0),
                                         stop=(h == 7 and Kb is None), inc=(hb == 1 and Kb is None))
                                if Kb is not None:
                                    for hb in range(2):
                                        P.op("pe", "matmul", [dg, Kb], [wacc], wacc[:, hb * 512:(hb + 1) * 512],
                                             lhsT=dg[:, tt, h, :],
                                             rhs=Kb[:, hb * 4:(hb + 1) * 4, :].rearrange("p a b -> p (a b)"),
                                             start=False, stop=(h == 7), inc=(hb == 1))
                            at(n0 + h + 1, acc)
                        Wt = Wtr.next()
                        at(n0 + 9, lambda Wt=Wt: P.op("act", "activation", [wacc], [Wt],
                                                     out=Wt[:].rearrange("p a b -> p (a b)"), in_=wacc[:], func=AF.Copy))

                        def tr(Wt=Wt):
                            for i in range(8):
                                P.op("pe", "matmul", [Wt, ident], [wtp], wtp[:, i * 128:(i + 1) * 128], lhsT=Wt[:, i, :],
                                     rhs=ident[:], start=True, stop=True, inc=(i == 7))
                        at(n0 + 11, tr)
                        at(n0 + 12, lambda ic=ic, tt=tt: P.op(
                            "act", "activation", [wtp], [wTs[ic % 2]], out=wTs[ic % 2][:, :, tt * 128:(tt + 1) * 128],
                            in_=wtp[:].rearrange("p (a b) -> p a b", a=8), func=AF.Copy))
                    if 1 <= step <= NIC:
                        ic = step - 1
                        for q_, i in enumerate((2 * tt, 2 * tt + 1)):
                            e = ic * 8 + i
                            Vc, Uc, a_ps, Gt, WA = Vr[e % NV], Ur.next(), psA.next(), Gr.next(), WAr[e % NV]
                            ta = n0 + 4 * q_ + 3
                            at(max(step * 32, ta - 12), lambda Vc=Vc, e=e: P.dma("pool", Vc[:], W["V"][e * 128:(e + 1) * 128, :], Vc, W["V"]))
                            at(ta - 6, lambda Uc=Uc, e=e: P.dma("pool", Uc[:], W["UT"][e], Uc, W["UT"]))

                            def amm(Uc=Uc, a_ps=a_ps):
                                for k in range(8):
                                    P.op("pe", "matmul", [Uc, hTb], [a_ps], a_ps[:], lhsT=Uc[:, k, :], rhs=hTb[:, k, :],
                                         start=(k == 0), stop=(k == 7), inc=(k == 7))
                            at(ta, amm)
                            at(ta + 1, lambda a_ps=a_ps, Gt=Gt: P.op("act", "activation", [a_ps], [Gt], out=Gt[:],
                                                                      in_=a_ps[:], func=AF.Gelu))
                            at(ta + 2, lambda Gt=Gt, WA=WA, ic=ic, i=i: P.op(
                                "dve", "tensor_tensor", [Gt, wTs[ic % 2]], [WA], out=WA[:], in0=Gt[:],
                                in1=wTs[ic % 2][:, i, :], op=ALU.mult))
                    if 2 <= step <= NIC + 1:
                        ic = step - 2
                        for q_, dc in enumerate((2 * tt, 2 * tt + 1)):
                            o_ps, ot = psO.next(), otr.next()
                            tv = n0 + 4 * q_ + 2

                            def vmm(ic=ic, dc=dc, o_ps=o_ps):
                                for i in range(8):
                                    e = ic * 8 + i
                                    P.op("pe", "matmul", [Vr[e % NV], WAr[e % NV]], [o_ps], o_ps[:],
                                         lhsT=Vr[e % NV][:, dc * 128:(dc + 1) * 128], rhs=WAr[e % NV][:], start=(i == 0),
                                         stop=(i == 7), inc=(i == 7))
                            at(tv, vmm)
                            at(tv + 1, lambda o_ps=o_ps, ot=ot, dc=dc: P.op(
                                "act", "activation", [o_ps, mod], [ot], out=ot[:], in_=o_ps[:], func=AF.Copy,
                                scale=mod[:, 16 + dc:17 + dc]))
                            at(tv + 3, lambda ot=ot, dc=dc: P.op("pool", "tensor_tensor", [ot, xg], [xg], out=xg[:, dc, :],
                                                                 in0=ot[:], in1=xg[:, dc, :], op=ALU.add))
            sched.sort(key=lambda x: (x[0], x[1]))
            for _, _, fn in sched:
                fn()
            P.dma("sp", xview(xout_d, g), xg[:], xout_d, xg)
        P.flush()
        P.recycle()


def alloc_psum(P):
    PS = {}
    PS["A"] = [P.ps([128, 512], F32, f"psA{i}") for i in range(2)]
    PS["O"] = [P.ps([128, 512], F32, f"psO{i}") for i in range(2)]
    PS["W"] = [P.ps([128, 1024], F32, f"psW{i}") for i in range(2)]
    PS["S"] = PS["O"][0]
    return PS


SHAPES = {"c": [128, 8], "flag": [128, 1], "ident": [128, 128],
          "mix_g": [128, 8], "mix_mod_w": [128, 8, 3072], "mix_mod_b": [128, 24],
          "in_w": [2 * NB, 128, 8, BS], "conv_w": [BS, NB, 4], "conv_b": [BS, NB],
          "ra_w": [NB, BS, BS], "ra_b": [BS, NB], "ri_w": [NB, BS, BS], "ri_b": [BS, NB],
          "lam": [BS, NB], "out_w": [8, BS, NB, 128],
          "ffn_g": [128, 8], "ffn_mod_w": [128, 8, 3072], "ffn_mod_b": [128, 24],
          "qwT": [16, 128, D], "skT": [16, 128, 128], "UT": [128, 128, 8, 128], "V": [NE, D]}


class WMap(dict):
    GLOBAL = ("c", "flag", "ident", "bd", "maskneg")

    def __init__(self, nc, prefix="", shared=None):
        super().__init__()
        self.nc = nc
        self.prefix = prefix
        self.shared = shared if shared is not None else {}

    def __missing__(self, k):
        if k in self.GLOBAL:
            if k not in self.shared:
                self.shared[k] = din(self.nc, k, SHAPES[k])
            b = self.shared[k]
        else:
            b = din(self.nc, self.prefix + k, SHAPES[k])
        self[k] = b
        return b

    def names(self):
        return [(k if k in self.GLOBAL else self.prefix + k) for k in self.keys()]


def build_l0(do_lru=True, do_peer=True):
    nc = bass.Bass("TRN2", target_bir_lowering=False)
    W = WMap(nc)
    xin = din(nc, "xT", [D, T])
    xprev = din(nc, "xTp", [D, T]) if do_lru else None
    xout = dout(nc, "xo", [D, T])
    with ExitStack() as st:
        P = Prog(nc, st)
        PS = alloc_psum(P)
        C = emit_consts(P)
        if do_lru and do_peer:
            xmid = P.dram("xmid", [D, T], F32)
        else:
            xmid = xout
        if do_lru:
            emit_lru(P, C, PS, W, xin, xprev, xmid)
        if do_peer:
            emit_peer(P, C, PS, W, xmid if do_lru else xin, xout, "l0")
        P.flush([xout])
    nc.used_inputs = ["xT"] + (["xTp"] if do_lru else []) + W.names()
    return nc


def col8(v):
    return np.ascontiguousarray(v.reshape(-1, 128).T)


def blk16(v):
    return np.ascontiguousarray(v.reshape(NB, BS).T)


def lay_mod(g, w, b):
    return col8(g), np.ascontiguousarray(w.reshape(128, 8, 3072)), col8(b)


def lay_peer(q_w, sk1, sk2, u, v):
    qwT = np.ascontiguousarray(q_w.T.reshape(16, 128, D))
    skT = np.empty((16, 128, 128), np.float32)
    skT[0::2] = sk1.transpose(0, 2, 1)
    skT[1::2] = sk2.transpose(0, 2, 1)
    UT = np.ascontiguousarray(u.reshape(128, 128, 8, 128).transpose(0, 3, 2, 1))
    return qwT, skT, UT, np.ascontiguousarray(v)


def lay_l0(inp):
    d = {}
    d["mix_g"], d["mix_mod_w"], d["mix_mod_b"] = lay_mod(inp["l0_mix_norm_g"], inp["l0_mix_mod_w"], inp["l0_mix_mod_b"])
    d["ffn_g"], d["ffn_mod_w"], d["ffn_mod_b"] = lay_mod(inp["l0_ffn_norm_g"], inp["l0_ffn_mod_w"], inp["l0_ffn_mod_b"])
    iw = inp["l0_lru_in_w"].reshape(8, 128, 2 * NB, BS)
    d["in_w"] = np.ascontiguousarray(iw.transpose(2, 1, 0, 3))
    d["conv_w"] = np.ascontiguousarray(inp["l0_lru_conv_w"].reshape(4, NB, BS).transpose(2, 1, 0))
    d["conv_b"] = blk16(inp["l0_lru_conv_b"])
    d["ra_w"] = np.ascontiguousarray(inp["l0_lru_ra_w"])
    d["ri_w"] = np.ascontiguousarray(inp["l0_lru_ri_w"])
    d["ra_b"] = blk16(inp["l0_lru_ra_b"])
    d["ri_b"] = blk16(inp["l0_lru_ri_b"])
    d["lam"] = blk16(inp["l0_lru_lambda"])
    ow = inp["l0_lru_out_w"].reshape(NB, BS, 8, 128)
    d["out_w"] = np.ascontiguousarray(ow.transpose(2, 1, 0, 3))
    d["qwT"], d["skT"], d["UT"], d["V"] = lay_peer(inp["l0_peer_q_w"], inp["l0_peer_subkey1"], inp["l0_peer_subkey2"],
                                                  inp["l0_peer_u"], inp["l0_peer_v"])
    d["ident"] = np.eye(128, dtype=np.float32)
    return d


def core_maps_l0(inp, shared, cores):
    maps = []
    x = inp["x"]
    for c in cores:
        b, hf = c // 2, c % 2
        m = dict(shared)
        m["xT"] = np.ascontiguousarray(x[b, hf * T:(hf + 1) * T, :].T)
        m["xTp"] = np.ascontiguousarray(x[b, 0:T, :].T) if hf else np.zeros((D, T), np.float32)
        m["flag"] = np.full((128, 1), float(hf), np.float32)
        m["c"] = np.ascontiguousarray(inp["c"][b].reshape(128, 8))
        maps.append(m)
    return maps


SHAPES.update({"wqk": [16, 128, 8, 128], "wv": [2, 128, 8, 512], "wf": [128, 8, 16], "wog": [8, 128, 8, 128],
               "fb": [16, 1], "gq": [128, 1], "gk": [128, 1], "bd": [128, 128],
               "fox_out_w": [8, 128, 8, 128], "maskneg": [4, 128, 512]})
S4 = 4096
PAIRS = [[0, 1], [2, 3], [4, 5], [6, 7]]


def emit_fox_b(P, C, PS, W, xin, qk_o, v_o, lf_o, og_o):
    with ExitStack() as st:
        A, mod = emit_mod(P, PS["A"][0], W["c"], W["mix_g"], W["mix_mod_w"], W["mix_mod_b"], "fm", st)
        bd = P.sb([128, 128], F32, "bd", st)
        P.dma("sp", bd[:], W["bd"][:], bd, W["bd"])
        gq = P.sb([128, 1], F32, "gq", st)
        gk = P.sb([128, 1], F32, "gk", st)
        nfb = P.sb([16, 1], F32, "nfb", st)
        P.dma("sp", gq[:], W["gq"][:], gq, W["gq"])
        P.dma("sp", gk[:], W["gk"][:], gk, W["gk"])
        P.dma("sp", nfb[:], W["fb"][:], nfb, W["fb"])
        P.op("dve", "tensor_scalar", [nfb], [nfb], out=nfb[:], in0=nfb[:], scalar1=-1.0, scalar2=None, op0=ALU.mult)
        xg = P.sb([128, 8, G], F32, "xg", st)
        hT = P.sb([128, 8, G], F32, "hT", st)
        hTb = P.sb([128, 8, G], BF16, "hTb", st)
        sqrot = Rot([P.sb([128, G], F32, f"sq{i}", st) for i in range(2)])
        rstd = P.sb([128, G], F32, "rstd", st)
        wr = Rot([P.sb([128, 8, 128], BF16, f"w{i}", st) for i in range(4)])
        wvr = Rot([P.sb([128, 8, 512], BF16, f"wv{i}", st) for i in range(2)])
        wfs = P.sb([128, 8, 16], F32, "wfs", st)
        P.dma("sp", wfs[:], W["wf"][:], wfs, W["wf"])
        q2r = Rot([P.sb([128, G], F32, f"q2{i}", st) for i in range(2)])
        rsr = Rot([P.sb([128, G], F32, f"rs{i}", st) for i in range(2)])
        qnr = Rot([P.sb([128, G], F32, f"qn{i}", st) for i in range(3)])
        vsr = Rot([P.sb([128, 512], F32, f"vs{i}", st) for i in range(3)])
        lfr = Rot([P.sb([16, G], F32, f"lf{i}", st) for i in range(2)])
        psA, psO, psW = Rot(PS["A"]), Rot(PS["O"]), Rot(PS["W"])
        for g in range(NG):
            gs = slice(g * G, (g + 1) * G)
            P.dma("sp", xg[:], xview(xin, g), xg, xin)
            emit_norm(P, C, xg, A, mod, hT, hTb, sqrot, PS["S"], rstd)
            for j in range(16):
                w = wr.next()
                P.dma("pool", w[:], W["wqk"][j], w, W["wqk"])
                ps = psA.next()
                for k in range(8):
                    P.op("pe", "matmul", [w, hTb], [ps], ps[:], lhsT=w[:, k, :], rhs=hTb[:, k, :], start=(k == 0),
                         stop=(k == 7), inc=(k == 7))
                q2, rs, qn = q2r.next(), rsr.next(), qnr.next()
                P.op("act", "activation", [ps], [q2], out=q2[:], in_=ps[:], func=AF.Square)
                pw = psW.next()
                P.op("pe", "matmul", [bd, q2], [pw], pw[:, 0:G], lhsT=bd[:], rhs=q2[:], start=True, stop=True)
                P.op("dve", "tensor_scalar", [pw], [rs], out=rs[:], in0=pw[:, 0:G], scalar1=1.0 / 64, scalar2=1e-6,
                     op0=ALU.mult, op1=ALU.add)
                P.op("act", "activation", [rs], [rs], out=rs[:], in_=rs[:], func=AF.Sqrt)
                P.op("dve", "reciprocal", [rs], [rs], out=rs[:], in_=rs[:])
                gcol = gq if j < 8 else gk
                P.op("dve", "scalar_tensor_tensor", [ps, gcol, rs], [qn], out=qn[:], in0=ps[:], scalar=gcol[:, 0:1],
                     in1=rs[:], op0=ALU.mult, op1=ALU.mult)
                P.dma("sp", qk_o[j // 2][(j % 2) * 128:(j % 2 + 1) * 128, gs], qn[:], qk_o[j // 2], qn)
            for n2 in range(2):
                wv = wvr.next()
                P.dma("pool", wv[:], W["wv"][n2], wv, W["wv"])
                for tt in range(4):
                    ps = psO.next()
                    for k in range(8):
                        P.op("pe", "matmul", [wv, hTb], [ps], ps[:], lhsT=hTb[:, k, tt * 128:(tt + 1) * 128],
                             rhs=wv[:, k, :], start=(k == 0), stop=(k == 7), inc=(k == 7))
                    vs = vsr.next()
                    P.op("act", "activation", [ps], [vs], out=vs[:], in_=ps[:], func=AF.Copy)
                    P.dma("sp", v_o[g][tt * 128:(tt + 1) * 128, n2 * 512:(n2 + 1) * 512], vs[:], v_o[g], vs)
            ps = psA.next()
            for k in range(8):
                P.op("pe", "matmul", [wfs, hT], [ps], ps[0:16, :], lhsT=wfs[:, k, :], rhs=hT[:, k, :], start=(k == 0),
                     stop=(k == 7), inc=(k == 7))
            lf = lfr.next()
            P.op("act", "activation", [ps, nfb], [lf], out=lf[:], in_=ps[0:16, :], func=AF.Exp, scale=-1.0,
                 bias=nfb[:, 0:1])
            P.op("act", "activation", [lf], [lf], out=lf[:], in_=lf[:], func=AF.Ln, bias=1.0, scale=1.0)
            P.op("dve", "tensor_scalar", [lf], [lf], out=lf[:], in0=lf[:], scalar1=-1.0, scalar2=None, op0=ALU.mult)
            P.dma("sp", lf_o[:, gs], lf[:], lf_o, lf)
            for j in range(8):
                w = wr.next()
                P.dma("pool", w[:], W["wog"][j], w, W["wog"])
                ps = psA.next()
                for k in range(8):
                    P.op("pe", "matmul", [w, hTb], [ps], ps[:], lhsT=w[:, k, :], rhs=hTb[:, k, :], start=(k == 0),
                         stop=(k == 7), inc=(k == 7))
                qn = qnr.next()
                P.op("act", "activation", [ps], [qn], out=qn[:], in_=ps[:], func=AF.Sigmoid)
                P.dma("sp", og_o[j * 128:(j + 1) * 128, gs], qn[:], og_o, qn)
        P.flush()
        P.recycle()


def emit_blend(P, dst_ap, dst, c0_ap, c1_ap, srcbuf, t0, t1, flag, nflag, np_):
    P.dma("sp", t0[0:np_], c0_ap, t0, srcbuf)
    P.dma("act", t1[0:np_], c1_ap, t1, srcbuf)
    P.op("pool", "tensor_scalar", [t1, flag], [t1], out=t1[0:np_], in0=t1[0:np_], scalar1=flag[0:np_, 0:1], scalar2=None,
         op0=ALU.mult)
    P.op("dve", "scalar_tensor_tensor", [t0, nflag, t1], [dst], out=dst_ap, in0=t0[0:np_], scalar=nflag[0:np_, 0:1],
         in1=t1[0:np_], op0=ALU.mult, op1=ALU.add)


def emit_blend2(P, dst_ap, dst, c0_ap, b0, c1_ap, b1, t0, t1, flag, nflag, np_):
    P.dma("sp", t0[0:np_], c0_ap, t0, b0)
    P.dma("act", t1[0:np_], c1_ap, t1, b1)
    P.op("pool", "tensor_scalar", [t1, flag], [t1], out=t1[0:np_], in0=t1[0:np_], scalar1=flag[0:np_, 0:1], scalar2=None,
         op0=ALU.mult)
    P.op("dve", "scalar_tensor_tensor", [t0, nflag, t1], [dst], out=dst_ap, in0=t0[0:np_], scalar=nflag[0:np_, 0:1],
         in1=t1[0:np_], op0=ALU.mult, op1=ALU.add)


def emit_fox_c(P, C, PS, W, qk_g, v_g, lf_g, o_s):
    with ExitStack() as st:
        flag = P.sb([128, 1], F32, "flag", st)
        nflag = P.sb([128, 1], F32, "nflag", st)
        P.dma("sp", flag[:], W["flag"][:], flag, W["flag"])
        P.op("dve", "tensor_scalar", [flag], [nflag], out=nflag[:], in0=flag[:], scalar1=-1.0, scalar2=1.0,
             op0=ALU.mult, op1=ALU.add)
        masks = P.sb([128, 4, 512], F32, "masks", st)
        P.dma("sp", masks[:], W["maskneg"].t.rearrange("j p t -> p j t"), masks, W["maskneg"])
        ones8 = P.sb([8, S4], F32, "ones8", st)
        P.op("pool", "memset", [], [ones8], ones8[:], 1.0)
        lfa = P.sb([8, S4], F32, "lfa", st)
        lfb = P.sb([8, S4], F32, "lfb", st)
        t0 = P.sb([128, T], F32, "bt0", st)
        t1 = P.sb([128, T], F32, "bt1", st)
        for r in range(2):
            rs_ = slice(r * T, (r + 1) * T)
            emit_blend(P, lfa[:, rs_], lfa, lf_g[r * 16:r * 16 + 8, :], lf_g[r * 16 + 8:r * 16 + 16, :], lf_g, t0, t1,
                       flag, nflag, 8)
        P.op("dve", "tensor_tensor_scan", [ones8, lfa], [lfb], out=lfb[:], data0=ones8[:], data1=lfa[:], initial=0.0,
             op0=ALU.mult, op1=ALU.add)
        P.op("dve", "tensor_scalar", [lfb], [lfa], out=lfa[:], in0=lfb[:], scalar1=8.0, scalar2=None, op0=ALU.mult)
        P.op("dve", "tensor_scalar", [lfb], [lfb], out=lfb[:], in0=lfb[:], scalar1=-8.0, scalar2=None, op0=ALU.mult)
        cs8, ncs8 = lfa, lfb
        qar = Rot([P.sb([128, S4], F32, f"qa{i}", st) for i in range(2)])
        kar = Rot([P.sb([128, S4], F32, f"ka{i}", st) for i in range(2)])
        vf = P.sb([128, 16, 64], F32, "vf", st)
        var_ = Rot([P.sb([128, 32, 65], BF16, f"va{i}", st) for i in range(2)])
        ptr = Rot([P.sb([128, 512], BF16, f"pt{i}", st) for i in range(4)])
        tmr = Rot([P.sb([128, 512], F32, f"tm{i}", st) for i in range(2)])
        rden = P.sb([128, 512], F32, "rden", st)
        bcs = P.sb([64, 512], F32, "bcs", st)
        osr = Rot([P.sb([64, 512], F32, f"os{i}", st) for i in range(2)])
        psS = Rot([PS["A"][0], PS["A"][1], PS["W"][0]])
        psBC = PS["W"][1]
        psO = Rot(PS["O"])
        v3 = [vg.t.rearrange("(r kt p) c -> p r kt c", r=2, p=128) for vg in v_g]

        def load_head(h, qa, ka, va):
            for r in range(2):
                rs_ = slice(r * T, (r + 1) * T)

                def cand(base_j, hg):
                    j = base_j + hg * 4 + h // 2
                    off = r * 256 + (j % 2) * 128 + (h % 2) * 64
                    return qk_g[j // 2], qk_g[j // 2][off:off + 64, :]
                (bq0, aq0), (bq1, aq1) = cand(0, 0), cand(0, 1)
                emit_blend2(P, qa[0:64, rs_], qa, aq0, bq0, aq1, bq1, t0, t1, flag, nflag, 64)
                yield
                (bk0, ak0), (bk1, ak1) = cand(8, 0), cand(8, 1)
                emit_blend2(P, ka[0:64, rs_], ka, ak0, bk0, ak1, bk1, t0, t1, flag, nflag, 64)
                yield
                c0, c1 = h * 64, 512 + h * 64
                t1v = t1[:, 0:1024].rearrange("p (a b) -> p a b", a=16)
                for i in range(4):
                    P.dma("sp", vf[:, i * 4:(i + 1) * 4, :], v3[i][:, r, :, c0:c0 + 64], vf, v_g[i])
                    P.dma("act", t1v[:, i * 4:(i + 1) * 4, :], v3[i][:, r, :, c1:c1 + 64], t1, v_g[i])
                P.op("pool", "tensor_scalar", [t1, flag], [t1], out=t1[:, 0:1024], in0=t1[:, 0:1024], scalar1=flag[:, 0:1],
                     scalar2=None, op0=ALU.mult)
                P.op("dve", "scalar_tensor_tensor", [vf, nflag, t1], [va], out=va[:, r * 16:(r + 1) * 16, 0:64],
                     in0=vf[:], scalar=nflag[:, 0:1], in1=t1v, op0=ALU.mult, op1=ALU.add)
                yield
            P.dma("sp", qa[64:65, :], cs8[h:h + 1, :], qa, cs8)
            P.dma("sp", ka[65:66, :], ncs8[h:h + 1, :], ka, ncs8)
            P.dma("sp", qa[65:66, :], ones8[0:1, :], qa, ones8)
            P.dma("sp", ka[64:65, :], ones8[0:1, :], ka, ones8)
            P.op("pool", "memset", [], [va], va[:, :, 64:65], 1.0)
            yield

        bufs = [(qar.next(), kar.next(), var_.next()) for _ in range(8)]
        for _ in load_head(0, *bufs[0]):
            pass
        for h in range(8):
            qa, ka, va = bufs[h]
            nxt = load_head(h + 1, *bufs[h + 1]) if h + 1 < 8 else iter(())
            items = [(qc, kt) for qc in range(8) for kt in range(4 * qc + 4)]
            LAG = 2
            pend = []
            o_cur = {}

            def qk(qc, kt):
                qs = slice(qc * 512, (qc + 1) * 512)
                ps = psS.next()
                P.op("pe", "matmul", [ka, qa], [ps], ps[:, 0:512], lhsT=ka[0:66, kt * 128:(kt + 1) * 128], rhs=qa[0:66, qs],
                     start=True, stop=True)
                pt = ptr.next()
                if kt >= 4 * qc:
                    tm = tmr.next()
                    P.op("dve", "tensor_tensor", [ps, masks], [tm], out=tm[:], in0=ps[:, 0:512], in1=masks[:, kt - 4 * qc, :],
                         op=ALU.add)
                    P.op("act", "activation", [tm], [pt], out=pt[:], in_=tm[:], func=AF.Exp, scale=0.125)
                else:
                    P.op("act", "activation", [ps], [pt], out=pt[:], in_=ps[:, 0:512], func=AF.Exp, scale=0.125)
                return pt

            def pv(qc, kt, pt):
                nk = 4 * qc + 4
                if kt == 0:
                    o_cur[qc] = psO.next()
                o_ps = o_cur[qc]
                P.op("pe", "matmul", [va, pt], [o_ps], o_ps[0:65, :], lhsT=va[:, kt, :], rhs=pt[:], start=(kt == 0),
                     stop=(kt == nk - 1), inc=True)
                if kt == nk - 1:
                    qs = slice(qc * 512, (qc + 1) * 512)
                    P.op("dve", "reciprocal", [o_ps], [rden], out=rden[64:65, :], in_=o_ps[64:65, :])
                    P.op("pe", "matmul", [C["ones"], rden], [psBC], psBC[0:64, 0:512], lhsT=C["ones"][64:65, 0:64],
                         rhs=rden[64:65, :], start=True, stop=True)
                    P.op("act", "activation", [psBC], [bcs], out=bcs[:], in_=psBC[0:64, 0:512], func=AF.Copy)
                    os_ = osr.next()
                    P.op("dve", "tensor_tensor", [o_ps, bcs], [os_], out=os_[:], in0=o_ps[0:64, :], in1=bcs[:], op=ALU.mult)
                    P.dma("sp", o_s[h // 2][(h % 2) * 64:(h % 2 + 1) * 64, qs], os_[:], o_s[h // 2], os_)
                    next(nxt, None)

            for (qc, kt) in items:
                pend.append((qc, kt, qk(qc, kt)))
                if len(pend) > LAG:
                    pv(*pend.pop(0))
            while pend:
                pv(*pend.pop(0))
            for _ in nxt:
                pass
        P.flush()
        P.recycle()


def emit_fox_d(P, C, PS, W, xin, o_g, og_i, xmid):
    with ExitStack() as s1:
        A, mod = emit_mod(P, PS["A"][0], W["c"], W["mix_g"], W["mix_mod_w"], W["mix_mod_b"], "dm", s1)
        flag = P.sb([128, 1], F32, "flag", s1)
        nflag = P.sb([128, 1], F32, "nflag", s1)
        P.dma("sp", flag[:], W["flag"][:], flag, W["flag"])
        P.op("dve", "tensor_scalar", [flag], [nflag], out=nflag[:], in0=flag[:], scalar1=-1.0, scalar2=1.0,
             op0=ALU.mult, op1=ALU.add)
        xg = P.sb([128, 8, G], F32, "xg", s1)
        og = P.sb([128, 8, G], F32, "og", s1)
        ot0 = P.sb([128, 8, G], F32, "ot0", s1)
        ot1 = P.sb([128, 8, G], F32, "ot1", s1)
        owr = Rot([P.sb([128, 8, 128], F32, f"ow{i}", s1) for i in range(2)])
        psO = Rot(PS["O"])
        for g in range(NG):
            P.dma("sp", xg[:], xview(xin, g), xg, xin)
            P.dma("sp", og[:], xview(og_i, g), og, og_i)
            for k in range(8):
                og_k = o_g[k % 4][(k // 4) * 128:(k // 4 + 1) * 128, :]
                P.dma("sp", ot0[:, k, :], og_k[:, g * G:(g + 1) * G], ot0, o_g[k % 4])
                P.dma("act", ot1[:, k, :], og_k[:, T + g * G:T + (g + 1) * G], ot1, o_g[k % 4])
            P.op("pool", "tensor_scalar", [ot1, flag], [ot1], out=ot1[:], in0=ot1[:], scalar1=flag[:, 0:1], scalar2=None,
                 op0=ALU.mult)
            P.op("dve", "scalar_tensor_tensor", [ot0, nflag, ot1], [ot0], out=ot0[:], in0=ot0[:], scalar=nflag[:, 0:1],
                 in1=ot1[:], op0=ALU.mult, op1=ALU.add)
            P.op("dve", "tensor_tensor", [og, ot0], [og], out=og[:], in0=og[:], in1=ot0[:], op=ALU.mult)
            for dc in range(8):
                ow = owr.next()
                P.dma("sp", ow[:], W["fox_out_w"][dc], ow, W["fox_out_w"])
                o_ps = psO.next()
                for k in range(8):
                    P.op("pe", "matmul", [ow, og], [o_ps], o_ps[:], lhsT=ow[:, k, :], rhs=og[:, k, :], start=(k == 0),
                         stop=(k == 7), inc=(k == 7))
                P.op("dve", "scalar_tensor_tensor", [o_ps, mod, xg], [xg], out=xg[:, dc, :], in0=o_ps[:],
                     scalar=mod[:, 16 + dc:17 + dc], in1=xg[:, dc, :], op0=ALU.mult, op1=ALU.add)
            P.dma("sp", xview(xmid, g), xg[:], xmid, xg)
        P.flush()
        P.recycle()


def build_fused(ncores=NCORES, do_l0=True, do_peer1=True):
    nc = bass.Bass("TRN2", target_bir_lowering=False)
    PAIRS = [[2 * i, 2 * i + 1] for i in range(ncores // 2)]
    shared = {}
    W0 = WMap(nc, "l0_", shared)
    W1 = WMap(nc, "l1_", shared)
    xin = din(nc, "xT", [D, T])
    xprev = din(nc, "xTp", [D, T])
    xout = dout(nc, "xo", [D, T])
    with ExitStack() as st:
        P = Prog(nc, st)
        PS = alloc_psum(P)
        C = emit_consts(P)
        xmid0 = P.dram("xmid0", [D, T])
        x1 = P.dram("x1", [D, T])
        qk_s = [P.dram(f"qk_s{i}", [256, T]) for i in range(8)]
        qk_g = [P.dram(f"qk_g{i}", [512, T]) for i in range(8)]
        v_s = [P.dram(f"v_s{i}", [512, D]) for i in range(4)]
        v_g = [P.dram(f"v_g{i}", [1024, D]) for i in range(4)]
        lf_s = P.dram("lf_s", [16, T])
        lf_g = P.dram("lf_g", [2 * 16, T])
        ogs = P.dram("ogs", [D, T])
        o_s = [P.dram(f"o_s{i}", [128, S4]) for i in range(4)]
        o_g = [P.dram(f"o_g{i}", [256, S4]) for i in range(4)]
        xmid1 = P.dram("xmid1", [D, T])
        if do_l0:
            emit_lru(P, C, PS, W0, xin, xprev, xmid0)
            emit_peer(P, C, PS, W0, xmid0, x1, "l0")
        else:
            x1 = xin
        emit_fox_b(P, C, PS, W1, x1, qk_s, v_s, lf_s, ogs)
        P.coll("AllGather", PAIRS, lf_s, lf_g)
        for a, b_ in zip(qk_s + v_s, qk_g + v_g):
            P.coll("AllGather", PAIRS, a, b_)
        emit_fox_c(P, C, PS, W1, qk_g, v_g, lf_g, o_s)
        for a, b_ in zip(o_s, o_g):
            P.coll("AllGather", PAIRS, a, b_)
        if do_peer1:
            emit_fox_d(P, C, PS, W1, x1, o_g, ogs, xmid1)
            emit_peer(P, C, PS, W1, xmid1, xout, "l1")
        else:
            emit_fox_d(P, C, PS, W1, x1, o_g, ogs, xout)
        P.flush([xout])
    nc.used_inputs = ["xT", "xTp"] + W0.names() + [n for n in W1.names() if n not in W0.names()]
    return nc


def lay_fox(inp):
    d = {}
    d["mix_g"], d["mix_mod_w"], d["mix_mod_b"] = lay_mod(inp["l1_mix_norm_g"], inp["l1_mix_mod_w"], inp["l1_mix_mod_b"])
    iw = inp["l1_fox_in_w"]

    def blocks(cols, nblk, w):
        return np.ascontiguousarray(cols.reshape(8, 128, nblk, w).transpose(2, 1, 0, 3))
    d["wqk"] = blocks(iw[:, 0:2048], 16, 128)
    d["wv"] = blocks(iw[:, 2048:3072], 2, 512)
    d["wf"] = np.ascontiguousarray(iw[:, 3072:3088].reshape(8, 128, 16).transpose(1, 0, 2))
    d["wog"] = blocks(iw[:, 3088:4112], 8, 128)
    d["fb"] = np.ascontiguousarray(inp["l1_fox_f_b"].reshape(16, 1))
    d["gq"] = np.ascontiguousarray(np.tile(inp["l1_fox_q_norm_g"], 2).reshape(128, 1))
    d["gk"] = np.ascontiguousarray(np.tile(inp["l1_fox_k_norm_g"], 2).reshape(128, 1))
    d["fox_out_w"] = blocks(inp["l1_fox_out_w"], 8, 128)
    return d


def lay_l1peer(inp):
    d = {}
    d["ffn_g"], d["ffn_mod_w"], d["ffn_mod_b"] = lay_mod(inp["l1_ffn_norm_g"], inp["l1_ffn_mod_w"], inp["l1_ffn_mod_b"])
    d["qwT"], d["skT"], d["UT"], d["V"] = lay_peer(inp["l1_peer_q_w"], inp["l1_peer_subkey1"], inp["l1_peer_subkey2"],
                                                  inp["l1_peer_u"], inp["l1_peer_v"])
    return d


def maskneg():
    m = np.zeros((4, 128, 512), np.float32)
    tk = np.arange(128)[:, None]
    tq = np.arange(512)[None, :]
    for j in range(4):
        m[j] = np.where(tk + j * 128 > tq, -240000.0, 0.0)
    return m


def block_diag_ones():
    bd = np.zeros((128, 128), np.float32)
    bd[0:64, 0:64] = 1.0
    bd[64:128, 64:128] = 1.0
    return bd


_CACHE = {}


def kernel(**inp):
    inp = {k: np.asarray(v, dtype=np.float32) for k, v in inp.items()}
    if "nc" not in _CACHE:
        _CACHE["nc"] = build_fused()
    nc = _CACHE["nc"]
    sh = {}
    l0 = lay_l0(inp)
    l0.pop("ident")
    for k, v in l0.items():
        sh["l0_" + k] = v
    for k, v in {**lay_fox(inp), **lay_l1peer(inp)}.items():
        sh["l1_" + k] = v
    sh["ident"] = np.eye(128, dtype=np.float32)
    sh["bd"] = block_diag_ones()
    sh["maskneg"] = maskneg()
    maps = []
    x = inp["x"]
    for c in range(NCORES):
        b, hf = c // 2, c % 2
        m = dict(sh)
        m["xT"] = np.ascontiguousarray(x[b, hf * T:(hf + 1) * T, :].T)
        m["xTp"] = np.ascontiguousarray(x[b, 0:T, :].T) if hf else np.zeros((D, T), np.float32)
        m["flag"] = np.full((128, 1), float(hf), np.float32)
        m["c"] = np.ascontiguousarray(inp["c"][b].reshape(128, 8))
        maps.append({k: v for k, v in m.items() if k in nc.used_inputs})
    res = run_bass_kernel_spmd(nc, maps, core_ids=list(range(NCORES)))
    out = np.empty((4, S4, D), np.float32)
    for c in range(NCORES):
        b, hf = c // 2, c % 2
        out[b, hf * T:(hf + 1) * T, :] = res.results[c]["xo"].T
    return out
```

```python
import numpy as np
from contextlib import ExitStack
import concourse.bass as bass
import concourse.mybir as mybir
from concourse.bass_utils import run_bass_kernel_spmd

F32 = mybir.dt.float32
BF16 = mybir.dt.bfloat16
AF = mybir.ActivationFunctionType
ALU = mybir.AluOpType
AX = mybir.AxisListType

NCORES = 8
T = 2048
G = 512
NG = T // G
D = 1024
DR = 1408
NB = 16
BS = 88
NE = 16384


class Buf:
    __slots__ = ("t", "name", "last_w", "readers", "dsem", "dcnt")

    def __init__(self, t, name):
        self.t = t
        self.name = name
        self.last_w = None
        self.readers = {}
        self.dsem = None
        self.dcnt = 0

    def __getitem__(self, k):
        return self.t[k]


class Prog:
    def __init__(self, nc, stack):
        self.nc = nc
        self.stack = stack
        self.eng = {"pe": nc.tensor, "act": nc.scalar, "dve": nc.vector,
                    "pool": nc.gpsimd, "sp": nc.sync}
        self.sem = {}
        self.cnt = {}
        for e in self.eng:
            self.sem[e] = stack.enter_context(nc.semaphore("s_" + e))
            self.cnt[e] = 0
        self.waited = {}
        self.nbuf = 0
        self.q = {e: [] for e in self.eng}
        self.nblk = 0
        self.live = []
        self.sem_pool = []
        self.ccsem = stack.enter_context(nc.semaphore("s_cc"))
        self.cccnt = 0

    def sb(self, shape, dt=F32, name=None, stack=None):
        self.nbuf += 1
        name = (name or "b") + f"_{self.nbuf}"
        t = (stack or self.stack).enter_context(self.nc.sbuf_tensor(name, list(shape), dt))
        b = Buf(t, name)
        self.live.append(b)
        return b

    def recycle(self):
        for b in self.live:
            if b.dsem is not None:
                self.sem_pool.append((b.dsem, b.dcnt))
                b.dsem = None
        self.live = []

    def coll(self, kind, groups, src, dst):
        self._waits("pool", [src], [dst])
        self.q["pool"].append(("op", "collective_compute", (kind, ALU.bypass),
                               dict(replica_groups=groups, ins=[src.t.opt()], outs=[dst.t.opt()]),
                               (self.ccsem, None)))
        self.cccnt += 1
        tok = (self.ccsem, self.cccnt)
        src.readers[tok[0]] = max(src.readers.get(tok[0], 0), tok[1])
        dst.last_w = tok
        dst.readers = {}

    def ps(self, shape, dt=F32, name=None, stack=None):
        self.nbuf += 1
        name = (name or "p") + f"_{self.nbuf}"
        t = (stack or self.stack).enter_context(self.nc.psum_tensor(name, list(shape), dt))
        return Buf(t, name)

    def dram(self, name, shape, dt=F32, kind="Internal"):
        t = self.nc.dram_tensor(name, list(shape), dt, kind=kind)
        return Buf(t.ap(), name)

    def _waits(self, e, reads, writes):
        need = {}
        for b in reads:
            if b.last_w is not None:
                s, v = b.last_w
                need[s] = max(need.get(s, 0), v)
        for b in writes:
            if b.last_w is not None:
                s, v = b.last_w
                need[s] = max(need.get(s, 0), v)
            for s, v in b.readers.items():
                if s is self.sem.get(e):
                    continue
                need[s] = max(need.get(s, 0), v)
        for s, v in need.items():
            key = (e, id(s))
            if s is self.sem.get(e) and (e == "pe" or v > self.cnt[e]):
                continue
            if self.waited.get(key, 0) < v:
                self.q[e].append(("wait", s, v))
                self.waited[key] = v

    def op(self, e, meth, reads, writes, *args, inc=True, **kw):
        self._waits(e, reads, writes)
        self.q[e].append(("op", meth, args, kw, (self.sem[e], 1) if inc else None))
        if inc:
            self.cnt[e] += 1
            tok = (self.sem[e], self.cnt[e])
        else:
            tok = (self.sem[e], self.cnt[e] + 1)
        for b in reads:
            b.readers[tok[0]] = max(b.readers.get(tok[0], 0), tok[1])
        for b in writes:
            b.last_w = tok
            b.readers = {}

    def dma(self, q, out_ap, in_ap, dst, src, **kw):
        self._waits(q, [src], [dst])
        if dst.dsem is None:
            if self.sem_pool:
                dst.dsem, dst.dcnt = self.sem_pool.pop()
            else:
                dst.dsem = self.stack.enter_context(self.nc.semaphore("d_" + dst.name))
        kw = dict(kw, out=out_ap, in_=in_ap)
        self.q[q].append(("op", "dma_start", (), kw, (dst.dsem, 16)))
        dst.dcnt += 16
        tok = (dst.dsem, dst.dcnt)
        src.readers[tok[0]] = max(src.readers.get(tok[0], 0), tok[1])
        dst.last_w = tok
        dst.readers = {}

    def flush(self, final_bufs=()):
        for b in final_bufs:
            if b.last_w is not None:
                s, v = b.last_w
                self.q["sp"].append(("wait", s, v))
        nc = self.nc
        if not any(self.q.values()):
            return
        self.nblk += 1
        with nc.Block() as block:
            for e, starter in (("sp", block.sync), ("pe", block.tensor), ("act", block.scalar),
                               ("dve", block.vector), ("pool", block.gpsimd)):
                items = self.q[e]
                if not items:
                    continue

                def body(eng, items=items):
                    for it in items:
                        if it[0] == "wait":
                            eng.wait_ge(it[1], it[2])
                        else:
                            _, meth, args, kw, inc = it
                            ins = getattr(eng, meth)(*args, **kw)
                            if inc is not None:
                                if inc[1] is None:
                                    ins.then_inc(inc[0])
                                else:
                                    ins.then_inc(inc[0], inc[1])
                starter(body)
        self.q = {e: [] for e in self.eng}


class Rot:
    def __init__(self, bufs):
        self.bufs = bufs
        self.i = 0

    def next(self):
        b = self.bufs[self.i % len(self.bufs)]
        self.i += 1
        return b


def din(nc, name, shape, dt=F32):
    return Buf(nc.dram_tensor(name, list(shape), dt, kind="ExternalInput").ap(), name)


def dout(nc, name, shape, dt=F32):
    return Buf(nc.dram_tensor(name, list(shape), dt, kind="ExternalOutput").ap(), name)


def emit_consts(P):
    C = {}
    C["ones"] = P.sb([128, 128], F32, "ones")
    P.op("pool", "memset", [], [C["ones"]], C["ones"][:], 1.0)
    C["zb"] = P.sb([128, 512], BF16, "zb")
    P.op("pool", "memset", [], [C["zb"]], C["zb"][:], 0.0)
    return C


def emit_mod(P, ps_small, c_d, g_d, w_d, b_d, pfx, pst=None):
    A = P.sb([128, 8], F32, pfx + "A", pst)
    mod = P.sb([128, 24], F32, pfx + "mod", pst)
    with ExitStack() as st:
        cs = P.sb([128, 8], F32, "cs", st)
        gs = P.sb([128, 8], F32, "gs", st)
        bs = P.sb([128, 24], F32, "bs", st)
        sc = P.sb([128, 8], F32, "sc", st)
        P.dma("sp", cs[:], c_d[:], cs, c_d)
        P.dma("sp", gs[:], g_d[:], gs, g_d)
        P.dma("sp", bs[:], b_d[:], bs, b_d)
        P.op("act", "activation", [cs], [sc], out=sc[:], in_=cs[:], func=AF.Silu)
        wrot = Rot([P.sb([128, 8, 512], F32, f"wb{i}", st) for i in range(2)])
        modp = ps_small
        for s in range(6):
            wb = wrot.next()
            P.dma("sp", wb[:], w_d[:, :, s * 512:(s + 1) * 512], wb, w_d)
            for j in range(4):
                col = s * 4 + j
                for k in range(8):
                    P.op("pe", "matmul", [wb, sc], [modp], modp[:, col:col + 1],
                         lhsT=wb[:, k, j * 128:(j + 1) * 128], rhs=sc[:, k:k + 1],
                         start=(k == 0), stop=(k == 7), inc=(k == 7))
        P.op("dve", "tensor_tensor", [modp, bs], [mod], out=mod[:], in0=modp[:, 0:24], in1=bs[:], op=ALU.add)
        P.op("dve", "scalar_tensor_tensor", [mod, gs], [A], out=A[:], in0=mod[:, 8:16], scalar=1.0,
             in1=gs[:], op0=ALU.add, op1=ALU.mult)
        P.flush()
    return A, mod


def emit_norm(P, C, xg, A, mod, hT, hT_bf, sqrot, ssp, rstd):
    for k in range(8):
        s_ = sqrot.next()
        P.op("act", "activation", [xg], [s_], out=s_[:], in_=xg[:, k, :], func=AF.Square)
        P.op("pe", "matmul", [C["ones"], s_], [ssp], ssp[:], lhsT=C["ones"][:], rhs=s_[:],
             start=(k == 0), stop=(k == 7))
    P.op("dve", "tensor_scalar", [ssp], [rstd], out=rstd[:], in0=ssp[:], scalar1=1.0 / D, scalar2=1e-6,
         op0=ALU.mult, op1=ALU.add)
    P.op("act", "activation", [rstd], [rstd], out=rstd[:], in_=rstd[:], func=AF.Sqrt)
    P.op("dve", "reciprocal", [rstd], [rstd], out=rstd[:], in_=rstd[:])
    for k in range(8):
        P.op("dve", "tensor_tensor", [xg, rstd], [hT], out=hT[:, k, :], in0=xg[:, k, :], in1=rstd[:], op=ALU.mult)
        P.op("act", "activation", [hT, A, mod], [hT], out=hT[:, k, :], in_=hT[:, k, :], func=AF.Identity,
             scale=A[:, k:k + 1], bias=mod[:, k:k + 1])
        if hT_bf is not None:
            P.op("pool", "tensor_copy", [hT], [hT_bf], out=hT_bf[:, k, :], in_=hT[:, k, :])


def xview(x_d, g):
    return x_d.t.rearrange("(k p) t -> p k t", p=128)[:, :, g * G:(g + 1) * G]


def emit_lru(P, C, PS, W, xin_d, xprev_d, xout_d):
    with ExitStack() as st:
        A, mod = emit_mod(P, PS["A"][0], W["c"], W["mix_g"], W["mix_mod_w"], W["mix_mod_b"], "lm", st)
        cw = P.sb([128, NB, 4], F32, "cw", st)
        tabs = {}
        for nm in ("conv_b", "ra_b", "ri_b", "lam"):
            tabs[nm] = P.sb([128, NB], F32, nm, st)
            P.dma("sp", tabs[nm][0:BS, :], W[nm][:], tabs[nm], W[nm])
        P.dma("sp", cw[0:BS, :, :], W["conv_w"][:], cw, W["conv_w"])
        flag = P.sb([128, 1], F32, "flag", st)
        P.dma("sp", flag[:], W["flag"][:], flag, W["flag"])
        raw = P.sb([128, NB, BS], F32, "raw", st)
        riw = P.sb([128, NB, BS], F32, "riw", st)
        P.dma("sp", raw[0:BS, :, :], W["ra_w"].t.rearrange("n c d -> c n d"), raw, W["ra_w"])
        P.dma("sp", riw[0:BS, :, :], W["ri_w"].t.rearrange("n c d -> c n d"), riw, W["ri_w"])
        nls = P.sb([128, NB], F32, "nls", st)
        nls2 = P.sb([128, NB], F32, "nls2", st)
        lam = tabs["lam"]
        P.op("act", "activation", [lam], [nls], out=nls[0:BS, :], in_=lam[0:BS, :], func=AF.Exp, scale=-1.0)
        P.op("act", "activation", [nls], [nls], out=nls[0:BS, :], in_=nls[0:BS, :], func=AF.Ln, bias=1.0, scale=1.0)
        P.op("dve", "tensor_scalar", [nls], [nls2], out=nls2[0:BS, :], in0=nls[0:BS, :], scalar1=-16.0, scalar2=None,
             op0=ALU.mult)
        P.op("dve", "tensor_scalar", [nls], [nls], out=nls[0:BS, :], in0=nls[0:BS, :], scalar1=-8.0, scalar2=None,
             op0=ALU.mult)
        hist = P.sb([128, NB, 3], F32, "hist", st)
        carry = P.sb([128, NB], F32, "carry", st)
        P.op("pool", "memset", [], [hist], hist[:], 0.0)
        P.op("pool", "memset", [], [carry], carry[:], 0.0)

        xg = P.sb([128, 8, G], F32, "xg", st)
        hT = P.sb([128, 8, G], F32, "hT", st)
        hTb = P.sb([128, 8, G], BF16, "hTb", st)
        sqrot = Rot([P.sb([128, G], F32, f"sq{i}", st) for i in range(2)])
        rstd = P.sb([128, G], F32, "rstd", st)
        wxr = Rot([P.sb([128, 8, BS], BF16, f"wx{i}", st) for i in range(3)])
        wgr = Rot([P.sb([128, 8, BS], BF16, f"wg{i}", st) for i in range(3)])
        xbr = Rot([P.sb([128, G + 3], F32, f"xbuf{i}", st) for i in range(2)])

        def tmp(nm, n=2):
            return Rot([P.sb([128, G], F32, f"{nm}{i}", st) for i in range(n)])
        cvr, rr, ir, ar, a2r, ur, hsr, glr = [tmp(n) for n in ("cv", "r", "i", "a", "a2", "u", "hs", "gl")]
        yT = P.sb([128, NB, G], BF16, "yT", st)
        owr = Rot([P.sb([128, NB, 128], BF16, f"ow{i}", st) for i in range(3)])
        psA, psO, psW = Rot(PS["A"]), Rot(PS["O"]), Rot(PS["W"])

        for step in range(2 * NG):
            main = step >= NG
            g = step % NG
            src = xin_d if main else xprev_d
            P.dma("sp", xg[:], xview(src, g), xg, src)
            emit_norm(P, C, xg, A, mod, hT, hTb, sqrot, PS["S"], rstd)
            for n in range(NB):
                wx = wxr.next()
                P.dma("pool", wx[:], W["in_w"][n], wx, W["in_w"])
                xb_ps = psA.next()
                for k in range(8):
                    P.op("pe", "matmul", [wx, hTb], [xb_ps], xb_ps[0:BS, :], lhsT=wx[:, k, :], rhs=hTb[:, k, :],
                         start=(k == 0), stop=(k == 7), inc=(k == 7))
                if main:
                    wg = wgr.next()
                    P.dma("pool", wg[:], W["in_w"][NB + n], wg, W["in_w"])
                    gb_ps = psO.next()
                    for k in range(8):
                        P.op("pe", "matmul", [wg, hTb], [gb_ps], gb_ps[0:BS, :], lhsT=wg[:, k, :], rhs=hTb[:, k, :],
                             start=(k == 0), stop=(k == 7), inc=(k == 7))
                xb = xbr.next()
                P.op("act", "activation", [xb_ps], [xb], out=xb[0:BS, 3:G + 3], in_=xb_ps[0:BS, :], func=AF.Copy)
                P.op("dve", "tensor_copy", [hist], [xb], out=xb[0:BS, 0:3], in_=hist[0:BS, n, :])
                P.op("dve", "tensor_copy", [xb], [hist], out=hist[0:BS, n, :], in_=xb[0:BS, G:G + 3])
                cv = cvr.next()
                P.op("act", "activation", [xb, cw, tabs["conv_b"]], [cv], out=cv[0:BS, :], in_=xb[0:BS, 3:G + 3],
                     func=AF.Identity, scale=cw[0:BS, n, 3:4], bias=tabs["conv_b"][0:BS, n:n + 1])
                for k in range(3):
                    P.op("dve", "scalar_tensor_tensor", [xb, cw, cv], [cv], out=cv[0:BS, :], in0=xb[0:BS, k:k + G],
                         scalar=cw[0:BS, n, k:k + 1], in1=cv[0:BS, :], op0=ALU.mult, op1=ALU.add)
                pw = psW.next()
                r_ps, i_ps = pw[0:BS, 0:G], pw[0:BS, G:2 * G]
                P.op("pe", "matmul", [raw, cv], [pw], r_ps, lhsT=raw[0:BS, n, :], rhs=cv[0:BS, :], start=True, stop=True,
                     inc=False)
                P.op("pe", "matmul", [riw, cv], [pw], i_ps, lhsT=riw[0:BS, n, :], rhs=cv[0:BS, :], start=True, stop=True)
                r, i_, a, a2, u, hs = rr.next(), ir.next(), ar.next(), a2r.next(), ur.next(), hsr.next()
                P.op("act", "activation", [pw, tabs["ra_b"]], [r], out=r[0:BS, :], in_=r_ps, func=AF.Sigmoid,
                     bias=tabs["ra_b"][0:BS, n:n + 1], scale=1.0)
                P.op("act", "activation", [pw, tabs["ri_b"]], [i_], out=i_[0:BS, :], in_=i_ps, func=AF.Sigmoid,
                     bias=tabs["ri_b"][0:BS, n:n + 1], scale=1.0)
                P.op("act", "activation", [r, nls], [a], out=a[0:BS, :], in_=r[0:BS, :], func=AF.Exp,
                     scale=nls[0:BS, n:n + 1])
                P.op("act", "activation", [r, nls2], [a2], out=a2[0:BS, :], in_=r[0:BS, :], func=AF.Exp,
                     scale=nls2[0:BS, n:n + 1])
                P.op("dve", "tensor_scalar", [a2], [a2], out=a2[0:BS, :], in0=a2[0:BS, :], scalar1=1.0, scalar2=-1.0,
                     op0=ALU.min, op1=ALU.mult)
                P.op("act", "activation", [a2], [a2], out=a2[0:BS, :], in_=a2[0:BS, :], func=AF.Sqrt, bias=1.0,
                     scale=1.0)
                P.op("dve", "tensor_tensor", [i_, cv], [u], out=u[0:BS, :], in0=i_[0:BS, :], in1=cv[0:BS, :], op=ALU.mult)
                P.op("dve", "tensor_tensor", [u, a2], [u], out=u[0:BS, :], in0=u[0:BS, :], in1=a2[0:BS, :], op=ALU.mult)
                P.op("dve", "tensor_tensor_scan", [a, u, carry], [hs], out=hs[0:BS, :], data0=a[0:BS, :],
                     data1=u[0:BS, :], initial=carry[0:BS, n:n + 1], op0=ALU.mult, op1=ALU.add)
                P.op("dve", "tensor_copy", [hs], [carry], out=carry[0:BS, n:n + 1], in_=hs[0:BS, G - 1:G])
                if main:
                    gl = glr.next()
                    P.op("act", "activation", [gb_ps], [gl], out=gl[0:BS, :], in_=gb_ps[0:BS, :], func=AF.Gelu)
                    P.op("pool", "tensor_tensor", [hs, gl], [yT], out=yT[0:BS, n, :], in0=hs[0:BS, :], in1=gl[0:BS, :],
                         op=ALU.mult)
            if not main and g == NG - 1:
                P.op("dve", "tensor_scalar", [hist, flag], [hist], out=hist[0:BS, :, :], in0=hist[0:BS, :, :],
                     scalar1=flag[0:BS, 0:1], scalar2=None, op0=ALU.mult)
                P.op("dve", "tensor_scalar", [carry, flag], [carry], out=carry[0:BS, :], in0=carry[0:BS, :],
                     scalar1=flag[0:BS, 0:1], scalar2=None, op0=ALU.mult)
            if main:
                for dc in range(8):
                    ow = owr.next()
                    P.dma("pool", ow[0:BS, :, :], W["out_w"][dc], ow, W["out_w"])
                    o_ps = psO.next()
                    for n in range(NB):
                        P.op("pe", "matmul", [ow, yT], [o_ps], o_ps[:], lhsT=ow[0:BS, n, :], rhs=yT[0:BS, n, :],
                             start=(n == 0), stop=(n == NB - 1), inc=(n == NB - 1))
                    P.op("dve", "scalar_tensor_tensor", [o_ps, mod, xg], [xg], out=xg[:, dc, :], in0=o_ps[:],
                         scalar=mod[:, 16 + dc:17 + dc], in1=xg[:, dc, :], op0=ALU.mult, op1=ALU.add)
                P.dma("sp", xview(xout_d, g), xg[:], xout_d, xg)
        P.flush()
        P.recycle()


def emit_peer(P, C, PS, W, xin_d, xout_d, pfx):
    nc = P.nc
    with ExitStack() as st:
        A, mod = emit_mod(P, PS["A"][0], W["c"], W["ffn_g"], W["ffn_mod_w"], W["ffn_mod_b"], pfx + "pm", st)
        wf_d = P.dram(pfx + "wf", [128, 8, 2048], F32)
        with ExitStack() as s2:
            qbr = Rot([P.sb([128, D], F32, f"qb{i}", s2) for i in range(2)])
            skr = Rot([P.sb([128, 128], F32, f"sk{i}", s2) for i in range(2)])
            wfr = Rot([P.sb([128, 8, 128], F32, f"wfs{i}", s2) for i in range(2)])
            psW = Rot(PS["W"])
            for blk in range(16):
                qb, sk, wfs, pw = qbr.next(), skr.next(), wfr.next(), psW.next()
                P.dma("sp", qb[:], W["qwT"][blk], qb, W["qwT"])
                P.dma("sp", sk[:], W["skT"][blk], sk, W["skT"])
                for dc in range(8):
                    P.op("pe", "matmul", [qb, sk], [pw], pw[:, dc * 128:(dc + 1) * 128],
                         lhsT=qb[:, dc * 128:(dc + 1) * 128], rhs=sk[:], start=True, stop=True, inc=(dc == 7))
                P.op("act", "activation", [pw], [wfs], out=wfs[:].rearrange("p a b -> p (a b)"), in_=pw[:], func=AF.Copy)
                P.dma("sp", wf_d[:, :, blk * 128:(blk + 1) * 128], wfs[:], wf_d, wfs)
            P.flush()

        identf = P.sb([128, 128], F32, "identf", st)
        ident = P.sb([128, 128], BF16, "ident", st)
        P.dma("sp", identf[:], W["ident"][:], identf, W["ident"])
        P.op("dve", "tensor_copy", [identf], [ident], out=ident[:], in_=identf[:])

        xg = P.sb([128, 8, G], F32, "xg", st)
        hT = P.sb([128, 8, G], F32, "hT", st)
        hTb = P.sb([128, 8, G], BF16, "hTb", st)
        sqrot = Rot([P.sb([128, G], F32, f"sq{i}", st) for i in range(2)])
        rstd = P.sb([128, G], F32, "rstd", st)
        s_sb = [P.sb([128, 16, 128], F32, f"s_sb{i}", st) for i in range(4)]
        wfpr = Rot([P.sb([128, 8, 128], F32, f"wfp{i}", st) for i in range(2)])
        vtop = P.sb([128, 16, 16], F32, "vtop", st)
        tmp128 = P.sb([128, 128], F32, "tmp128", st)
        candr = Rot([P.sb([128, 256], F32, f"cand{i}", st) for i in range(2)])
        ctmp = P.sb([128, 256], F32, "ctmp", st)
        ctmp2 = P.sb([128, 256], F32, "ctmp2", st)
        ttop = P.sb([128, 8, 24], F32, "ttop", st)
        etop = P.sb([128, 16, 16], F32, "etop", st)
        mneg = P.sb([128, 8], F32, "mneg", st)
        Zs = P.sb([128, 8], F32, "Zs", st)
        tmid = P.sb([128, 8], F32, "tmid", st)
        thr = [P.sb([128, 8], F32, f"thr{i}", st) for i in range(4)]
        Er = Rot([P.sb([128, 8, 128], F32, f"E{i}", st) for i in range(3)])
        Mr = Rot([P.sb([128, 8, 128], BF16, f"M{i}", st) for i in range(3)])
        Kr = Rot([P.sb([128, 8, 128], BF16, f"K{i}", st) for i in range(3)])
        dg = P.sb([128, 4, 8, 128], BF16, "dg", st)
        Wtr = Rot([P.sb([128, 8, 128], BF16, f"Wt{i}", st) for i in range(2)])
        wTs = [P.sb([128, 8, G], BF16, f"wT{i}", st) for i in range(2)]
        Ur = Rot([P.sb([128, 8, 128], BF16, f"U{i}", st) for i in range(3)])
        NV = 16
        Vr = [P.sb([128, D], BF16, f"V{i}", st) for i in range(NV)]
        WAr = [P.sb([128, G], BF16, f"WA{i}", st) for i in range(NV)]
        Gr = Rot([P.sb([128, G], BF16, f"G{i}", st) for i in range(2)])
        otr = Rot([P.sb([128, G], F32, f"ot{i}", st) for i in range(2)])
        psA, psO = Rot(PS["A"]), Rot(PS["O"])
        wacc, wtp = PS["W"][0], PS["W"][1]
        ev = 0
        ACT_HEADS = ()
        DVE_HEADS = (0, 2, 4, 6)

        for g in range(NG):
            P.dma("sp", xg[:], xview(xin_d, g), xg, xin_d)
            emit_norm(P, C, xg, A, mod, hT, hTb, sqrot, PS["S"], rstd)
            for ns in range(16):
                wfp = wfpr.next()
                P.dma("sp", wfp[:], wf_d[:, :, ns * 128:(ns + 1) * 128], wfp, wf_d)
                for tt in range(4):
                    ps = psA.next()
                    for k in range(8):
                        P.op("pe", "matmul", [hT, wfp], [ps], ps[:, 0:128], lhsT=hT[:, k, tt * 128:(tt + 1) * 128],
                             rhs=wfp[:, k, :], start=(k == 0), stop=(k == 7), inc=(k == 7))
                    dst = s_sb[tt][:, ns, :]
                    if ev % 2 == 0:
                        P.op("act", "activation", [ps], [s_sb[tt]], out=dst, in_=ps[:, 0:128], func=AF.Copy)
                    else:
                        P.op("dve", "tensor_copy", [ps], [s_sb[tt]], out=dst, in_=ps[:, 0:128])
                    ev += 1
            for tt in range(4):
                s_ = s_sb[tt]
                for blk in range(16):
                    P.op("dve", "max", [s_], [vtop], out=vtop[:, blk, 0:8], in_=s_[:, blk, :])
                    P.op("dve", "match_replace", [vtop, s_], [tmp128], out=tmp128[:], in_to_replace=vtop[:, blk, 0:8],
                         in_values=s_[:, blk, :], imm_value=-1e30)
                    P.op("dve", "max", [tmp128], [vtop], out=vtop[:, blk, 8:16], in_=tmp128[:])
                P.op("dve", "scalar_tensor_tensor", [vtop], [mneg], out=mneg[:], in0=vtop[:, 0:16:2, 0], scalar=-1.0,
                     in1=vtop[:, 1:16:2, 0], op0=ALU.mult, op1=ALU.subtract)
                for h in range(8):
                    P.op("act", "activation", [vtop, mneg], [etop], out=etop[:, 2 * h, :], in_=vtop[:, 2 * h, :],
                         func=AF.Exp, bias=mneg[:, h:h + 1], scale=1.0)
                    P.op("act", "activation", [s_, mneg], [s_], out=s_[:, 2 * h, :], in_=s_[:, 2 * h, :], func=AF.Exp,
                         bias=mneg[:, h:h + 1], scale=1.0)
                P.op("act", "activation", [vtop], [etop], out=etop[:, 1:16:2, :], in_=vtop[:, 1:16:2, :], func=AF.Exp)
                P.op("act", "activation", [s_], [s_], out=s_[:, 1:16:2, :], in_=s_[:, 1:16:2, :], func=AF.Exp)
                for h in range(8):
                    in0 = etop[:, 2 * h, :].unsqueeze(2).to_broadcast([128, 16, 16])
                    in1 = etop[:, 2 * h + 1, :].unsqueeze(1).to_broadcast([128, 16, 16])
                    cand = candr.next()
                    P.op("dve", "tensor_tensor", [etop], [cand],
                         out=cand[:].rearrange("p (a b) -> p a b", a=16), in0=in0, in1=in1, op=ALU.mult)
                    P.op("dve", "max", [cand], [ttop], out=ttop[:, h, 0:8], in_=cand[:])
                    P.op("dve", "match_replace", [ttop, cand], [ctmp], out=ctmp[:], in_to_replace=ttop[:, h, 0:8],
                         in_values=cand[:], imm_value=-1.0)
                    P.op("dve", "max", [ctmp], [ttop], out=ttop[:, h, 8:16], in_=ctmp[:])
                P.op("dve", "tensor_reduce", [ttop], [Zs], out=Zs[:], in_=ttop[:, :, 0:16], axis=AX.X, op=ALU.add)
                P.op("dve", "reciprocal", [Zs], [Zs], out=Zs[:], in_=Zs[:])
                P.op("dve", "scalar_tensor_tensor", [ttop, Zs], [thr[tt]], out=thr[tt][:], in0=ttop[:, :, 15],
                     scalar=1.0 - 2e-6, in1=Zs[:], op0=ALU.mult, op1=ALU.mult)
                for h in range(8):
                    P.op("pool", "tensor_scalar", [s_, Zs], [s_], out=s_[:, 2 * h, :], in0=s_[:, 2 * h, :],
                         scalar1=Zs[:, h:h + 1], scalar2=None, op0=ALU.mult)
                    P.op("pool", "tensor_scalar", [ident, thr[tt]], [dg], out=dg[:, tt, h, :], in0=ident[:],
                         scalar1=thr[tt][:, h:h + 1], scalar2=None, op0=ALU.mult)

            sched = []

            def at(t, fn):
                sched.append((t, len(sched), fn))

            NIC = 16
            for step in range(NIC + 2):
                for tt in range(4):
                    n0 = (step * 4 + tt) * 8
                    if step < NIC:
                        ic = step
                        s_ = s_sb[tt]
                        for h in range(8):
                            E, M = Er.next(), Mr.next()
                            in0 = s_[:, 2 * h, ic * 8:(ic + 1) * 8].unsqueeze(2).to_broadcast([128, 8, 128])
                            in1 = s_[:, 2 * h + 1, :].unsqueeze(1).to_broadcast([128, 8, 128])
                            eng = "dve" if h in DVE_HEADS else "pool"
                            at(n0 + h - 2, lambda eng=eng, s_=s_, E=E, in0=in0, in1=in1: P.op(
                                eng, "tensor_tensor", [s_], [E], out=E[:], in0=in0, in1=in1, op=ALU.mult))
                            Kb = Kr.next()
                            at(n0 + h - 1, lambda E=E, M=M, tt=tt, h=h: P.op(
                                "dve", "tensor_scalar", [E, thr[tt]], [M], out=M[:], in0=E[:], scalar1=thr[tt][:, h:h + 1],
                                scalar2=0.0, op0=ALU.subtract, op1=ALU.max))
                            at(n0 + h, lambda M=M, Kb=Kb: P.op("act", "activation", [M], [Kb], out=Kb[:], in_=M[:],
                                                                func=AF.Sign))

                            def acc(M=M, Kb=Kb, h=h, tt=tt):
                                for hb in range(2):
                                    P.op("pe", "matmul", [ident, M], [wacc], wacc[:, hb * 512:(hb + 1) * 512], lhsT=ident[:],
                                         rhs=M[:, hb * 4:(hb + 1) * 4, :].rearrange("p a b -> p (a b)"), start=(h == 0),
                                         stop=False, inc=False)
                                for hb in range(2):
                                    P.op("pe", "matmul", [dg, Kb], [wacc], wacc[:, hb * 512:(hb + 1) * 512],
                                         lhsT=dg[:, tt, h, :], rhs=Kb[:, hb * 4:(hb + 1) * 4, :].rearrange("p a b -> p (a b)"),
                                         start=False, stop=(h == 7), inc=(hb == 1))
                            at(n0 + h + 1, acc)
                        Wt = Wtr.next()
                        at(n0 + 9, lambda Wt=Wt: P.op("act", "activation", [wacc], [Wt],
                                                     out=Wt[:].rearrange("p a b -> p (a b)"), in_=wacc[:], func=AF.Copy))

                        def tr(Wt=Wt):
                            for i in range(8):
                                P.op("pe", "matmul", [Wt, ident], [wtp], wtp[:, i * 128:(i + 1) * 128], lhsT=Wt[:, i, :],
                                     rhs=ident[:], start=True, stop=True, inc=(i == 7))
                        at(n0 + 11, tr)
                        at(n0 + 12, lambda ic=ic, tt=tt: P.op(
                            "act", "activation", [wtp], [wTs[ic % 2]], out=wTs[ic % 2][:, :, tt * 128:(tt + 1) * 128],
                            in_=wtp[:].rearrange("p (a b) -> p a b", a=8), func=AF.Copy))
                    if 1 <= step <= NIC:
                        ic = step - 1
                        for q_, i in enumerate((2 * tt, 2 * tt + 1)):
                            e = ic * 8 + i
                            Vc, Uc, a_ps, Gt, WA = Vr[e % NV], Ur.next(), psA.next(), Gr.next(), WAr[e % NV]
                            ta = n0 + 4 * q_ + 3
                            at(max(step * 32, ta - 12), lambda Vc=Vc, e=e: P.dma("pool", Vc[:], W["V"][e * 128:(e + 1) * 128, :], Vc, W["V"]))
                            at(ta - 6, lambda Uc=Uc, e=e: P.dma("pool", Uc[:], W["UT"][e], Uc, W["UT"]))

                            def amm(Uc=Uc, a_ps=a_ps):
                                for k in range(8):
                                    P.op("pe", "matmul", [Uc, hTb], [a_ps], a_ps[:], lhsT=Uc[:, k, :], rhs=hTb[:, k, :],
                                         start=(k == 0), stop=(k == 7), inc=(k == 7))
                            at(ta, amm)
                            at(ta + 1, lambda a_ps=a_ps, Gt=Gt: P.op("act", "activation", [a_ps], [Gt], out=Gt[:],
                                                                      in_=a_ps[:], func=AF.Gelu))
                            at(ta + 2, lambda Gt=Gt, WA=WA, ic=ic, i=i: P.op(
                                "dve", "tensor_tensor", [Gt, wTs[ic % 2]], [WA], out=WA[:], in0=Gt[:],
                                in1=wTs[ic % 2][:, i, :], op=ALU.mult))
                    if 2 <= step <= NIC + 1:
                        ic = step - 2
                        for q_, dc in enumerate((2 * tt, 2 * tt + 1)):
                            o_ps, ot = psO.next(), otr.next()
                            tv = n0 + 4 * q_ + 2

                            def vmm(ic=ic, dc=dc, o_ps=o_ps):
                                for i in range(8):
                                    e = ic * 8 + i
                                    P.op("pe", "matmul", [Vr[e % NV], WAr[e % NV]], [o_ps], o_ps[:],
                                         lhsT=Vr[e % NV][:, dc * 128:(dc + 1) * 128], rhs=WAr[e % NV][:], start=(i == 0),
                                         stop=(i == 7), inc=(i == 7))
                            at(tv, vmm)
                            at(tv + 1, lambda o_ps=o_ps, ot=ot, dc=dc: P.op(
                                "act", "activation", [o_ps, mod], [ot], out=ot[:], in_=o_ps[:], func=AF.Copy,
                                scale=mod[:, 16 + dc:17 + dc]))
                            at(tv + 3, lambda ot=ot, dc=dc: P.op("pool", "tensor_tensor", [ot, xg], [xg], out=xg[:, dc, :],
                                                                 in0=ot[:], in1=xg[:, dc, :], op=ALU.add))
            sched.sort(key=lambda x: (x[0], x[1]))
            for _, _, fn in sched:
                fn()
            P.dma("sp", xview(xout_d, g), xg[:], xout_d, xg)
        P.flush()
        P.recycle()


def alloc_psum(P):
    PS = {}
    PS["A"] = [P.ps([128, 512], F32, f"psA{i}") for i in range(2)]
    PS["O"] = [P.ps([128, 512], F32, f"psO{i}") for i in range(2)]
    PS["W"] = [P.ps([128, 1024], F32, f"psW{i}") for i in range(2)]
    PS["S"] = PS["O"][0]
    return PS


SHAPES = {"c": [128, 8], "flag": [128, 1], "ident": [128, 128],
          "mix_g": [128, 8], "mix_mod_w": [128, 8, 3072], "mix_mod_b": [128, 24],
          "in_w": [2 * NB, 128, 8, BS], "conv_w": [BS, NB, 4], "conv_b": [BS, NB],
          "ra_w": [NB, BS, BS], "ra_b": [BS, NB], "ri_w": [NB, BS, BS], "ri_b": [BS, NB],
          "lam": [BS, NB], "out_w": [8, BS, NB, 128],
          "ffn_g": [128, 8], "ffn_mod_w": [128, 8, 3072], "ffn_mod_b": [128, 24],
          "qwT": [16, 128, D], "skT": [16, 128, 128], "UT": [128, 128, 8, 128], "V": [NE, D]}


class WMap(dict):
    GLOBAL = ("c", "flag", "ident", "bd", "maskneg")

    def __init__(self, nc, prefix="", shared=None):
        super().__init__()
        self.nc = nc
        self.prefix = prefix
        self.shared = shared if shared is not None else {}

    def __missing__(self, k):
        if k in self.GLOBAL:
            if k not in self.shared:
                self.shared[k] = din(self.nc, k, SHAPES[k])
            b = self.shared[k]
        else:
            b = din(self.nc, self.prefix + k, SHAPES[k])
        self[k] = b
        return b

    def names(self):
        return [(k if k in self.GLOBAL else self.prefix + k) for k in self.keys()]


def build_l0(do_lru=True, do_peer=True):
    nc = bass.Bass("TRN2", target_bir_lowering=False)
    W = WMap(nc)
    xin = din(nc, "xT", [D, T])
    xprev = din(nc, "xTp", [D, T]) if do_lru else None
    xout = dout(nc, "xo", [D, T])
    with ExitStack() as st:
        P = Prog(nc, st)
        PS = alloc_psum(P)
        C = emit_consts(P)
        if do_lru and do_peer:
            xmid = P.dram("xmid", [D, T], F32)
        else:
            xmid = xout
        if do_lru:
            emit_lru(P, C, PS, W, xin, xprev, xmid)
        if do_peer:
            emit_peer(P, C, PS, W, xmid if do_lru else xin, xout, "l0")
        P.flush([xout])
    nc.used_inputs = ["xT"] + (["xTp"] if do_lru else []) + W.names()
    return nc


def col8(v):
    return np.ascontiguousarray(v.reshape(-1, 128).T)


def blk16(v):
    return np.ascontiguousarray(v.reshape(NB, BS).T)


def lay_mod(g, w, b):
    return col8(g), np.ascontiguousarray(w.reshape(128, 8, 3072)), col8(b)


def lay_peer(q_w, sk1, sk2, u, v):
    qwT = np.ascontiguousarray(q_w.T.reshape(16, 128, D))
    skT = np.empty((16, 128, 128), np.float32)
    skT[0::2] = sk1.transpose(0, 2, 1)
    skT[1::2] = sk2.transpose(0, 2, 1)
    UT = np.ascontiguousarray(u.reshape(128, 128, 8, 128).transpose(0, 3, 2, 1))
    return qwT, skT, UT, np.ascontiguousarray(v)


def lay_l0(inp):
    d = {}
    d["mix_g"], d["mix_mod_w"], d["mix_mod_b"] = lay_mod(inp["l0_mix_norm_g"], inp["l0_mix_mod_w"], inp["l0_mix_mod_b"])
    d["ffn_g"], d["ffn_mod_w"], d["ffn_mod_b"] = lay_mod(inp["l0_ffn_norm_g"], inp["l0_ffn_mod_w"], inp["l0_ffn_mod_b"])
    iw = inp["l0_lru_in_w"].reshape(8, 128, 2 * NB, BS)
    d["in_w"] = np.ascontiguousarray(iw.transpose(2, 1, 0, 3))
    d["conv_w"] = np.ascontiguousarray(inp["l0_lru_conv_w"].reshape(4, NB, BS).transpose(2, 1, 0))
    d["conv_b"] = blk16(inp["l0_lru_conv_b"])
    d["ra_w"] = np.ascontiguousarray(inp["l0_lru_ra_w"])
    d["ri_w"] = np.ascontiguousarray(inp["l0_lru_ri_w"])
    d["ra_b"] = blk16(inp["l0_lru_ra_b"])
    d["ri_b"] = blk16(inp["l0_lru_ri_b"])
    d["lam"] = blk16(inp["l0_lru_lambda"])
    ow = inp["l0_lru_out_w"].reshape(NB, BS, 8, 128)
    d["out_w"] = np.ascontiguousarray(ow.transpose(2, 1, 0, 3))
    d["qwT"], d["skT"], d["UT"], d["V"] = lay_peer(inp["l0_peer_q_w"], inp["l0_peer_subkey1"], inp["l0_peer_subkey2"],
                                                  inp["l0_peer_u"], inp["l0_peer_v"])
    d["ident"] = np.eye(128, dtype=np.float32)
    return d


def core_maps_l0(inp, shared, cores):
    maps = []
    x = inp["x"]
    for c in cores:
        b, hf = c // 2, c % 2
        m = dict(shared)
        m["xT"] = np.ascontiguousarray(x[b, hf * T:(hf + 1) * T, :].T)
        m["xTp"] = np.ascontiguousarray(x[b, 0:T, :].T) if hf else np.zeros((D, T), np.float32)
        m["flag"] = np.full((128, 1), float(hf), np.float32)
        m["c"] = np.ascontiguousarray(inp["c"][b].reshape(128, 8))
        maps.append(m)
    return maps


SHAPES.update({"wqk": [16, 128, 8, 128], "wv": [2, 128, 8, 512], "wf": [128, 8, 16], "wog": [8, 128, 8, 128],
               "fb": [16, 1], "gq": [128, 1], "gk": [128, 1], "bd": [128, 128],
               "fox_out_w": [8, 128, 8, 128], "maskneg": [4, 128, 512]})
S4 = 4096
PAIRS = [[0, 1], [2, 3], [4, 5], [6, 7]]


def emit_fox_b(P, C, PS, W, xin, qk_o, v_o, lf_o, og_o):
    with ExitStack() as st:
        A, mod = emit_mod(P, PS["A"][0], W["c"], W["mix_g"], W["mix_mod_w"], W["mix_mod_b"], "fm", st)
        bd = P.sb([128, 128], F32, "bd", st)
        P.dma("sp", bd[:], W["bd"][:], bd, W["bd"])
        gq = P.sb([128, 1], F32, "gq", st)
        gk = P.sb([128, 1], F32, "gk", st)
        nfb = P.sb([16, 1], F32, "nfb", st)
        P.dma("sp", gq[:], W["gq"][:], gq, W["gq"])
        P.dma("sp", gk[:], W["gk"][:], gk, W["gk"])
        P.dma("sp", nfb[:], W["fb"][:], nfb, W["fb"])
        P.op("dve", "tensor_scalar", [nfb], [nfb], out=nfb[:], in0=nfb[:], scalar1=-1.0, scalar2=None, op0=ALU.mult)
        xg = P.sb([128, 8, G], F32, "xg", st)
        hT = P.sb([128, 8, G], F32, "hT", st)
        hTb = P.sb([128, 8, G], BF16, "hTb", st)
        sqrot = Rot([P.sb([128, G], F32, f"sq{i}", st) for i in range(2)])
        rstd = P.sb([128, G], F32, "rstd", st)
        wr = Rot([P.sb([128, 8, 128], BF16, f"w{i}", st) for i in range(4)])
        wvr = Rot([P.sb([128, 8, 512], BF16, f"wv{i}", st) for i in range(2)])
        wfs = P.sb([128, 8, 16], F32, "wfs", st)
        P.dma("sp", wfs[:], W["wf"][:], wfs, W["wf"])
        q2r = Rot([P.sb([128, G], F32, f"q2{i}", st) for i in range(2)])
        rsr = Rot([P.sb([128, G], F32, f"rs{i}", st) for i in range(2)])
        qnr = Rot([P.sb([128, G], F32, f"qn{i}", st) for i in range(3)])
        vsr = Rot([P.sb([128, 512], F32, f"vs{i}", st) for i in range(3)])
        lfr = Rot([P.sb([16, G], F32, f"lf{i}", st) for i in range(2)])
        psA, psO, psW = Rot(PS["A"]), Rot(PS["O"]), Rot(PS["W"])
        for g in range(NG):
            gs = slice(g * G, (g + 1) * G)
            P.dma("sp", xg[:], xview(xin, g), xg, xin)
            emit_norm(P, C, xg, A, mod, hT, hTb, sqrot, PS["S"], rstd)
            for j in range(16):
                w = wr.next()
                P.dma("pool", w[:], W["wqk"][j], w, W["wqk"])
                ps = psA.next()
                for k in range(8):
                    P.op("pe", "matmul", [w, hTb], [ps], ps[:], lhsT=w[:, k, :], rhs=hTb[:, k, :], start=(k == 0),
                         stop=(k == 7), inc=(k == 7))
                q2, rs, qn = q2r.next(), rsr.next(), qnr.next()
                P.op("act", "activation", [ps], [q2], out=q2[:], in_=ps[:], func=AF.Square)
                pw = psW.next()
                P.op("pe", "matmul", [bd, q2], [pw], pw[:, 0:G], lhsT=bd[:], rhs=q2[:], start=True, stop=True)
                P.op("dve", "tensor_scalar", [pw], [rs], out=rs[:], in0=pw[:, 0:G], scalar1=1.0 / 64, scalar2=1e-6,
                     op0=ALU.mult, op1=ALU.add)
                P.op("act", "activation", [rs], [rs], out=rs[:], in_=rs[:], func=AF.Sqrt)
                P.op("dve", "reciprocal", [rs], [rs], out=rs[:], in_=rs[:])
                gcol = gq if j < 8 else gk
                P.op("dve", "scalar_tensor_tensor", [ps, gcol, rs], [qn], out=qn[:], in0=ps[:], scalar=gcol[:, 0:1],
                     in1=rs[:], op0=ALU.mult, op1=ALU.mult)
                P.dma("sp", qk_o[j // 2][(j % 2) * 128:(j % 2 + 1) * 128, gs], qn[:], qk_o[j // 2], qn)
            for n2 in range(2):
                wv = wvr.next()
                P.dma("pool", wv[:], W["wv"][n2], wv, W["wv"])
                for tt in range(4):
                    ps = psO.next()
                    for k in range(8):
                        P.op("pe", "matmul", [wv, hTb], [ps], ps[:], lhsT=hTb[:, k, tt * 128:(tt + 1) * 128],
                             rhs=wv[:, k, :], start=(k == 0), stop=(k == 7), inc=(k == 7))
                    vs = vsr.next()
                    P.op("act", "activation", [ps], [vs], out=vs[:], in_=ps[:], func=AF.Copy)
                    P.dma("sp", v_o[g][tt * 128:(tt + 1) * 128, n2 * 512:(n2 + 1) * 512], vs[:], v_o[g], vs)
            ps = psA.next()
            for k in range(8):
                P.op("pe", "matmul", [wfs, hT], [ps], ps[0:16, :], lhsT=wfs[:, k, :], rhs=hT[:, k, :], start=(k == 0),
                     stop=(k == 7), inc=(k == 7))
            lf = lfr.next()
            P.op("act", "activation", [ps, nfb], [lf], out=lf[:], in_=ps[0:16, :], func=AF.Exp, scale=-1.0,
                 bias=nfb[:, 0:1])
            P.op("act", "activation", [lf], [lf], out=lf[:], in_=lf[:], func=AF.Ln, bias=1.0, scale=1.0)
            P.op("dve", "tensor_scalar", [lf], [lf], out=lf[:], in0=lf[:], scalar1=-1.0, scalar2=None, op0=ALU.mult)
            P.dma("sp", lf_o[:, gs], lf[:], lf_o, lf)
            for j in range(8):
                w = wr.next()
                P.dma("pool", w[:], W["wog"][j], w, W["wog"])
                ps = psA.next()
                for k in range(8):
                    P.op("pe", "matmul", [w, hTb], [ps], ps[:], lhsT=w[:, k, :], rhs=hTb[:, k, :], start=(k == 0),
                         stop=(k == 7), inc=(k == 7))
                qn = qnr.next()
                P.op("act", "activation", [ps], [qn], out=qn[:], in_=ps[:], func=AF.Sigmoid)
                P.dma("sp", og_o[j * 128:(j + 1) * 128, gs], qn[:], og_o, qn)
        P.flush()
        P.recycle()


def emit_blend(P, dst_ap, dst, c0_ap, c1_ap, srcbuf, t0, t1, flag, nflag, np_):
    P.dma("sp", t0[0:np_], c0_ap, t0, srcbuf)
    P.dma("act", t1[0:np_], c1_ap, t1, srcbuf)
    P.op("pool", "tensor_scalar", [t1, flag], [t1], out=t1[0:np_], in0=t1[0:np_], scalar1=flag[0:np_, 0:1], scalar2=None,
         op0=ALU.mult)
    P.op("dve", "scalar_tensor_tensor", [t0, nflag, t1], [dst], out=dst_ap, in0=t0[0:np_], scalar=nflag[0:np_, 0:1],
         in1=t1[0:np_], op0=ALU.mult, op1=ALU.add)


def emit_blend2(P, dst_ap, dst, c0_ap, b0, c1_ap, b1, t0, t1, flag, nflag, np_):
    P.dma("sp", t0[0:np_], c0_ap, t0, b0)
    P.dma("act", t1[0:np_], c1_ap, t1, b1)
    P.op("pool", "tensor_scalar", [t1, flag], [t1], out=t1[0:np_], in0=t1[0:np_], scalar1=flag[0:np_, 0:1], scalar2=None,
         op0=ALU.mult)
    P.op("dve", "scalar_tensor_tensor", [t0, nflag, t1], [dst], out=dst_ap, in0=t0[0:np_], scalar=nflag[0:np_, 0:1],
         in1=t1[0:np_], op0=ALU.mult, op1=ALU.add)


def emit_fox_c(P, C, PS, W, qk_g, v_g, lf_g, o_s):
    with ExitStack() as st:
        flag = P.sb([128, 1], F32, "flag", st)
        nflag = P.sb([128, 1], F32, "nflag", st)
        P.dma("sp", flag[:], W["flag"][:], flag, W["flag"])
        P.op("dve", "tensor_scalar", [flag], [nflag], out=nflag[:], in0=flag[:], scalar1=-1.0, scalar2=1.0,
             op0=ALU.mult, op1=ALU.add)
        masks = P.sb([128, 4, 512], F32, "masks", st)
        P.dma("sp", masks[:], W["maskneg"].t.rearrange("j p t -> p j t"), masks, W["maskneg"])
        ones8 = P.sb([8, S4], F32, "ones8", st)
        P.op("pool", "memset", [], [ones8], ones8[:], 1.0)
        lfa = P.sb([8, S4], F32, "lfa", st)
        lfb = P.sb([8, S4], F32, "lfb", st)
        t0 = P.sb([128, T], F32, "bt0", st)
        t1 = P.sb([128, T], F32, "bt1", st)
        for r in range(2):
            rs_ = slice(r * T, (r + 1) * T)
            emit_blend(P, lfa[:, rs_], lfa, lf_g[r * 16:r * 16 + 8, :], lf_g[r * 16 + 8:r * 16 + 16, :], lf_g, t0, t1,
                       flag, nflag, 8)
        P.op("dve", "tensor_tensor_scan", [ones8, lfa], [lfb], out=lfb[:], data0=ones8[:], data1=lfa[:], initial=0.0,
             op0=ALU.mult, op1=ALU.add)
        P.op("dve", "tensor_scalar", [lfb], [lfa], out=lfa[:], in0=lfb[:], scalar1=8.0, scalar2=None, op0=ALU.mult)
        P.op("dve", "tensor_scalar", [lfb], [lfb], out=lfb[:], in0=lfb[:], scalar1=-8.0, scalar2=None, op0=ALU.mult)
        cs8, ncs8 = lfa, lfb
        qar = Rot([P.sb([128, S4], F32, f"qa{i}", st) for i in range(2)])
        kar = Rot([P.sb([128, S4], F32, f"ka{i}", st) for i in range(2)])
        vf = P.sb([128, 16, 64], F32, "vf", st)
        var_ = Rot([P.sb([128, 32, 65], BF16, f"va{i}", st) for i in range(2)])
        ptr = Rot([P.sb([128, 512], BF16, f"pt{i}", st) for i in range(4)])
        tmr = Rot([P.sb([128, 512], F32, f"tm{i}", st) for i in range(2)])
        rden = P.sb([128, 512], F32, "rden", st)
        bcs = P.sb([64, 512], F32, "bcs", st)
        osr = Rot([P.sb([64, 512], F32, f"os{i}", st) for i in range(2)])
        psS = Rot([PS["A"][0], PS["A"][1], PS["W"][0]])
        psBC = PS["W"][1]
        psO = Rot(PS["O"])
        v3 = [vg.t.rearrange("(r kt p) c -> p r kt c", r=2, p=128) for vg in v_g]

        def load_head(h, qa, ka, va):
            for r in range(2):
                rs_ = slice(r * T, (r + 1) * T)

                def cand(base_j, hg):
                    j = base_j + hg * 4 + h // 2
                    off = r * 256 + (j % 2) * 128 + (h % 2) * 64
                    return qk_g[j // 2], qk_g[j // 2][off:off + 64, :]
                (bq0, aq0), (bq1, aq1) = cand(0, 0), cand(0, 1)
                emit_blend2(P, qa[0:64, rs_], qa, aq0, bq0, aq1, bq1, t0, t1, flag, nflag, 64)
                yield
                (bk0, ak0), (bk1, ak1) = cand(8, 0), cand(8, 1)
                emit_blend2(P, ka[0:64, rs_], ka, ak0, bk0, ak1, bk1, t0, t1, flag, nflag, 64)
                yield
                c0, c1 = h * 64, 512 + h * 64
                t1v = t1[:, 0:1024].rearrange("p (a b) -> p a b", a=16)
                for i in range(4):
                    P.dma("sp", vf[:, i * 4:(i + 1) * 4, :], v3[i][:, r, :, c0:c0 + 64], vf, v_g[i])
                    P.dma("act", t1v[:, i * 4:(i + 1) * 4, :], v3[i][:, r, :, c1:c1 + 64], t1, v_g[i])
                P.op("pool", "tensor_scalar", [t1, flag], [t1], out=t1[:, 0:1024], in0=t1[:, 0:1024], scalar1=flag[:, 0:1],
                     scalar2=None, op0=ALU.mult)
                P.op("dve", "scalar_tensor_tensor", [vf, nflag, t1], [va], out=va[:, r * 16:(r + 1) * 16, 0:64],
                     in0=vf[:], scalar=nflag[:, 0:1], in1=t1v, op0=ALU.mult, op1=ALU.add)
                yield
            P.dma("sp", qa[64:65, :], cs8[h:h + 1, :], qa, cs8)
            P.dma("sp", ka[65:66, :], ncs8[h:h + 1, :], ka, ncs8)
            P.dma("sp", qa[65:66, :], ones8[0:1, :], qa, ones8)
            P.dma("sp", ka[64:65, :], ones8[0:1, :], ka, ones8)
            P.op("pool", "memset", [], [va], va[:, :, 64:65], 1.0)
            yield

        bufs = [(qar.next(), kar.next(), var_.next()) for _ in range(8)]
        for _ in load_head(0, *bufs[0]):
            pass
        for h in range(8):
            qa, ka, va = bufs[h]
            nxt = load_head(h + 1, *bufs[h + 1]) if h + 1 < 8 else iter(())
            items = [(qc, kt) for qc in range(8) for kt in range(4 * qc + 4)]
            LAG = 2
            pend = []
            o_cur = {}

            def qk(qc, kt):
                qs = slice(qc * 512, (qc + 1) * 512)
                ps = psS.next()
                P.op("pe", "matmul", [ka, qa], [ps], ps[:, 0:512], lhsT=ka[0:66, kt * 128:(kt + 1) * 128], rhs=qa[0:66, qs],
                     start=True, stop=True)
                pt = ptr.next()
                if kt >= 4 * qc:
                    tm = tmr.next()
                    P.op("dve", "tensor_tensor", [ps, masks], [tm], out=tm[:], in0=ps[:, 0:512], in1=masks[:, kt - 4 * qc, :],
                         op=ALU.add)
                    P.op("act", "activation", [tm], [pt], out=pt[:], in_=tm[:], func=AF.Exp, scale=0.125)
                else:
                    P.op("act", "activation", [ps], [pt], out=pt[:], in_=ps[:, 0:512], func=AF.Exp, scale=0.125)
                return pt

            def pv(qc, kt, pt):
                nk = 4 * qc + 4
                if kt == 0:
                    o_cur[qc] = psO.next()
                o_ps = o_cur[qc]
                P.op("pe", "matmul", [va, pt], [o_ps], o_ps[0:65, :], lhsT=va[:, kt, :], rhs=pt[:], start=(kt == 0),
                     stop=(kt == nk - 1), inc=True)
                if kt == nk - 1:
                    qs = slice(qc * 512, (qc + 1) * 512)
                    P.op("dve", "reciprocal", [o_ps], [rden], out=rden[64:65, :], in_=o_ps[64:65, :])
                    P.op("pe", "matmul", [C["ones"], rden], [psBC], psBC[0:64, 0:512], lhsT=C["ones"][64:65, 0:64],
                         rhs=rden[64:65, :], start=True, stop=True)
                    P.op("act", "activation", [psBC], [bcs], out=bcs[:], in_=psBC[0:64, 0:512], func=AF.Copy)
                    os_ = osr.next()
                    P.op("dve", "tensor_tensor", [o_ps, bcs], [os_], out=os_[:], in0=o_ps[0:64, :], in1=bcs[:], op=ALU.mult)
                    P.dma("sp", o_s[h // 2][(h % 2) * 64:(h % 2 + 1) * 64, qs], os_[:], o_s[h // 2], os_)
                    next(nxt, None)

            for (qc, kt) in items:
                pend.append((qc, kt, qk(qc, kt)))
                if len(pend) > LAG:
                    pv(*pend.pop(0))
            while pend:
                pv(*pend.pop(0))
            for _ in nxt:
                pass
        P.flush()
        P.recycle()


def emit_fox_d(P, C, PS, W, xin, o_g, og_i, xmid):
    with ExitStack() as s1:
        A, mod = emit_mod(P, PS["A"][0], W["c"], W["mix_g"], W["mix_mod_w"], W["mix_mod_b"], "dm", s1)
        flag = P.sb([128, 1], F32, "flag", s1)
        nflag = P.sb([128, 1], F32, "nflag", s1)
        P.dma("sp", flag[:], W["flag"][:], flag, W["flag"])
        P.op("dve", "tensor_scalar", [flag], [nflag], out=nflag[:], in0=flag[:], scalar1=-1.0, scalar2=1.0,
             op0=ALU.mult, op1=ALU.add)
        xg = P.sb([128, 8, G], F32, "xg", s1)
        og = P.sb([128, 8, G], F32, "og", s1)
        ogb = P.sb([128, 8, G], BF16, "ogb", s1)
        ot0 = P.sb([128, 8, G], F32, "ot0", s1)
        ot1 = P.sb([128, 8, G], F32, "ot1", s1)
        owr = Rot([P.sb([128, 8, 128], BF16, f"ow{i}", s1) for i in range(3)])
        psO = Rot(PS["O"])
        for g in range(NG):
            P.dma("sp", xg[:], xview(xin, g), xg, xin)
            P.dma("sp", og[:], xview(og_i, g), og, og_i)
            for k in range(8):
                og_k = o_g[k % 4][(k // 4) * 128:(k // 4 + 1) * 128, :]
                P.dma("sp", ot0[:, k, :], og_k[:, g * G:(g + 1) * G], ot0, o_g[k % 4])
                P.dma("act", ot1[:, k, :], og_k[:, T + g * G:T + (g + 1) * G], ot1, o_g[k % 4])
            P.op("pool", "tensor_scalar", [ot1, flag], [ot1], out=ot1[:], in0=ot1[:], scalar1=flag[:, 0:1], scalar2=None,
                 op0=ALU.mult)
            P.op("dve", "scalar_tensor_tensor", [ot0, nflag, ot1], [ot0], out=ot0[:], in0=ot0[:], scalar=nflag[:, 0:1],
                 in1=ot1[:], op0=ALU.mult, op1=ALU.add)
            P.op("dve", "tensor_tensor", [og, ot0], [ogb], out=ogb[:], in0=og[:], in1=ot0[:], op=ALU.mult)
            for dc in range(8):
                ow = owr.next()
                P.dma("pool", ow[:], W["fox_out_w"][dc], ow, W["fox_out_w"])
                o_ps = psO.next()
                for k in range(8):
                    P.op("pe", "matmul", [ow, ogb], [o_ps], o_ps[:], lhsT=ow[:, k, :], rhs=ogb[:, k, :], start=(k == 0),
                         stop=(k == 7), inc=(k == 7))
                P.op("dve", "scalar_tensor_tensor", [o_ps, mod, xg], [xg], out=xg[:, dc, :], in0=o_ps[:],
                     scalar=mod[:, 16 + dc:17 + dc], in1=xg[:, dc, :], op0=ALU.mult, op1=ALU.add)
            P.dma("sp", xview(xmid, g), xg[:], xmid, xg)
        P.flush()
        P.recycle()


def build_fused(ncores=NCORES, do_l0=True, do_peer1=True):
    nc = bass.Bass("TRN2", target_bir_lowering=False)
    PAIRS = [[2 * i, 2 * i + 1] for i in range(ncores // 2)]
    shared = {}
    W0 = WMap(nc, "l0_", shared)
    W1 = WMap(nc, "l1_", shared)
    xin = din(nc, "xT", [D, T])
    xprev = din(nc, "xTp", [D, T])
    xout = dout(nc, "xo", [D, T])
    with ExitStack() as st:
        P = Prog(nc, st)
        PS = alloc_psum(P)
        C = emit_consts(P)
        xmid0 = P.dram("xmid0", [D, T])
        x1 = P.dram("x1", [D, T])
        qk_s = [P.dram(f"qk_s{i}", [256, T]) for i in range(8)]
        qk_g = [P.dram(f"qk_g{i}", [512, T]) for i in range(8)]
        v_s = [P.dram(f"v_s{i}", [512, D]) for i in range(4)]
        v_g = [P.dram(f"v_g{i}", [1024, D]) for i in range(4)]
        lf_s = P.dram("lf_s", [16, T])
        lf_g = P.dram("lf_g", [2 * 16, T])
        ogs = P.dram("ogs", [D, T])
        o_s = [P.dram(f"o_s{i}", [128, S4]) for i in range(4)]
        o_g = [P.dram(f"o_g{i}", [256, S4]) for i in range(4)]
        xmid1 = P.dram("xmid1", [D, T])
        if do_l0:
            emit_lru(P, C, PS, W0, xin, xprev, xmid0)
            emit_peer(P, C, PS, W0, xmid0, x1, "l0")
        else:
            x1 = xin
        emit_fox_b(P, C, PS, W1, x1, qk_s, v_s, lf_s, ogs)
        P.coll("AllGather", PAIRS, lf_s, lf_g)
        for a, b_ in zip(qk_s + v_s, qk_g + v_g):
            P.coll("AllGather", PAIRS, a, b_)
        emit_fox_c(P, C, PS, W1, qk_g, v_g, lf_g, o_s)
        for a, b_ in zip(o_s, o_g):
            P.coll("AllGather", PAIRS, a, b_)
        if do_peer1:
            emit_fox_d(P, C, PS, W1, x1, o_g, ogs, xmid1)
            emit_peer(P, C, PS, W1, xmid1, xout, "l1")
        else:
            emit_fox_d(P, C, PS, W1, x1, o_g, ogs, xout)
        P.flush([xout])
    nc.used_inputs = ["xT", "xTp"] + W0.names() + [n for n in W1.names() if n not in W0.names()]
    return nc


def lay_fox(inp):
    d = {}
    d["mix_g"], d["mix_mod_w"], d["mix_mod_b"] = lay_mod(inp["l1_mix_norm_g"], inp["l1_mix_mod_w"], inp["l1_mix_mod_b"])
    iw = inp["l1_fox_in_w"]

    def blocks(cols, nblk, w):
        return np.ascontiguousarray(cols.reshape(8, 128, nblk, w).transpose(2, 1, 0, 3))
    d["wqk"] = blocks(iw[:, 0:2048], 16, 128)
    d["wv"] = blocks(iw[:, 2048:3072], 2, 512)
    d["wf"] = np.ascontiguousarray(iw[:, 3072:3088].reshape(8, 128, 16).transpose(1, 0, 2))
    d["wog"] = blocks(iw[:, 3088:4112], 8, 128)
    d["fb"] = np.ascontiguousarray(inp["l1_fox_f_b"].reshape(16, 1))
    d["gq"] = np.ascontiguousarray(np.tile(inp["l1_fox_q_norm_g"], 2).reshape(128, 1))
    d["gk"] = np.ascontiguousarray(np.tile(inp["l1_fox_k_norm_g"], 2).reshape(128, 1))
    d["fox_out_w"] = blocks(inp["l1_fox_out_w"], 8, 128)
    return d


def lay_l1peer(inp):
    d = {}
    d["ffn_g"], d["ffn_mod_w"], d["ffn_mod_b"] = lay_mod(inp["l1_ffn_norm_g"], inp["l1_ffn_mod_w"], inp["l1_ffn_mod_b"])
    d["qwT"], d["skT"], d["UT"], d["V"] = lay_peer(inp["l1_peer_q_w"], inp["l1_peer_subkey1"], inp["l1_peer_subkey2"],
                                                  inp["l1_peer_u"], inp["l1_peer_v"])
    return d


def maskneg():
    m = np.zeros((4, 128, 512), np.float32)
    tk = np.arange(128)[:, None]
    tq = np.arange(512)[None, :]
    for j in range(4):
        m[j] = np.where(tk + j * 128 > tq, -240000.0, 0.0)
    return m


def block_diag_ones():
    bd = np.zeros((128, 128), np.float32)
    bd[0:64, 0:64] = 1.0
    bd[64:128, 64:128] = 1.0
    return bd


_CACHE = {}


def kernel(**inp):
    inp = {k: np.asarray(v, dtype=np.float32) for k, v in inp.items()}
    if "nc" not in _CACHE:
        _CACHE["nc"] = build_fused()
    nc = _CACHE["nc"]
    sh = {}
    l0 = lay_l0(inp)
    l0.pop("ident")
    for k, v in l0.items():
        sh["l0_" + k] = v
    for k, v in {**lay_fox(inp), **lay_l1peer(inp)}.items():
        sh["l1_" + k] = v
    sh["ident"] = np.eye(128, dtype=np.float32)
    sh["bd"] = block_diag_ones()
    sh["maskneg"] = maskneg()
    maps = []
    x = inp["x"]
    for c in range(NCORES):
        b, hf = c // 2, c % 2
        m = dict(sh)
        m["xT"] = np.ascontiguousarray(x[b, hf * T:(hf + 1) * T, :].T)
        m["xTp"] = np.ascontiguousarray(x[b, 0:T, :].T) if hf else np.zeros((D, T), np.float32)
        m["flag"] = np.full((128, 1), float(hf), np.float32)
        m["c"] = np.ascontiguousarray(inp["c"][b].reshape(128, 8))
        maps.append({k: v for k, v in m.items() if k in nc.used_inputs})
    res = run_bass_kernel_spmd(nc, maps, core_ids=list(range(NCORES)))
    out = np.empty((4, S4, D), np.float32)
    for c in range(NCORES):
        b, hf = c // 2, c % 2
        out[b, hf * T:(hf + 1) * T, :] = res.results[c]["xo"].T
    return out
```

```python
import numpy as np
from contextlib import ExitStack
import concourse.bass as bass
import concourse.mybir as mybir
from concourse.bass_utils import run_bass_kernel_spmd

F32 = mybir.dt.float32
BF16 = mybir.dt.bfloat16
AF = mybir.ActivationFunctionType
ALU = mybir.AluOpType
AX = mybir.AxisListType

NCORES = 8
T = 2048
G = 512
NG = T // G
D = 1024
DR = 1408
NB = 16
BS = 88
NE = 16384


class Buf:
    __slots__ = ("t", "name", "last_w", "readers", "dsem", "dcnt")

    def __init__(self, t, name):
        self.t = t
        self.name = name
        self.last_w = None
        self.readers = {}
        self.dsem = None
        self.dcnt = 0

    def __getitem__(self, k):
        return self.t[k]


class Prog:
    def __init__(self, nc, stack):
        self.nc = nc
        self.stack = stack
        self.eng = {"pe": nc.tensor, "act": nc.scalar, "dve": nc.vector,
                    "pool": nc.gpsimd, "sp": nc.sync}
        self.sem = {}
        self.cnt = {}
        for e in self.eng:
            self.sem[e] = stack.enter_context(nc.semaphore("s_" + e))
            self.cnt[e] = 0
        self.waited = {}
        self.nbuf = 0
        self.q = {e: [] for e in self.eng}
        self.nblk = 0
        self.live = []
        self.sem_pool = []
        self.ccsem = stack.enter_context(nc.semaphore("s_cc"))
        self.cccnt = 0

    def sb(self, shape, dt=F32, name=None, stack=None):
        self.nbuf += 1
        name = (name or "b") + f"_{self.nbuf}"
        t = (stack or self.stack).enter_context(self.nc.sbuf_tensor(name, list(shape), dt))
        b = Buf(t, name)
        self.live.append(b)
        return b

    def recycle(self):
        for b in self.live:
            if b.dsem is not None:
                self.sem_pool.append((b.dsem, b.dcnt))
                b.dsem = None
        self.live = []

    def coll(self, kind, groups, src, dst):
        self._waits("pool", [src], [dst])
        self.q["pool"].append(("op", "collective_compute", (kind, ALU.bypass),
                               dict(replica_groups=groups, ins=[src.t.opt()], outs=[dst.t.opt()]),
                               (self.ccsem, None)))
        self.cccnt += 1
        tok = (self.ccsem, self.cccnt)
        src.readers[tok[0]] = max(src.readers.get(tok[0], 0), tok[1])
        dst.last_w = tok
        dst.readers = {}

    def ps(self, shape, dt=F32, name=None, stack=None):
        self.nbuf += 1
        name = (name or "p") + f"_{self.nbuf}"
        t = (stack or self.stack).enter_context(self.nc.psum_tensor(name, list(shape), dt))
        return Buf(t, name)

    def dram(self, name, shape, dt=F32, kind="Internal"):
        t = self.nc.dram_tensor(name, list(shape), dt, kind=kind)
        return Buf(t.ap(), name)

    def _waits(self, e, reads, writes):
        need = {}
        for b in reads:
            if b.last_w is not None:
                s, v = b.last_w
                need[s] = max(need.get(s, 0), v)
        for b in writes:
            if b.last_w is not None:
                s, v = b.last_w
                need[s] = max(need.get(s, 0), v)
            for s, v in b.readers.items():
                if s is self.sem.get(e):
                    continue
                need[s] = max(need.get(s, 0), v)
        for s, v in need.items():
            key = (e, id(s))
            if s is self.sem.get(e) and (e == "pe" or v > self.cnt[e]):
                continue
            if self.waited.get(key, 0) < v:
                self.q[e].append(("wait", s, v))
                self.waited[key] = v

    def op(self, e, meth, reads, writes, *args, inc=True, **kw):
        self._waits(e, reads, writes)
        self.q[e].append(("op", meth, args, kw, (self.sem[e], 1) if inc else None))
        if inc:
            self.cnt[e] += 1
            tok = (self.sem[e], self.cnt[e])
        else:
            tok = (self.sem[e], self.cnt[e] + 1)
        for b in reads:
            b.readers[tok[0]] = max(b.readers.get(tok[0], 0), tok[1])
        for b in writes:
            b.last_w = tok
            b.readers = {}

    def dma(self, q, out_ap, in_ap, dst, src, **kw):
        self._waits(q, [src], [dst])
        if dst.dsem is None:
            if self.sem_pool:
                dst.dsem, dst.dcnt = self.sem_pool.pop()
            else:
                dst.dsem = self.stack.enter_context(self.nc.semaphore("d_" + dst.name))
        kw = dict(kw, out=out_ap, in_=in_ap)
        self.q[q].append(("op", "dma_start", (), kw, (dst.dsem, 16)))
        dst.dcnt += 16
        tok = (dst.dsem, dst.dcnt)
        src.readers[tok[0]] = max(src.readers.get(tok[0], 0), tok[1])
        dst.last_w = tok
        dst.readers = {}

    def flush(self, final_bufs=()):
        for b in final_bufs:
            if b.last_w is not None:
                s, v = b.last_w
                self.q["sp"].append(("wait", s, v))
        nc = self.nc
        if not any(self.q.values()):
            return
        self.nblk += 1
        with nc.Block() as block:
            for e, starter in (("sp", block.sync), ("pe", block.tensor), ("act", block.scalar),
                               ("dve", block.vector), ("pool", block.gpsimd)):
                items = self.q[e]
                if not items:
                    continue

                def body(eng, items=items):
                    for it in items:
                        if it[0] == "wait":
                            eng.wait_ge(it[1], it[2])
                        else:
                            _, meth, args, kw, inc = it
                            ins = getattr(eng, meth)(*args, **kw)
                            if inc is not None:
                                if inc[1] is None:
                                    ins.then_inc(inc[0])
                                else:
                                    ins.then_inc(inc[0], inc[1])
                starter(body)
        self.q = {e: [] for e in self.eng}


class Rot:
    def __init__(self, bufs):
        self.bufs = bufs
        self.i = 0

    def next(self):
        b = self.bufs[self.i % len(self.bufs)]
        self.i += 1
        return b


def din(nc, name, shape, dt=F32):
    return Buf(nc.dram_tensor(name, list(shape), dt, kind="ExternalInput").ap(), name)


def dout(nc, name, shape, dt=F32):
    return Buf(nc.dram_tensor(name, list(shape), dt, kind="ExternalOutput").ap(), name)


def emit_consts(P):
    C = {}
    C["ones"] = P.sb([128, 128], F32, "ones")
    P.op("pool", "memset", [], [C["ones"]], C["ones"][:], 1.0)
    C["zb"] = P.sb([128, 512], BF16, "zb")
    P.op("pool", "memset", [], [C["zb"]], C["zb"][:], 0.0)
    return C


def emit_mod(P, ps_small, c_d, g_d, w_d, b_d, pfx, pst=None):
    A = P.sb([128, 8], F32, pfx + "A", pst)
    mod = P.sb([128, 24], F32, pfx + "mod", pst)
    with ExitStack() as st:
        cs = P.sb([128, 8], F32, "cs", st)
        gs = P.sb([128, 8], F32, "gs", st)
        bs = P.sb([128, 24], F32, "bs", st)
        sc = P.sb([128, 8], F32, "sc", st)
        P.dma("sp", cs[:], c_d[:], cs, c_d)
        P.dma("sp", gs[:], g_d[:], gs, g_d)
        P.dma("sp", bs[:], b_d[:], bs, b_d)
        P.op("act", "activation", [cs], [sc], out=sc[:], in_=cs[:], func=AF.Silu)
        wrot = Rot([P.sb([128, 8, 512], F32, f"wb{i}", st) for i in range(2)])
        modp = ps_small
        for s in range(6):
            wb = wrot.next()
            P.dma("sp", wb[:], w_d[:, :, s * 512:(s + 1) * 512], wb, w_d)
            for j in range(4):
                col = s * 4 + j
                for k in range(8):
                    P.op("pe", "matmul", [wb, sc], [modp], modp[:, col:col + 1],
                         lhsT=wb[:, k, j * 128:(j + 1) * 128], rhs=sc[:, k:k + 1],
                         start=(k == 0), stop=(k == 7), inc=(k == 7))
        P.op("dve", "tensor_tensor", [modp, bs], [mod], out=mod[:], in0=modp[:, 0:24], in1=bs[:], op=ALU.add)
        P.op("dve", "scalar_tensor_tensor", [mod, gs], [A], out=A[:], in0=mod[:, 8:16], scalar=1.0,
             in1=gs[:], op0=ALU.add, op1=ALU.mult)
        P.flush()
    return A, mod


def emit_norm(P, C, xg, A, mod, hT, hT_bf, sqrot, ssp, rstd):
    for k in range(8):
        s_ = sqrot.next()
        P.op("act", "activation", [xg], [s_], out=s_[:], in_=xg[:, k, :], func=AF.Square)
        P.op("pe", "matmul", [C["ones"], s_], [ssp], ssp[:], lhsT=C["ones"][:], rhs=s_[:],
             start=(k == 0), stop=(k == 7))
    P.op("dve", "tensor_scalar", [ssp], [rstd], out=rstd[:], in0=ssp[:], scalar1=1.0 / D, scalar2=1e-6,
         op0=ALU.mult, op1=ALU.add)
    P.op("act", "activation", [rstd], [rstd], out=rstd[:], in_=rstd[:], func=AF.Sqrt)
    P.op("dve", "reciprocal", [rstd], [rstd], out=rstd[:], in_=rstd[:])
    for k in range(8):
        P.op("dve", "tensor_tensor", [xg, rstd], [hT], out=hT[:, k, :], in0=xg[:, k, :], in1=rstd[:], op=ALU.mult)
        P.op("act", "activation", [hT, A, mod], [hT], out=hT[:, k, :], in_=hT[:, k, :], func=AF.Identity,
             scale=A[:, k:k + 1], bias=mod[:, k:k + 1])
        if hT_bf is not None:
            P.op("pool", "tensor_copy", [hT], [hT_bf], out=hT_bf[:, k, :], in_=hT[:, k, :])


def xview(x_d, g):
    return x_d.t.rearrange("(k p) t -> p k t", p=128)[:, :, g * G:(g + 1) * G]


def emit_lru(P, C, PS, W, xin_d, xprev_d, xout_d):
    with ExitStack() as st:
        A, mod = emit_mod(P, PS["A"][0], W["c"], W["mix_g"], W["mix_mod_w"], W["mix_mod_b"], "lm", st)
        cw = P.sb([128, NB, 4], F32, "cw", st)
        tabs = {}
        for nm in ("conv_b", "ra_b", "ri_b", "lam"):
            tabs[nm] = P.sb([128, NB], F32, nm, st)
            P.dma("sp", tabs[nm][0:BS, :], W[nm][:], tabs[nm], W[nm])
        P.dma("sp", cw[0:BS, :, :], W["conv_w"][:], cw, W["conv_w"])
        flag = P.sb([128, 1], F32, "flag", st)
        P.dma("sp", flag[:], W["flag"][:], flag, W["flag"])
        raw = P.sb([128, NB, BS], F32, "raw", st)
        riw = P.sb([128, NB, BS], F32, "riw", st)
        P.dma("sp", raw[0:BS, :, :], W["ra_w"].t.rearrange("n c d -> c n d"), raw, W["ra_w"])
        P.dma("sp", riw[0:BS, :, :], W["ri_w"].t.rearrange("n c d -> c n d"), riw, W["ri_w"])
        nls = P.sb([128, NB], F32, "nls", st)
        nls2 = P.sb([128, NB], F32, "nls2", st)
        lam = tabs["lam"]
        P.op("act", "activation", [lam], [nls], out=nls[0:BS, :], in_=lam[0:BS, :], func=AF.Exp, scale=-1.0)
        P.op("act", "activation", [nls], [nls], out=nls[0:BS, :], in_=nls[0:BS, :], func=AF.Ln, bias=1.0, scale=1.0)
        P.op("dve", "tensor_scalar", [nls], [nls2], out=nls2[0:BS, :], in0=nls[0:BS, :], scalar1=-16.0, scalar2=None,
             op0=ALU.mult)
        P.op("dve", "tensor_scalar", [nls], [nls], out=nls[0:BS, :], in0=nls[0:BS, :], scalar1=-8.0, scalar2=None,
             op0=ALU.mult)
        hist = P.sb([128, NB, 3], F32, "hist", st)
        carry = P.sb([128, NB], F32, "carry", st)
        P.op("pool", "memset", [], [hist], hist[:], 0.0)
        P.op("pool", "memset", [], [carry], carry[:], 0.0)

        xg = P.sb([128, 8, G], F32, "xg", st)
        hT = P.sb([128, 8, G], F32, "hT", st)
        hTb = P.sb([128, 8, G], BF16, "hTb", st)
        sqrot = Rot([P.sb([128, G], F32, f"sq{i}", st) for i in range(2)])
        rstd = P.sb([128, G], F32, "rstd", st)
        wxr = Rot([P.sb([128, 8, BS], BF16, f"wx{i}", st) for i in range(5)])
        wgr = Rot([P.sb([128, 8, BS], BF16, f"wg{i}", st) for i in range(5)])
        xbr = Rot([P.sb([128, G + 3], F32, f"xbuf{i}", st) for i in range(4)])

        def tmp(nm, n=2):
            return Rot([P.sb([128, G], F32, f"{nm}{i}", st) for i in range(n)])
        cvr, rr, ir, ar, a2r, ur, hsr, glr = [tmp(n, k) for n, k in (("cv", 7), ("r", 4), ("i", 4), ("a", 5), ("a2", 5),
                                                                      ("u", 4), ("hs", 4), ("gl", 3))]
        yT = P.sb([128, NB, G], BF16, "yT", st)
        owr = Rot([P.sb([128, NB, 128], BF16, f"ow{i}", st) for i in range(3)])
        psA, psO, psW = Rot(PS["A"]), Rot(PS["O"]), Rot(PS["W"])

        for step in range(2 * NG):
            main = step >= NG
            g = step % NG
            src = xin_d if main else xprev_d
            P.dma("sp", xg[:], xview(src, g), xg, src)
            emit_norm(P, C, xg, A, mod, hT, hTb, sqrot, PS["S"], rstd)
            sched = []

            def at(t, fn):
                sched.append((t, len(sched), fn))

            for n in range(NB):
                wx, xb_ps, xb, cv, pw = wxr.next(), psA.next(), xbr.next(), cvr.next(), psW.next()
                r, i_, a, a2, u, hs = rr.next(), ir.next(), ar.next(), a2r.next(), ur.next(), hsr.next()
                r_ps, i_ps = pw[0:BS, 0:G], pw[0:BS, G:2 * G]
                at(n - 3, lambda wx=wx, n=n: P.dma("pool", wx[:], W["in_w"][n], wx, W["in_w"]))

                def mm_x(wx=wx, xb_ps=xb_ps):
                    for k in range(8):
                        P.op("pe", "matmul", [wx, hTb], [xb_ps], xb_ps[0:BS, :], lhsT=wx[:, k, :], rhs=hTb[:, k, :],
                             start=(k == 0), stop=(k == 7), inc=(k == 7))
                at(n, mm_x)
                at(n + 1, lambda xb_ps=xb_ps, xb=xb: P.op("act", "activation", [xb_ps], [xb], out=xb[0:BS, 3:G + 3],
                                                          in_=xb_ps[0:BS, :], func=AF.Copy))
                at(n + 1, lambda xb=xb, n=n: P.op("dve", "tensor_copy", [hist], [xb], out=xb[0:BS, 0:3], in_=hist[0:BS, n, :]))
                at(n + 2, lambda xb=xb, n=n: P.op("dve", "tensor_copy", [xb], [hist], out=hist[0:BS, n, :], in_=xb[0:BS, G:G + 3]))
                at(n + 2, lambda xb=xb, cv=cv, n=n: P.op(
                    "act", "activation", [xb, cw, tabs["conv_b"]], [cv], out=cv[0:BS, :], in_=xb[0:BS, 3:G + 3],
                    func=AF.Identity, scale=cw[0:BS, n, 3:4], bias=tabs["conv_b"][0:BS, n:n + 1]))

                def conv3(xb=xb, cv=cv, n=n):
                    for k in range(3):
                        P.op("dve", "scalar_tensor_tensor", [xb, cw, cv], [cv], out=cv[0:BS, :], in0=xb[0:BS, k:k + G],
                             scalar=cw[0:BS, n, k:k + 1], in1=cv[0:BS, :], op0=ALU.mult, op1=ALU.add)
                at(n + 3, conv3)

                def gates(cv=cv, pw=pw, r_ps=r_ps, i_ps=i_ps, n=n):
                    P.op("pe", "matmul", [raw, cv], [pw], r_ps, lhsT=raw[0:BS, n, :], rhs=cv[0:BS, :], start=True, stop=True,
                         inc=False)
                    P.op("pe", "matmul", [riw, cv], [pw], i_ps, lhsT=riw[0:BS, n, :], rhs=cv[0:BS, :], start=True, stop=True)
                at(n + 4, gates)

                def sig(pw=pw, r_ps=r_ps, i_ps=i_ps, r=r, i_=i_, n=n):
                    P.op("act", "activation", [pw, tabs["ra_b"]], [r], out=r[0:BS, :], in_=r_ps, func=AF.Sigmoid,
                         bias=tabs["ra_b"][0:BS, n:n + 1], scale=1.0)
                    P.op("act", "activation", [pw, tabs["ri_b"]], [i_], out=i_[0:BS, :], in_=i_ps, func=AF.Sigmoid,
                         bias=tabs["ri_b"][0:BS, n:n + 1], scale=1.0)
                at(n + 5, sig)

                def exps(r=r, a=a, a2=a2, n=n):
                    P.op("act", "activation", [r, nls], [a], out=a[0:BS, :], in_=r[0:BS, :], func=AF.Exp,
                         scale=nls[0:BS, n:n + 1])
                    P.op("act", "activation", [r, nls2], [a2], out=a2[0:BS, :], in_=r[0:BS, :], func=AF.Exp,
                         scale=nls2[0:BS, n:n + 1])
                at(n + 6, exps)

                def pre_u(a2=a2, i_=i_, cv=cv, u=u):
                    P.op("dve", "tensor_scalar", [a2], [a2], out=a2[0:BS, :], in0=a2[0:BS, :], scalar1=1.0, scalar2=-1.0,
                         op0=ALU.min, op1=ALU.mult)
                    P.op("dve", "tensor_tensor", [i_, cv], [u], out=u[0:BS, :], in0=i_[0:BS, :], in1=cv[0:BS, :], op=ALU.mult)
                at(n + 7, pre_u)
                at(n + 8, lambda a2=a2: P.op("act", "activation", [a2], [a2], out=a2[0:BS, :], in_=a2[0:BS, :], func=AF.Sqrt,
                                             bias=1.0, scale=1.0))

                def scan(u=u, a2=a2, a=a, hs=hs, n=n):
                    P.op("dve", "tensor_tensor", [u, a2], [u], out=u[0:BS, :], in0=u[0:BS, :], in1=a2[0:BS, :], op=ALU.mult)
                    P.op("dve", "tensor_tensor_scan", [a, u, carry], [hs], out=hs[0:BS, :], data0=a[0:BS, :],
                         data1=u[0:BS, :], initial=carry[0:BS, n:n + 1], op0=ALU.mult, op1=ALU.add)
                    P.op("dve", "tensor_copy", [hs], [carry], out=carry[0:BS, n:n + 1], in_=hs[0:BS, G - 1:G])
                at(n + 9, scan)
                if main:
                    wg, gb_ps, gl = wgr.next(), psO.next(), glr.next()
                    at(n + 4, lambda wg=wg, n=n: P.dma("pool", wg[:], W["in_w"][NB + n], wg, W["in_w"]))

                    def mm_g(wg=wg, gb_ps=gb_ps):
                        for k in range(8):
                            P.op("pe", "matmul", [wg, hTb], [gb_ps], gb_ps[0:BS, :], lhsT=wg[:, k, :], rhs=hTb[:, k, :],
                                 start=(k == 0), stop=(k == 7), inc=(k == 7))
                    at(n + 8, mm_g)
                    at(n + 9, lambda gb_ps=gb_ps, gl=gl: P.op("act", "activation", [gb_ps], [gl], out=gl[0:BS, :],
                                                              in_=gb_ps[0:BS, :], func=AF.Gelu))
                    at(n + 11, lambda hs=hs, gl=gl, n=n: P.op("pool", "tensor_tensor", [hs, gl], [yT], out=yT[0:BS, n, :],
                                                               in0=hs[0:BS, :], in1=gl[0:BS, :], op=ALU.mult))
            sched.sort(key=lambda x: (x[0], x[1]))
            for _, _, fn in sched:
                fn()
            if not main and g == NG - 1:
                P.op("dve", "tensor_scalar", [hist, flag], [hist], out=hist[0:BS, :, :], in0=hist[0:BS, :, :],
                     scalar1=flag[0:BS, 0:1], scalar2=None, op0=ALU.mult)
                P.op("dve", "tensor_scalar", [carry, flag], [carry], out=carry[0:BS, :], in0=carry[0:BS, :],
                     scalar1=flag[0:BS, 0:1], scalar2=None, op0=ALU.mult)
            if main:
                for dc in range(8):
                    ow = owr.next()
                    P.dma("pool", ow[0:BS, :, :], W["out_w"][dc], ow, W["out_w"])
                    o_ps = psO.next()
                    for n in range(NB):
                        P.op("pe", "matmul", [ow, yT], [o_ps], o_ps[:], lhsT=ow[0:BS, n, :], rhs=yT[0:BS, n, :],
                             start=(n == 0), stop=(n == NB - 1), inc=(n == NB - 1))
                    P.op("dve", "scalar_tensor_tensor", [o_ps, mod, xg], [xg], out=xg[:, dc, :], in0=o_ps[:],
                         scalar=mod[:, 16 + dc:17 + dc], in1=xg[:, dc, :], op0=ALU.mult, op1=ALU.add)
                P.dma("sp", xview(xout_d, g), xg[:], xout_d, xg)
        P.flush()
        P.recycle()


def emit_peer(P, C, PS, W, xin_d, xout_d, pfx):
    nc = P.nc
    with ExitStack() as st:
        A, mod = emit_mod(P, PS["A"][0], W["c"], W["ffn_g"], W["ffn_mod_w"], W["ffn_mod_b"], pfx + "pm", st)
        wf_d = P.dram(pfx + "wf", [128, 8, 2048], F32)
        with ExitStack() as s2:
            qbr = Rot([P.sb([128, D], F32, f"qb{i}", s2) for i in range(2)])
            skr = Rot([P.sb([128, 128], F32, f"sk{i}", s2) for i in range(2)])
            wfr = Rot([P.sb([128, 8, 128], F32, f"wfs{i}", s2) for i in range(2)])
            psW = Rot(PS["W"])
            for blk in range(16):
                qb, sk, wfs, pw = qbr.next(), skr.next(), wfr.next(), psW.next()
                P.dma("sp", qb[:], W["qwT"][blk], qb, W["qwT"])
                P.dma("sp", sk[:], W["skT"][blk], sk, W["skT"])
                for dc in range(8):
                    P.op("pe", "matmul", [qb, sk], [pw], pw[:, dc * 128:(dc + 1) * 128],
                         lhsT=qb[:, dc * 128:(dc + 1) * 128], rhs=sk[:], start=True, stop=True, inc=(dc == 7))
                P.op("act", "activation", [pw], [wfs], out=wfs[:].rearrange("p a b -> p (a b)"), in_=pw[:], func=AF.Copy)
                P.dma("sp", wf_d[:, :, blk * 128:(blk + 1) * 128], wfs[:], wf_d, wfs)
            P.flush()

        identf = P.sb([128, 128], F32, "identf", st)
        ident = P.sb([128, 128], BF16, "ident", st)
        P.dma("sp", identf[:], W["ident"][:], identf, W["ident"])
        P.op("dve", "tensor_copy", [identf], [ident], out=ident[:], in_=identf[:])

        xg = P.sb([128, 8, G], F32, "xg", st)
        hT = P.sb([128, 8, G], F32, "hT", st)
        hTb = P.sb([128, 8, G], BF16, "hTb", st)
        sqrot = Rot([P.sb([128, G], F32, f"sq{i}", st) for i in range(2)])
        rstd = P.sb([128, G], F32, "rstd", st)
        s_sb = [P.sb([128, 16, 128], F32, f"s_sb{i}", st) for i in range(4)]
        wfpr = Rot([P.sb([128, 8, 128], F32, f"wfp{i}", st) for i in range(2)])
        vtop = P.sb([128, 16, 16], F32, "vtop", st)
        tmp128 = P.sb([128, 128], F32, "tmp128", st)
        candr = Rot([P.sb([128, 256], F32, f"cand{i}", st) for i in range(2)])
        ctmp = P.sb([128, 256], F32, "ctmp", st)
        ctmp2 = P.sb([128, 256], F32, "ctmp2", st)
        ttop = P.sb([128, 8, 24], F32, "ttop", st)
        etop = P.sb([128, 16, 16], F32, "etop", st)
        mneg = P.sb([128, 8], F32, "mneg", st)
        Zs = P.sb([128, 8], F32, "Zs", st)
        tmid = P.sb([128, 8], F32, "tmid", st)
        thr = [P.sb([128, 8], F32, f"thr{i}", st) for i in range(4)]
        Er = Rot([P.sb([128, 8, 128], F32, f"E{i}", st) for i in range(3)])
        Mr = Rot([P.sb([128, 8, 128], BF16, f"M{i}", st) for i in range(3)])
        Kr = Rot([P.sb([128, 8, 128], BF16, f"K{i}", st) for i in range(3)])
        dg = P.sb([128, 4, 8, 128], BF16, "dg", st)
        Wtr = Rot([P.sb([128, 8, 128], BF16, f"Wt{i}", st) for i in range(2)])
        wTs = [P.sb([128, 8, G], BF16, f"wT{i}", st) for i in range(2)]
        Ur = Rot([P.sb([128, 8, 128], BF16, f"U{i}", st) for i in range(3)])
        NV = 16
        Vr = [P.sb([128, D], BF16, f"V{i}", st) for i in range(NV)]
        WAr = [P.sb([128, G], BF16, f"WA{i}", st) for i in range(NV)]
        Gr = Rot([P.sb([128, G], BF16, f"G{i}", st) for i in range(2)])
        otr = Rot([P.sb([128, G], F32, f"ot{i}", st) for i in range(2)])
        psA, psO = Rot(PS["A"]), Rot(PS["O"])
        wacc, wtp = PS["W"][0], PS["W"][1]
        ev = 0
        ACT_HEADS = ()
        DVE_HEADS = (0, 2, 4, 5, 7)

        for g in range(NG):
            P.dma("sp", xg[:], xview(xin_d, g), xg, xin_d)
            emit_norm(P, C, xg, A, mod, hT, hTb, sqrot, PS["S"], rstd)
            for ns in range(16):
                wfp = wfpr.next()
                P.dma("sp", wfp[:], wf_d[:, :, ns * 128:(ns + 1) * 128], wfp, wf_d)
                for tt in range(4):
                    ps = psA.next()
                    for k in range(8):
                        P.op("pe", "matmul", [hT, wfp], [ps], ps[:, 0:128], lhsT=hT[:, k, tt * 128:(tt + 1) * 128],
                             rhs=wfp[:, k, :], start=(k == 0), stop=(k == 7), inc=(k == 7))
                    dst = s_sb[tt][:, ns, :]
                    if ev % 2 == 0:
                        P.op("act", "activation", [ps], [s_sb[tt]], out=dst, in_=ps[:, 0:128], func=AF.Copy)
                    else:
                        P.op("dve", "tensor_copy", [ps], [s_sb[tt]], out=dst, in_=ps[:, 0:128])
                    ev += 1
            for tt in range(4):
                s_ = s_sb[tt]
                for blk in range(16):
                    P.op("dve", "max", [s_], [vtop], out=vtop[:, blk, 0:8], in_=s_[:, blk, :])
                    P.op("dve", "match_replace", [vtop, s_], [tmp128], out=tmp128[:], in_to_replace=vtop[:, blk, 0:8],
                         in_values=s_[:, blk, :], imm_value=-1e30)
                    P.op("dve", "max", [tmp128], [vtop], out=vtop[:, blk, 8:16], in_=tmp128[:])
                P.op("dve", "scalar_tensor_tensor", [vtop], [mneg], out=mneg[:], in0=vtop[:, 0:16:2, 0], scalar=-1.0,
                     in1=vtop[:, 1:16:2, 0], op0=ALU.mult, op1=ALU.subtract)
                for h in range(8):
                    P.op("act", "activation", [vtop, mneg], [etop], out=etop[:, 2 * h, :], in_=vtop[:, 2 * h, :],
                         func=AF.Exp, bias=mneg[:, h:h + 1], scale=1.0)
                    P.op("act", "activation", [s_, mneg], [s_], out=s_[:, 2 * h, :], in_=s_[:, 2 * h, :], func=AF.Exp,
                         bias=mneg[:, h:h + 1], scale=1.0)
                P.op("act", "activation", [vtop], [etop], out=etop[:, 1:16:2, :], in_=vtop[:, 1:16:2, :], func=AF.Exp)
                P.op("act", "activation", [s_], [s_], out=s_[:, 1:16:2, :], in_=s_[:, 1:16:2, :], func=AF.Exp)
                for h in range(8):
                    in0 = etop[:, 2 * h, :].unsqueeze(2).to_broadcast([128, 16, 16])
                    in1 = etop[:, 2 * h + 1, :].unsqueeze(1).to_broadcast([128, 16, 16])
                    cand = candr.next()
                    P.op("dve", "tensor_tensor", [etop], [cand],
                         out=cand[:].rearrange("p (a b) -> p a b", a=16), in0=in0, in1=in1, op=ALU.mult)
                    P.op("dve", "max", [cand], [ttop], out=ttop[:, h, 0:8], in_=cand[:])
                    P.op("dve", "match_replace", [ttop, cand], [ctmp], out=ctmp[:], in_to_replace=ttop[:, h, 0:8],
                         in_values=cand[:], imm_value=-1.0)
                    P.op("dve", "max", [ctmp], [ttop], out=ttop[:, h, 8:16], in_=ctmp[:])
                P.op("dve", "tensor_reduce", [ttop], [Zs], out=Zs[:], in_=ttop[:, :, 0:16], axis=AX.X, op=ALU.add)
                P.op("dve", "reciprocal", [Zs], [Zs], out=Zs[:], in_=Zs[:])
                P.op("dve", "scalar_tensor_tensor", [ttop, Zs], [thr[tt]], out=thr[tt][:], in0=ttop[:, :, 15],
                     scalar=1.0 - 2e-6, in1=Zs[:], op0=ALU.mult, op1=ALU.mult)
                for h in range(8):
                    P.op("pool", "tensor_scalar", [s_, Zs], [s_], out=s_[:, 2 * h, :], in0=s_[:, 2 * h, :],
                         scalar1=Zs[:, h:h + 1], scalar2=None, op0=ALU.mult)
                    P.op("pool", "tensor_scalar", [ident, thr[tt]], [dg], out=dg[:, tt, h, :], in0=ident[:],
                         scalar1=thr[tt][:, h:h + 1], scalar2=None, op0=ALU.mult)

            sched = []

            def at(t, fn):
                sched.append((t, len(sched), fn))

            NIC = 16
            for step in range(NIC + 2):
                for tt in range(4):
                    n0 = (step * 4 + tt) * 8
                    if step < NIC:
                        ic = step
                        s_ = s_sb[tt]
                        for h in range(8):
                            E, M = Er.next(), Mr.next()
                            in0 = s_[:, 2 * h, ic * 8:(ic + 1) * 8].unsqueeze(2).to_broadcast([128, 8, 128])
                            in1 = s_[:, 2 * h + 1, :].unsqueeze(1).to_broadcast([128, 8, 128])
                            eng = "dve" if h in DVE_HEADS else "pool"
                            at(n0 + h - 2, lambda eng=eng, s_=s_, E=E, in0=in0, in1=in1: P.op(
                                eng, "tensor_tensor", [s_], [E], out=E[:], in0=in0, in1=in1, op=ALU.mult))
                            Kb = Kr.next()
                            at(n0 + h - 1, lambda E=E, M=M, tt=tt, h=h: P.op(
                                "dve", "tensor_scalar", [E, thr[tt]], [M], out=M[:], in0=E[:], scalar1=thr[tt][:, h:h + 1],
                                scalar2=0.0, op0=ALU.subtract, op1=ALU.max))
                            at(n0 + h, lambda M=M, Kb=Kb: P.op("act", "activation", [M], [Kb], out=Kb[:], in_=M[:],
                                                                func=AF.Sign))

                            def acc(M=M, Kb=Kb, h=h, tt=tt):
                                for hb in range(2):
                                    P.op("pe", "matmul", [ident, M], [wacc], wacc[:, hb * 512:(hb + 1) * 512], lhsT=ident[:],
                                         rhs=M[:, hb * 4:(hb + 1) * 4, :].rearrange("p a b -> p (a b)"), start=(h == 0),
                                         stop=False, inc=False)
                                for hb in range(2):
                                    P.op("pe", "matmul", [dg, Kb], [wacc], wacc[:, hb * 512:(hb + 1) * 512],
                                         lhsT=dg[:, tt, h, :], rhs=Kb[:, hb * 4:(hb + 1) * 4, :].rearrange("p a b -> p (a b)"),
                                         start=False, stop=(h == 7), inc=(hb == 1))
                            at(n0 + h + 1, acc)
                        Wt = Wtr.next()
                        at(n0 + 9, lambda Wt=Wt: P.op("act", "activation", [wacc], [Wt],
                                                     out=Wt[:].rearrange("p a b -> p (a b)"), in_=wacc[:], func=AF.Copy))

                        def tr(Wt=Wt):
                            for i in range(8):
                                P.op("pe", "matmul", [Wt, ident], [wtp], wtp[:, i * 128:(i + 1) * 128], lhsT=Wt[:, i, :],
                                     rhs=ident[:], start=True, stop=True, inc=(i == 7))
                        at(n0 + 11, tr)
                        at(n0 + 12, lambda ic=ic, tt=tt: P.op(
                            "act", "activation", [wtp], [wTs[ic % 2]], out=wTs[ic % 2][:, :, tt * 128:(tt + 1) * 128],
                            in_=wtp[:].rearrange("p (a b) -> p a b", a=8), func=AF.Copy))
                    if 1 <= step <= NIC:
                        ic = step - 1
                        for q_, i in enumerate((2 * tt, 2 * tt + 1)):
                            e = ic * 8 + i
                            Vc, Uc, a_ps, Gt, WA = Vr[e % NV], Ur.next(), psA.next(), Gr.next(), WAr[e % NV]
                            ta = n0 + 4 * q_ + 3
                            at(max(step * 32, ta - 12), lambda Vc=Vc, e=e: P.dma("pool", Vc[:], W["V"][e * 128:(e + 1) * 128, :], Vc, W["V"]))
                            at(ta - 6, lambda Uc=Uc, e=e: P.dma("pool", Uc[:], W["UT"][e], Uc, W["UT"]))

                            def amm(Uc=Uc, a_ps=a_ps):
                                for k in range(8):
                                    P.op("pe", "matmul", [Uc, hTb], [a_ps], a_ps[:], lhsT=Uc[:, k, :], rhs=hTb[:, k, :],
                                         start=(k == 0), stop=(k == 7), inc=(k == 7))
                            at(ta, amm)
                            at(ta + 1, lambda a_ps=a_ps, Gt=Gt: P.op("act", "activation", [a_ps], [Gt], out=Gt[:],
                                                                      in_=a_ps[:], func=AF.Gelu))
                            at(ta + 2, lambda Gt=Gt, WA=WA, ic=ic, i=i: P.op(
                                "dve", "tensor_tensor", [Gt, wTs[ic % 2]], [WA], out=WA[:], in0=Gt[:],
                                in1=wTs[ic % 2][:, i, :], op=ALU.mult))
                    if 2 <= step <= NIC + 1:
                        ic = step - 2
                        for q_, dc in enumerate((2 * tt, 2 * tt + 1)):
                            o_ps, ot = psO.next(), otr.next()
                            tv = n0 + 4 * q_ + 2

                            def vmm(ic=ic, dc=dc, o_ps=o_ps):
                                for i in range(8):
                                    e = ic * 8 + i
                                    P.op("pe", "matmul", [Vr[e % NV], WAr[e % NV]], [o_ps], o_ps[:],
                                         lhsT=Vr[e % NV][:, dc * 128:(dc + 1) * 128], rhs=WAr[e % NV][:], start=(i == 0),
                                         stop=(i == 7), inc=(i == 7))
                            at(tv, vmm)
                            at(tv + 1, lambda o_ps=o_ps, ot=ot, dc=dc: P.op(
                                "act", "activation", [o_ps, mod], [ot], out=ot[:], in_=o_ps[:], func=AF.Copy,
                                scale=mod[:, 16 + dc:17 + dc]))
                            at(tv + 3, lambda ot=ot, dc=dc: P.op("pool", "tensor_tensor", [ot, xg], [xg], out=xg[:, dc, :],
                                                                 in0=ot[:], in1=xg[:, dc, :], op=ALU.add))
            sched.sort(key=lambda x: (x[0], x[1]))
            for _, _, fn in sched:
                fn()
            P.dma("sp", xview(xout_d, g), xg[:], xout_d, xg)
        P.flush()
        P.recycle()


def alloc_psum(P):
    PS = {}
    PS["A"] = [P.ps([128, 512], F32, f"psA{i}") for i in range(2)]
    PS["O"] = [P.ps([128, 512], F32, f"psO{i}") for i in range(2)]
    PS["W"] = [P.ps([128, 1024], F32, f"psW{i}") for i in range(2)]
    PS["S"] = PS["O"][0]
    return PS


SHAPES = {"c": [128, 8], "flag": [128, 1], "ident": [128, 128],
          "mix_g": [128, 8], "mix_mod_w": [128, 8, 3072], "mix_mod_b": [128, 24],
          "in_w": [2 * NB, 128, 8, BS], "conv_w": [BS, NB, 4], "conv_b": [BS, NB],
          "ra_w": [NB, BS, BS], "ra_b": [BS, NB], "ri_w": [NB, BS, BS], "ri_b": [BS, NB],
          "lam": [BS, NB], "out_w": [8, BS, NB, 128],
          "ffn_g": [128, 8], "ffn_mod_w": [128, 8, 3072], "ffn_mod_b": [128, 24],
          "qwT": [16, 128, D], "skT": [16, 128, 128], "UT": [128, 128, 8, 128], "V": [NE, D]}


class WMap(dict):
    GLOBAL = ("c", "flag", "ident", "bd", "maskneg")

    def __init__(self, nc, prefix="", shared=None):
        super().__init__()
        self.nc = nc
        self.prefix = prefix
        self.shared = shared if shared is not None else {}

    def __missing__(self, k):
        if k in self.GLOBAL:
            if k not in self.shared:
                self.shared[k] = din(self.nc, k, SHAPES[k])
            b = self.shared[k]
        else:
            b = din(self.nc, self.prefix + k, SHAPES[k])
        self[k] = b
        return b

    def names(self):
        return [(k if k in self.GLOBAL else self.prefix + k) for k in self.keys()]


def build_l0(do_lru=True, do_peer=True):
    nc = bass.Bass("TRN2", target_bir_lowering=False)
    W = WMap(nc)
    xin = din(nc, "xT", [D, T])
    xprev = din(nc, "xTp", [D, T]) if do_lru else None
    xout = dout(nc, "xo", [D, T])
    with ExitStack() as st:
        P = Prog(nc, st)
        PS = alloc_psum(P)
        C = emit_consts(P)
        if do_lru and do_peer:
            xmid = P.dram("xmid", [D, T], F32)
        else:
            xmid = xout
        if do_lru:
            emit_lru(P, C, PS, W, xin, xprev, xmid)
        if do_peer:
            emit_peer(P, C, PS, W, xmid if do_lru else xin, xout, "l0")
        P.flush([xout])
    nc.used_inputs = ["xT"] + (["xTp"] if do_lru else []) + W.names()
    return nc


def col8(v):
    return np.ascontiguousarray(v.reshape(-1, 128).T)


def blk16(v):
    return np.ascontiguousarray(v.reshape(NB, BS).T)


def lay_mod(g, w, b):
    return col8(g), np.ascontiguousarray(w.reshape(128, 8, 3072)), col8(b)


def lay_peer(q_w, sk1, sk2, u, v):
    qwT = np.ascontiguousarray(q_w.T.reshape(16, 128, D))
    skT = np.empty((16, 128, 128), np.float32)
    skT[0::2] = sk1.transpose(0, 2, 1)
    skT[1::2] = sk2.transpose(0, 2, 1)
    UT = np.ascontiguousarray(u.reshape(128, 128, 8, 128).transpose(0, 3, 2, 1))
    return qwT, skT, UT, np.ascontiguousarray(v)


def lay_l0(inp):
    d = {}
    d["mix_g"], d["mix_mod_w"], d["mix_mod_b"] = lay_mod(inp["l0_mix_norm_g"], inp["l0_mix_mod_w"], inp["l0_mix_mod_b"])
    d["ffn_g"], d["ffn_mod_w"], d["ffn_mod_b"] = lay_mod(inp["l0_ffn_norm_g"], inp["l0_ffn_mod_w"], inp["l0_ffn_mod_b"])
    iw = inp["l0_lru_in_w"].reshape(8, 128, 2 * NB, BS)
    d["in_w"] = np.ascontiguousarray(iw.transpose(2, 1, 0, 3))
    d["conv_w"] = np.ascontiguousarray(inp["l0_lru_conv_w"].reshape(4, NB, BS).transpose(2, 1, 0))
    d["conv_b"] = blk16(inp["l0_lru_conv_b"])
    d["ra_w"] = np.ascontiguousarray(inp["l0_lru_ra_w"])
    d["ri_w"] = np.ascontiguousarray(inp["l0_lru_ri_w"])
    d["ra_b"] = blk16(inp["l0_lru_ra_b"])
    d["ri_b"] = blk16(inp["l0_lru_ri_b"])
    d["lam"] = blk16(inp["l0_lru_lambda"])
    ow = inp["l0_lru_out_w"].reshape(NB, BS, 8, 128)
    d["out_w"] = np.ascontiguousarray(ow.transpose(2, 1, 0, 3))
    d["qwT"], d["skT"], d["UT"], d["V"] = lay_peer(inp["l0_peer_q_w"], inp["l0_peer_subkey1"], inp["l0_peer_subkey2"],
                                                  inp["l0_peer_u"], inp["l0_peer_v"])
    d["ident"] = np.eye(128, dtype=np.float32)
    return d


def core_maps_l0(inp, shared, cores):
    maps = []
    x = inp["x"]
    for c in cores:
        b, hf = c // 2, c % 2
        m = dict(shared)
        m["xT"] = np.ascontiguousarray(x[b, hf * T:(hf + 1) * T, :].T)
        m["xTp"] = np.ascontiguousarray(x[b, 0:T, :].T) if hf else np.zeros((D, T), np.float32)
        m["flag"] = np.full((128, 1), float(hf), np.float32)
        m["c"] = np.ascontiguousarray(inp["c"][b].reshape(128, 8))
        maps.append(m)
    return maps


SHAPES.update({"wqk": [16, 128, 8, 128], "wv": [2, 128, 8, 512], "wf": [128, 8, 16], "wog": [8, 128, 8, 128],
               "fb": [16, 1], "gq": [128, 1], "gk": [128, 1], "bd": [128, 128],
               "fox_out_w": [8, 128, 8, 128], "maskneg": [4, 128, 512]})
S4 = 4096
PAIRS = [[0, 1], [2, 3], [4, 5], [6, 7]]


def emit_fox_b(P, C, PS, W, xin, qk_o, v_o, lf_o, og_o):
    with ExitStack() as st:
        A, mod = emit_mod(P, PS["A"][0], W["c"], W["mix_g"], W["mix_mod_w"], W["mix_mod_b"], "fm", st)
        bd = P.sb([128, 128], F32, "bd", st)
        P.dma("sp", bd[:], W["bd"][:], bd, W["bd"])
        gq = P.sb([128, 1], F32, "gq", st)
        gk = P.sb([128, 1], F32, "gk", st)
        nfb = P.sb([16, 1], F32, "nfb", st)
        P.dma("sp", gq[:], W["gq"][:], gq, W["gq"])
        P.dma("sp", gk[:], W["gk"][:], gk, W["gk"])
        P.dma("sp", nfb[:], W["fb"][:], nfb, W["fb"])
        P.op("dve", "tensor_scalar", [nfb], [nfb], out=nfb[:], in0=nfb[:], scalar1=-1.0, scalar2=None, op0=ALU.mult)
        xg = P.sb([128, 8, G], F32, "xg", st)
        hT = P.sb([128, 8, G], F32, "hT", st)
        hTb = P.sb([128, 8, G], BF16, "hTb", st)
        sqrot = Rot([P.sb([128, G], F32, f"sq{i}", st) for i in range(2)])
        rstd = P.sb([128, G], F32, "rstd", st)
        wr = Rot([P.sb([128, 8, 128], BF16, f"w{i}", st) for i in range(4)])
        wvr = Rot([P.sb([128, 8, 512], BF16, f"wv{i}", st) for i in range(2)])
        wfs = P.sb([128, 8, 16], F32, "wfs", st)
        P.dma("sp", wfs[:], W["wf"][:], wfs, W["wf"])
        q2r = Rot([P.sb([128, G], F32, f"q2{i}", st) for i in range(2)])
        rsr = Rot([P.sb([128, G], F32, f"rs{i}", st) for i in range(2)])
        qnr = Rot([P.sb([128, G], F32, f"qn{i}", st) for i in range(3)])
        vsr = Rot([P.sb([128, 512], F32, f"vs{i}", st) for i in range(3)])
        lfr = Rot([P.sb([16, G], F32, f"lf{i}", st) for i in range(2)])
        psA, psO, psW = Rot(PS["A"]), Rot(PS["O"]), Rot(PS["W"])
        for g in range(NG):
            gs = slice(g * G, (g + 1) * G)
            P.dma("sp", xg[:], xview(xin, g), xg, xin)
            emit_norm(P, C, xg, A, mod, hT, hTb, sqrot, PS["S"], rstd)
            for j in range(16):
                w = wr.next()
                P.dma("pool", w[:], W["wqk"][j], w, W["wqk"])
                ps = psA.next()
                for k in range(8):
                    P.op("pe", "matmul", [w, hTb], [ps], ps[:], lhsT=w[:, k, :], rhs=hTb[:, k, :], start=(k == 0),
                         stop=(k == 7), inc=(k == 7))
                q2, rs, qn = q2r.next(), rsr.next(), qnr.next()
                P.op("act", "activation", [ps], [q2], out=q2[:], in_=ps[:], func=AF.Square)
                pw = psW.next()
                P.op("pe", "matmul", [bd, q2], [pw], pw[:, 0:G], lhsT=bd[:], rhs=q2[:], start=True, stop=True)
                P.op("dve", "tensor_scalar", [pw], [rs], out=rs[:], in0=pw[:, 0:G], scalar1=1.0 / 64, scalar2=1e-6,
                     op0=ALU.mult, op1=ALU.add)
                P.op("act", "activation", [rs], [rs], out=rs[:], in_=rs[:], func=AF.Sqrt)
                P.op("dve", "reciprocal", [rs], [rs], out=rs[:], in_=rs[:])
                gcol = gq if j < 8 else gk
                P.op("dve", "scalar_tensor_tensor", [ps, gcol, rs], [qn], out=qn[:], in0=ps[:], scalar=gcol[:, 0:1],
                     in1=rs[:], op0=ALU.mult, op1=ALU.mult)
                P.dma("sp", qk_o[j // 2][(j % 2) * 128:(j % 2 + 1) * 128, gs], qn[:], qk_o[j // 2], qn)
            for n2 in range(2):
                wv = wvr.next()
                P.dma("pool", wv[:], W["wv"][n2], wv, W["wv"])
                for tt in range(4):
                    ps = psO.next()
                    for k in range(8):
                        P.op("pe", "matmul", [wv, hTb], [ps], ps[:], lhsT=hTb[:, k, tt * 128:(tt + 1) * 128],
                             rhs=wv[:, k, :], start=(k == 0), stop=(k == 7), inc=(k == 7))
                    vs = vsr.next()
                    P.op("act", "activation", [ps], [vs], out=vs[:], in_=ps[:], func=AF.Copy)
                    P.dma("sp", v_o[g][tt * 128:(tt + 1) * 128, n2 * 512:(n2 + 1) * 512], vs[:], v_o[g], vs)
            ps = psA.next()
            for k in range(8):
                P.op("pe", "matmul", [wfs, hT], [ps], ps[0:16, :], lhsT=wfs[:, k, :], rhs=hT[:, k, :], start=(k == 0),
                     stop=(k == 7), inc=(k == 7))
            lf = lfr.next()
            P.op("act", "activation", [ps, nfb], [lf], out=lf[:], in_=ps[0:16, :], func=AF.Exp, scale=-1.0,
                 bias=nfb[:, 0:1])
            P.op("act", "activation", [lf], [lf], out=lf[:], in_=lf[:], func=AF.Ln, bias=1.0, scale=1.0)
            P.op("dve", "tensor_scalar", [lf], [lf], out=lf[:], in0=lf[:], scalar1=-1.0, scalar2=None, op0=ALU.mult)
            P.dma("sp", lf_o[:, gs], lf[:], lf_o, lf)
            for j in range(8):
                w = wr.next()
                P.dma("pool", w[:], W["wog"][j], w, W["wog"])
                ps = psA.next()
                for k in range(8):
                    P.op("pe", "matmul", [w, hTb], [ps], ps[:], lhsT=w[:, k, :], rhs=hTb[:, k, :], start=(k == 0),
                         stop=(k == 7), inc=(k == 7))
                qn = qnr.next()
                P.op("act", "activation", [ps], [qn], out=qn[:], in_=ps[:], func=AF.Sigmoid)
                P.dma("sp", og_o[j * 128:(j + 1) * 128, gs], qn[:], og_o, qn)
        P.flush()
        P.recycle()


def emit_blend(P, dst_ap, dst, c0_ap, c1_ap, srcbuf, t0, t1, flag, nflag, np_):
    P.dma("sp", t0[0:np_], c0_ap, t0, srcbuf)
    P.dma("act", t1[0:np_], c1_ap, t1, srcbuf)
    P.op("pool", "tensor_scalar", [t1, flag], [t1], out=t1[0:np_], in0=t1[0:np_], scalar1=flag[0:np_, 0:1], scalar2=None,
         op0=ALU.mult)
    P.op("dve", "scalar_tensor_tensor", [t0, nflag, t1], [dst], out=dst_ap, in0=t0[0:np_], scalar=nflag[0:np_, 0:1],
         in1=t1[0:np_], op0=ALU.mult, op1=ALU.add)


def emit_blend2(P, dst_ap, dst, c0_ap, b0, c1_ap, b1, t0, t1, flag, nflag, np_):
    P.dma("sp", t0[0:np_], c0_ap, t0, b0)
    P.dma("act", t1[0:np_], c1_ap, t1, b1)
    P.op("pool", "tensor_scalar", [t1, flag], [t1], out=t1[0:np_], in0=t1[0:np_], scalar1=flag[0:np_, 0:1], scalar2=0.0,
         op0=ALU.mult, op1=ALU.add)
    P.op("dve", "scalar_tensor_tensor", [t0, nflag, t1], [dst], out=dst_ap, in0=t0[0:np_], scalar=nflag[0:np_, 0:1],
         in1=t1[0:np_], op0=ALU.mult, op1=ALU.add)


def emit_fox_c(P, C, PS, W, qk_g, v_g, lf_g, o_s):
    with ExitStack() as st:
        flag = P.sb([128, 1], F32, "flag", st)
        nflag = P.sb([128, 1], F32, "nflag", st)
        P.dma("sp", flag[:], W["flag"][:], flag, W["flag"])
        P.op("dve", "tensor_scalar", [flag], [nflag], out=nflag[:], in0=flag[:], scalar1=-1.0, scalar2=1.0,
             op0=ALU.mult, op1=ALU.add)
        masks = P.sb([128, 4, 512], F32, "masks", st)
        P.dma("sp", masks[:], W["maskneg"].t.rearrange("j p t -> p j t"), masks, W["maskneg"])
        t0 = P.sb([128, T], F32, "bt0", st)
        t1 = P.sb([128, T], F32, "bt1", st)
        rows = {nm: (P.sb([8, S4], BF16, nm + "hi", st), P.sb([8, S4], BF16, nm + "lo", st)) for nm in ("cs", "ncs")}
        sp_ = ExitStack()
        ones8 = P.sb([8, S4], F32, "ones8", sp_)
        P.op("pool", "memset", [], [ones8], ones8[:], 1.0)
        lfa = P.sb([8, S4], F32, "lfa", sp_)
        lfb = P.sb([8, S4], F32, "lfb", sp_)
        for r in range(2):
            rs_ = slice(r * T, (r + 1) * T)
            emit_blend(P, lfa[:, rs_], lfa, lf_g[r * 16:r * 16 + 8, :], lf_g[r * 16 + 8:r * 16 + 16, :], lf_g, t0, t1,
                       flag, nflag, 8)
        P.op("dve", "tensor_tensor_scan", [ones8, lfa], [lfb], out=lfb[:], data0=ones8[:], data1=lfa[:], initial=0.0,
             op0=ALU.mult, op1=ALU.add)
        P.op("dve", "tensor_scalar", [lfb], [lfa], out=lfa[:], in0=lfb[:], scalar1=8.0, scalar2=0.0, op0=ALU.mult, op1=ALU.add)
        P.op("dve", "tensor_scalar", [lfb], [lfb], out=lfb[:], in0=lfb[:], scalar1=-8.0, scalar2=0.0, op0=ALU.mult, op1=ALU.add)
        cs8, ncs8 = lfa, lfb
        for nm, src in (("cs", cs8), ("ncs", ncs8)):
            hi, lo = rows[nm]
            P.op("act", "activation", [src], [hi], out=hi[:], in_=src[:], func=AF.Copy)
            P.op("dve", "tensor_tensor", [src, hi], [lo], out=lo[:], in0=src[:], in1=hi[:], op=ALU.subtract)
        P.flush()
        sp_.close()
        qar = Rot([(P.sb([128, S4], BF16, f"qah{i}", st), P.sb([128, S4], BF16, f"qal{i}", st)) for i in range(2)])
        kar = Rot([(P.sb([128, S4], BF16, f"kah{i}", st), P.sb([128, S4], BF16, f"kal{i}", st)) for i in range(2)])
        bl = P.sb([128, T], F32, "bl", st)
        vf = P.sb([128, 16, 64], F32, "vf", st)
        var_ = Rot([P.sb([128, 32, 65], BF16, f"va{i}", st) for i in range(2)])
        ptr = Rot([P.sb([128, 512], BF16, f"pt{i}", st) for i in range(4)])
        tmr = Rot([P.sb([128, 512], F32, f"tm{i}", st) for i in range(2)])
        rden = P.sb([128, 512], F32, "rden", st)
        bcs = P.sb([64, 512], F32, "bcs", st)
        osr = Rot([P.sb([64, 512], F32, f"os{i}", st) for i in range(2)])
        psS = Rot([PS["A"][0], PS["A"][1], PS["W"][0]])
        psBC = PS["W"][1]
        psO = Rot(PS["O"])
        v3 = [vg.t.rearrange("(r kt p) c -> p r kt c", r=2, p=128) for vg in v_g]

        def load_head(h, qa, ka, va):
            for r in range(2):
                rs_ = slice(r * T, (r + 1) * T)

                def cand(base_j, hg):
                    j = base_j + hg * 4 + h // 2
                    off = r * 256 + (j % 2) * 128 + (h % 2) * 64
                    return qk_g[j // 2], qk_g[j // 2][off:off + 64, :]
                for base_j, (xh, xl) in ((0, qa), (8, ka)):
                    (b0_, a0_), (b1_, a1_) = cand(base_j, 0), cand(base_j, 1)
                    emit_blend2(P, bl[0:64, :], bl, a0_, b0_, a1_, b1_, t0, t1, flag, nflag, 64)
                    P.op("act", "activation", [bl], [xh], out=xh[0:64, rs_], in_=bl[0:64, :], func=AF.Copy)
                    P.op("dve", "tensor_tensor", [bl, xh], [xl], out=xl[0:64, rs_], in0=bl[0:64, :], in1=xh[0:64, rs_],
                         op=ALU.subtract)
                    yield
                c0, c1 = h * 64, 512 + h * 64
                t1v = t1[:, 0:1024].rearrange("p (a b) -> p a b", a=16)
                for i in range(4):
                    P.dma("sp", vf[:, i * 4:(i + 1) * 4, :], v3[i][:, r, :, c0:c0 + 64], vf, v_g[i])
                    P.dma("act", t1v[:, i * 4:(i + 1) * 4, :], v3[i][:, r, :, c1:c1 + 64], t1, v_g[i])
                P.op("pool", "tensor_scalar", [t1, flag], [t1], out=t1[:, 0:1024], in0=t1[:, 0:1024], scalar1=flag[:, 0:1],
                     scalar2=0.0, op0=ALU.mult, op1=ALU.add)
                P.op("dve", "scalar_tensor_tensor", [vf, nflag, t1], [va], out=va[:, r * 16:(r + 1) * 16, 0:64],
                     in0=vf[:], scalar=nflag[:, 0:1], in1=t1v, op0=ALU.mult, op1=ALU.add)
                yield
            for xx, val in ((qa[0], 1.0), (qa[1], 0.0), (ka[0], 1.0), (ka[1], 0.0)):
                P.op("pool", "memset", [], [xx], xx[64:66, :], val)
            P.dma("sp", qa[0][64:65, :], rows["cs"][0][h:h + 1, :], qa[0], rows["cs"][0])
            P.dma("sp", qa[1][64:65, :], rows["cs"][1][h:h + 1, :], qa[1], rows["cs"][1])
            P.dma("sp", ka[0][65:66, :], rows["ncs"][0][h:h + 1, :], ka[0], rows["ncs"][0])
            P.dma("sp", ka[1][65:66, :], rows["ncs"][1][h:h + 1, :], ka[1], rows["ncs"][1])
            P.op("pool", "memset", [], [va], va[:, :, 64:65], 1.0)
            yield

        bufs = [(qar.next(), kar.next(), var_.next()) for _ in range(8)]
        for _ in load_head(0, *bufs[0]):
            pass
        for h in range(8):
            qa, ka, va = bufs[h]
            nxt = load_head(h + 1, *bufs[h + 1]) if h + 1 < 8 else iter(())
            items = [(qc, kt) for qc in range(8) for kt in range(4 * qc + 4)]
            LAG = 2
            pend = []
            o_cur = {}

            def qk(qc, kt):
                qs = slice(qc * 512, (qc + 1) * 512)
                ps = psS.next()
                terms = ((ka[0], qa[0]), (ka[0], qa[1]), (ka[1], qa[0]))
                for ti, (kk, qq) in enumerate(terms):
                    P.op("pe", "matmul", [kk, qq], [ps], ps[:, 0:512], lhsT=kk[0:66, kt * 128:(kt + 1) * 128], rhs=qq[0:66, qs],
                         start=(ti == 0), stop=(ti == 2), inc=(ti == 2))
                pt = ptr.next()
                if kt >= 4 * qc:
                    tm = tmr.next()
                    P.op("dve", "tensor_tensor", [ps, masks], [tm], out=tm[:], in0=ps[:, 0:512], in1=masks[:, kt - 4 * qc, :],
                         op=ALU.add)
                    P.op("act", "activation", [tm], [pt], out=pt[:], in_=tm[:], func=AF.Exp, scale=0.125)
                else:
                    P.op("act", "activation", [ps], [pt], out=pt[:], in_=ps[:, 0:512], func=AF.Exp, scale=0.125)
                return pt

            def pv(qc, kt, pt):
                nk = 4 * qc + 4
                if kt == 0:
                    o_cur[qc] = psO.next()
                o_ps = o_cur[qc]
                P.op("pe", "matmul", [va, pt], [o_ps], o_ps[0:65, :], lhsT=va[:, kt, :], rhs=pt[:], start=(kt == 0),
                     stop=(kt == nk - 1), inc=True)
                if kt == nk - 1:
                    qs = slice(qc * 512, (qc + 1) * 512)
                    P.op("dve", "reciprocal", [o_ps], [rden], out=rden[64:65, :], in_=o_ps[64:65, :])
                    P.op("pe", "matmul", [C["ones"], rden], [psBC], psBC[0:64, 0:512], lhsT=C["ones"][64:65, 0:64],
                         rhs=rden[64:65, :], start=True, stop=True)
                    P.op("act", "activation", [psBC], [bcs], out=bcs[:], in_=psBC[0:64, 0:512], func=AF.Copy)
                    os_ = osr.next()
                    P.op("dve", "tensor_tensor", [o_ps, bcs], [os_], out=os_[:], in0=o_ps[0:64, :], in1=bcs[:], op=ALU.mult)
                    P.dma("sp", o_s[h // 2][(h % 2) * 64:(h % 2 + 1) * 64, qs], os_[:], o_s[h // 2], os_)
                    next(nxt, None)

            for (qc, kt) in items:
                pend.append((qc, kt, qk(qc, kt)))
                if len(pend) > LAG:
                    pv(*pend.pop(0))
            while pend:
                pv(*pend.pop(0))
            for _ in nxt:
                pass
        P.flush()
        P.recycle()


def emit_fox_d(P, C, PS, W, xin, o_g, og_i, xmid):
    with ExitStack() as s1:
        A, mod = emit_mod(P, PS["A"][0], W["c"], W["mix_g"], W["mix_mod_w"], W["mix_mod_b"], "dm", s1)
        flag = P.sb([128, 1], F32, "flag", s1)
        nflag = P.sb([128, 1], F32, "nflag", s1)
        P.dma("sp", flag[:], W["flag"][:], flag, W["flag"])
        P.op("dve", "tensor_scalar", [flag], [nflag], out=nflag[:], in0=flag[:], scalar1=-1.0, scalar2=1.0,
             op0=ALU.mult, op1=ALU.add)
        xg = P.sb([128, 8, G], F32, "xg", s1)
        og = P.sb([128, 8, G], F32, "og", s1)
        ogb = P.sb([128, 8, G], BF16, "ogb", s1)
        ot0 = P.sb([128, 8, G], F32, "ot0", s1)
        ot1 = P.sb([128, 8, G], F32, "ot1", s1)
        owr = Rot([P.sb([128, 8, 128], BF16, f"ow{i}", s1) for i in range(3)])
        psO = Rot(PS["O"])
        for g in range(NG):
            P.dma("sp", xg[:], xview(xin, g), xg, xin)
            P.dma("sp", og[:], xview(og_i, g), og, og_i)
            for k in range(8):
                og_k = o_g[k % 4][(k // 4) * 128:(k // 4 + 1) * 128, :]
                P.dma("sp", ot0[:, k, :], og_k[:, g * G:(g + 1) * G], ot0, o_g[k % 4])
                P.dma("act", ot1[:, k, :], og_k[:, T + g * G:T + (g + 1) * G], ot1, o_g[k % 4])
            P.op("pool", "tensor_scalar", [ot1, flag], [ot1], out=ot1[:], in0=ot1[:], scalar1=flag[:, 0:1], scalar2=0.0,
                 op0=ALU.mult, op1=ALU.add)
            P.op("dve", "scalar_tensor_tensor", [ot0, nflag, ot1], [ot0], out=ot0[:], in0=ot0[:], scalar=nflag[:, 0:1],
                 in1=ot1[:], op0=ALU.mult, op1=ALU.add)
            P.op("dve", "tensor_tensor", [og, ot0], [ogb], out=ogb[:], in0=og[:], in1=ot0[:], op=ALU.mult)
            for dc in range(8):
                ow = owr.next()
                P.dma("pool", ow[:], W["fox_out_w"][dc], ow, W["fox_out_w"])
                o_ps = psO.next()
                for k in range(8):
                    P.op("pe", "matmul", [ow, ogb], [o_ps], o_ps[:], lhsT=ow[:, k, :], rhs=ogb[:, k, :], start=(k == 0),
                         stop=(k == 7), inc=(k == 7))
                P.op("dve", "scalar_tensor_tensor", [o_ps, mod, xg], [xg], out=xg[:, dc, :], in0=o_ps[:],
                     scalar=mod[:, 16 + dc:17 + dc], in1=xg[:, dc, :], op0=ALU.mult, op1=ALU.add)
            P.dma("sp", xview(xmid, g), xg[:], xmid, xg)
        P.flush()
        P.recycle()


def build_fused(ncores=NCORES, do_l0=True, do_peer1=True):
    nc = bass.Bass("TRN2", target_bir_lowering=False)
    PAIRS = [[2 * i, 2 * i + 1] for i in range(ncores // 2)]
    shared = {}
    W0 = WMap(nc, "l0_", shared)
    W1 = WMap(nc, "l1_", shared)
    xin = din(nc, "xT", [D, T])
    xprev = din(nc, "xTp", [D, T])
    xout = dout(nc, "xo", [D, T])
    with ExitStack() as st:
        P = Prog(nc, st)
        PS = alloc_psum(P)
        C = emit_consts(P)
        xmid0 = P.dram("xmid0", [D, T])
        x1 = P.dram("x1", [D, T])
        qk_s = [P.dram(f"qk_s{i}", [256, T]) for i in range(8)]
        qk_g = [P.dram(f"qk_g{i}", [512, T]) for i in range(8)]
        v_s = [P.dram(f"v_s{i}", [512, D]) for i in range(4)]
        v_g = [P.dram(f"v_g{i}", [1024, D]) for i in range(4)]
        lf_s = P.dram("lf_s", [16, T])
        lf_g = P.dram("lf_g", [2 * 16, T])
        ogs = P.dram("ogs", [D, T])
        o_s = [P.dram(f"o_s{i}", [128, S4]) for i in range(4)]
        o_g = [P.dram(f"o_g{i}", [256, S4]) for i in range(4)]
        xmid1 = P.dram("xmid1", [D, T])
        if do_l0:
            emit_lru(P, C, PS, W0, xin, xprev, xmid0)
            emit_peer(P, C, PS, W0, xmid0, x1, "l0")
        else:
            x1 = xin
        emit_fox_b(P, C, PS, W1, x1, qk_s, v_s, lf_s, ogs)
        P.coll("AllGather", PAIRS, lf_s, lf_g)
        for a, b_ in zip(qk_s + v_s, qk_g + v_g):
            P.coll("AllGather", PAIRS, a, b_)
        emit_fox_c(P, C, PS, W1, qk_g, v_g, lf_g, o_s)
        for a, b_ in zip(o_s, o_g):
            P.coll("AllGather", PAIRS, a, b_)
        if do_peer1:
            emit_fox_d(P, C, PS, W1, x1, o_g, ogs, xmid1)
            emit_peer(P, C, PS, W1, xmid1, xout, "l1")
        else:
            emit_fox_d(P, C, PS, W1, x1, o_g, ogs, xout)
        P.flush([xout])
    nc.used_inputs = ["xT", "xTp"] + W0.names() + [n for n in W1.names() if n not in W0.names()]
    return nc


def lay_fox(inp):
    d = {}
    d["mix_g"], d["mix_mod_w"], d["mix_mod_b"] = lay_mod(inp["l1_mix_norm_g"], inp["l1_mix_mod_w"], inp["l1_mix_mod_b"])
    iw = inp["l1_fox_in_w"]

    def blocks(cols, nblk, w):
        return np.ascontiguousarray(cols.reshape(8, 128, nblk, w).transpose(2, 1, 0, 3))
    d["wqk"] = blocks(iw[:, 0:2048], 16, 128)
    d["wv"] = blocks(iw[:, 2048:3072], 2, 512)
    d["wf"] = np.ascontiguousarray(iw[:, 3072:3088].reshape(8, 128, 16).transpose(1, 0, 2))
    d["wog"] = blocks(iw[:, 3088:4112], 8, 128)
    d["fb"] = np.ascontiguousarray(inp["l1_fox_f_b"].reshape(16, 1))
    d["gq"] = np.ascontiguousarray(np.tile(inp["l1_fox_q_norm_g"], 2).reshape(128, 1))
    d["gk"] = np.ascontiguousarray(np.tile(inp["l1_fox_k_norm_g"], 2).reshape(128, 1))
    d["fox_out_w"] = blocks(inp["l1_fox_out_w"], 8, 128)
    return d


def lay_l1peer(inp):
    d = {}
    d["ffn_g"], d["ffn_mod_w"], d["ffn_mod_b"] = lay_mod(inp["l1_ffn_norm_g"], inp["l1_ffn_mod_w"], inp["l1_ffn_mod_b"])
    d["qwT"], d["skT"], d["UT"], d["V"] = lay_peer(inp["l1_peer_q_w"], inp["l1_peer_subkey1"], inp["l1_peer_subkey2"],
                                                  inp["l1_peer_u"], inp["l1_peer_v"])
    return d


def maskneg():
    m = np.zeros((4, 128, 512), np.float32)
    tk = np.arange(128)[:, None]
    tq = np.arange(512)[None, :]
    for j in range(4):
        m[j] = np.where(tk + j * 128 > tq, -240000.0, 0.0)
    return m


def block_diag_ones():
    bd = np.zeros((128, 128), np.float32)
    bd[0:64, 0:64] = 1.0
    bd[64:128, 64:128] = 1.0
    return bd


_CACHE = {}


def kernel(**inp):
    inp = {k: np.asarray(v, dtype=np.float32) for k, v in inp.items()}
    if "nc" not in _CACHE:
        _CACHE["nc"] = build_fused()
    nc = _CACHE["nc"]
    sh = {}
    l0 = lay_l0(inp)
    l0.pop("ident")
    for k, v in l0.items():
        sh["l0_" + k] = v
    for k, v in {**lay_fox(inp), **lay_l1peer(inp)}.items():
        sh["l1_" + k] = v
    sh["ident"] = np.eye(128, dtype=np.float32)
    sh["bd"] = block_diag_ones()
    sh["maskneg"] = maskneg()
    maps = []
    x = inp["x"]
    for c in range(NCORES):
        b, hf = c // 2, c % 2
        m = dict(sh)
        m["xT"] = np.ascontiguousarray(x[b, hf * T:(hf + 1) * T, :].T)
        m["xTp"] = np.ascontiguousarray(x[b, 0:T, :].T) if hf else np.zeros((D, T), np.float32)
        m["flag"] = np.full((128, 1), float(hf), np.float32)
        m["c"] = np.ascontiguousarray(inp["c"][b].reshape(128, 8))
        maps.append({k: v for k, v in m.items() if k in nc.used_inputs})
    res = run_bass_kernel_spmd(nc, maps, core_ids=list(range(NCORES)))
    out = np.empty((4, S4, D), np.float32)
    for c in range(NCORES):
        b, hf = c // 2, c % 2
        out[b, hf * T:(hf + 1) * T, :] = res.results[c]["xo"].T
    return out
```

```python
import numpy as np
from contextlib import ExitStack
import concourse.bass as bass
import concourse.mybir as mybir
from concourse.bass_utils import run_bass_kernel_spmd

F32 = mybir.dt.float32
BF16 = mybir.dt.bfloat16
AF = mybir.ActivationFunctionType
ALU = mybir.AluOpType
AX = mybir.AxisListType

NCORES = 8
T = 2048
G = 512
NG = T // G
D = 1024
DR = 1408
NB = 16
BS = 88
NE = 16384


class Buf:
    __slots__ = ("t", "name", "last_w", "readers", "dsem", "dcnt")

    def __init__(self, t, name):
        self.t = t
        self.name = name
        self.last_w = None
        self.readers = {}
        self.dsem = None
        self.dcnt = 0

    def __getitem__(self, k):
        return self.t[k]


class Prog:
    def __init__(self, nc, stack):
        self.nc = nc
        self.stack = stack
        self.eng = {"pe": nc.tensor, "act": nc.scalar, "dve": nc.vector,
                    "pool": nc.gpsimd, "sp": nc.sync}
        self.sem = {}
        self.cnt = {}
        for e in self.eng:
            self.sem[e] = stack.enter_context(nc.semaphore("s_" + e))
            self.cnt[e] = 0
        self.waited = {}
        self.nbuf = 0
        self.q = {e: [] for e in self.eng}
        self.nblk = 0
        self.live = []
        self.sem_pool = []
        self.ccsem = stack.enter_context(nc.semaphore("s_cc"))
        self.cccnt = 0

    def sb(self, shape, dt=F32, name=None, stack=None):
        self.nbuf += 1
        name = (name or "b") + f"_{self.nbuf}"
        t = (stack or self.stack).enter_context(self.nc.sbuf_tensor(name, list(shape), dt))
        b = Buf(t, name)
        self.live.append(b)
        return b

    def recycle(self):
        for b in self.live:
            if b.dsem is not None:
                self.sem_pool.append((b.dsem, b.dcnt))
                b.dsem = None
        self.live = []

    def coll(self, kind, groups, src, dst):
        self._waits("pool", [src], [dst])
        self.q["pool"].append(("op", "collective_compute", (kind, ALU.bypass),
                               dict(replica_groups=groups, ins=[src.t.opt()], outs=[dst.t.opt()]),
                               (self.ccsem, None)))
        self.cccnt += 1
        tok = (self.ccsem, self.cccnt)
        src.readers[tok[0]] = max(src.readers.get(tok[0], 0), tok[1])
        dst.last_w = tok
        dst.readers = {}

    def ps(self, shape, dt=F32, name=None, stack=None):
        self.nbuf += 1
        name = (name or "p") + f"_{self.nbuf}"
        t = (stack or self.stack).enter_context(self.nc.psum_tensor(name, list(shape), dt))
        return Buf(t, name)

    def dram(self, name, shape, dt=F32, kind="Internal"):
        t = self.nc.dram_tensor(name, list(shape), dt, kind=kind)
        return Buf(t.ap(), name)

    def _waits(self, e, reads, writes):
        need = {}
        for b in reads:
            if b.last_w is not None:
                s, v = b.last_w
                need[s] = max(need.get(s, 0), v)
        for b in writes:
            if b.last_w is not None:
                s, v = b.last_w
                need[s] = max(need.get(s, 0), v)
            for s, v in b.readers.items():
                if s is self.sem.get(e):
                    continue
                need[s] = max(need.get(s, 0), v)
        for s, v in need.items():
            key = (e, id(s))
            if s is self.sem.get(e) and (e == "pe" or v > self.cnt[e]):
                continue
            if self.waited.get(key, 0) < v:
                self.q[e].append(("wait", s, v))
                self.waited[key] = v

    def op(self, e, meth, reads, writes, *args, inc=True, **kw):
        self._waits(e, reads, writes)
        self.q[e].append(("op", meth, args, kw, (self.sem[e], 1) if inc else None))
        if inc:
            self.cnt[e] += 1
            tok = (self.sem[e], self.cnt[e])
        else:
            tok = (self.sem[e], self.cnt[e] + 1)
        for b in reads:
            b.readers[tok[0]] = max(b.readers.get(tok[0], 0), tok[1])
        for b in writes:
            b.last_w = tok
            b.readers = {}

    def dma(self, q, out_ap, in_ap, dst, src, **kw):
        self._waits(q, [src], [dst])
        if dst.dsem is None:
            if self.sem_pool:
                dst.dsem, dst.dcnt = self.sem_pool.pop()
            else:
                dst.dsem = self.stack.enter_context(self.nc.semaphore("d_" + dst.name))
        kw = dict(kw, out=out_ap, in_=in_ap)
        self.q[q].append(("op", "dma_start", (), kw, (dst.dsem, 16)))
        dst.dcnt += 16
        tok = (dst.dsem, dst.dcnt)
        src.readers[tok[0]] = max(src.readers.get(tok[0], 0), tok[1])
        dst.last_w = tok
        dst.readers = {}

    def flush(self, final_bufs=()):
        for b in final_bufs:
            if b.last_w is not None:
                s, v = b.last_w
                self.q["sp"].append(("wait", s, v))
        nc = self.nc
        if not any(self.q.values()):
            return
        self.nblk += 1
        with nc.Block() as block:
            for e, starter in (("sp", block.sync), ("pe", block.tensor), ("act", block.scalar),
                               ("dve", block.vector), ("pool", block.gpsimd)):
                items = self.q[e]
                if not items:
                    continue

                def body(eng, items=items):
                    for it in items:
                        if it[0] == "wait":
                            eng.wait_ge(it[1], it[2])
                        else:
                            _, meth, args, kw, inc = it
                            ins = getattr(eng, meth)(*args, **kw)
                            if inc is not None:
                                if inc[1] is None:
                                    ins.then_inc(inc[0])
                                else:
                                    ins.then_inc(inc[0], inc[1])
                starter(body)
        self.q = {e: [] for e in self.eng}


class Rot:
    def __init__(self, bufs):
        self.bufs = bufs
        self.i = 0

    def next(self):
        b = self.bufs[self.i % len(self.bufs)]
        self.i += 1
        return b


def din(nc, name, shape, dt=F32):
    return Buf(nc.dram_tensor(name, list(shape), dt, kind="ExternalInput").ap(), name)


def dout(nc, name, shape, dt=F32):
    return Buf(nc.dram_tensor(name, list(shape), dt, kind="ExternalOutput").ap(), name)


def emit_consts(P):
    C = {}
    C["ones"] = P.sb([128, 128], F32, "ones")
    P.op("pool", "memset", [], [C["ones"]], C["ones"][:], 1.0)
    C["zb"] = P.sb([128, 512], BF16, "zb")
    P.op("pool", "memset", [], [C["zb"]], C["zb"][:], 0.0)
    return C


def emit_mod(P, ps_small, c_d, g_d, w_d, b_d, pfx, pst=None):
    A = P.sb([128, 8], F32, pfx + "A", pst)
    mod = P.sb([128, 24], F32, pfx + "mod", pst)
    with ExitStack() as st:
        cs = P.sb([128, 8], F32, "cs", st)
        gs = P.sb([128, 8], F32, "gs", st)
        bs = P.sb([128, 24], F32, "bs", st)
        sc = P.sb([128, 8], F32, "sc", st)
        P.dma("sp", cs[:], c_d[:], cs, c_d)
        P.dma("sp", gs[:], g_d[:], gs, g_d)
        P.dma("sp", bs[:], b_d[:], bs, b_d)
        P.op("act", "activation", [cs], [sc], out=sc[:], in_=cs[:], func=AF.Silu)
        wrot = Rot([P.sb([128, 8, 512], F32, f"wb{i}", st) for i in range(2)])
        modp = ps_small
        for s in range(6):
            wb = wrot.next()
            P.dma("sp", wb[:], w_d[:, :, s * 512:(s + 1) * 512], wb, w_d)
            for j in range(4):
                col = s * 4 + j
                for k in range(8):
                    P.op("pe", "matmul", [wb, sc], [modp], modp[:, col:col + 1],
                         lhsT=wb[:, k, j * 128:(j + 1) * 128], rhs=sc[:, k:k + 1],
                         start=(k == 0), stop=(k == 7), inc=(k == 7))
        P.op("dve", "tensor_tensor", [modp, bs], [mod], out=mod[:], in0=modp[:, 0:24], in1=bs[:], op=ALU.add)
        P.op("dve", "scalar_tensor_tensor", [mod, gs], [A], out=A[:], in0=mod[:, 8:16], scalar=1.0,
             in1=gs[:], op0=ALU.add, op1=ALU.mult)
        P.flush()
    return A, mod


def emit_norm(P, C, xg, A, mod, hT, hT_bf, sqrot, ssp, rstd):
    for k in range(8):
        s_ = sqrot.next()
        P.op("act", "activation", [xg], [s_], out=s_[:], in_=xg[:, k, :], func=AF.Square)
        P.op("pe", "matmul", [C["ones"], s_], [ssp], ssp[:], lhsT=C["ones"][:], rhs=s_[:],
             start=(k == 0), stop=(k == 7))
    P.op("dve", "tensor_scalar", [ssp], [rstd], out=rstd[:], in0=ssp[:], scalar1=1.0 / D, scalar2=1e-6,
         op0=ALU.mult, op1=ALU.add)
    P.op("act", "activation", [rstd], [rstd], out=rstd[:], in_=rstd[:], func=AF.Sqrt)
    P.op("dve", "reciprocal", [rstd], [rstd], out=rstd[:], in_=rstd[:])
    for k in range(8):
        P.op("dve", "tensor_tensor", [xg, rstd], [hT], out=hT[:, k, :], in0=xg[:, k, :], in1=rstd[:], op=ALU.mult)
        P.op("act", "activation", [hT, A, mod], [hT], out=hT[:, k, :], in_=hT[:, k, :], func=AF.Identity,
             scale=A[:, k:k + 1], bias=mod[:, k:k + 1])
        if hT_bf is not None:
            P.op("pool", "tensor_copy", [hT], [hT_bf], out=hT_bf[:, k, :], in_=hT[:, k, :])


def xview(x_d, g):
    return x_d.t.rearrange("(k p) t -> p k t", p=128)[:, :, g * G:(g + 1) * G]


def emit_lru(P, C, PS, W, xin_d, xprev_d, xout_d):
    with ExitStack() as st:
        A, mod = emit_mod(P, PS["A"][0], W["c"], W["mix_g"], W["mix_mod_w"], W["mix_mod_b"], "lm", st)
        cw = P.sb([128, NB, 4], F32, "cw", st)
        tabs = {}
        for nm in ("conv_b", "ra_b", "ri_b", "lam"):
            tabs[nm] = P.sb([128, NB], F32, nm, st)
            P.dma("sp", tabs[nm][0:BS, :], W[nm][:], tabs[nm], W[nm])
        P.dma("sp", cw[0:BS, :, :], W["conv_w"][:], cw, W["conv_w"])
        flag = P.sb([128, 1], F32, "flag", st)
        P.dma("sp", flag[:], W["flag"][:], flag, W["flag"])
        raw = P.sb([128, NB, BS], F32, "raw", st)
        riw = P.sb([128, NB, BS], F32, "riw", st)
        P.dma("sp", raw[0:BS, :, :], W["ra_w"].t.rearrange("n c d -> c n d"), raw, W["ra_w"])
        P.dma("sp", riw[0:BS, :, :], W["ri_w"].t.rearrange("n c d -> c n d"), riw, W["ri_w"])
        nls = P.sb([128, NB], F32, "nls", st)
        nls2 = P.sb([128, NB], F32, "nls2", st)
        lam = tabs["lam"]
        P.op("act", "activation", [lam], [nls], out=nls[0:BS, :], in_=lam[0:BS, :], func=AF.Exp, scale=-1.0)
        P.op("act", "activation", [nls], [nls], out=nls[0:BS, :], in_=nls[0:BS, :], func=AF.Ln, bias=1.0, scale=1.0)
        P.op("dve", "tensor_scalar", [nls], [nls2], out=nls2[0:BS, :], in0=nls[0:BS, :], scalar1=-16.0, scalar2=None,
             op0=ALU.mult)
        P.op("dve", "tensor_scalar", [nls], [nls], out=nls[0:BS, :], in0=nls[0:BS, :], scalar1=-8.0, scalar2=None,
             op0=ALU.mult)
        hist = P.sb([128, NB, 3], F32, "hist", st)
        carry = P.sb([128, NB], F32, "carry", st)
        P.op("pool", "memset", [], [hist], hist[:], 0.0)
        P.op("pool", "memset", [], [carry], carry[:], 0.0)

        xg = P.sb([128, 8, G], F32, "xg", st)
        hT = P.sb([128, 8, G], F32, "hT", st)
        hTb = P.sb([128, 8, G], BF16, "hTb", st)
        sqrot = Rot([P.sb([128, G], F32, f"sq{i}", st) for i in range(2)])
        rstd = P.sb([128, G], F32, "rstd", st)
        wxr = Rot([P.sb([128, 8, BS], BF16, f"wx{i}", st) for i in range(5)])
        wgr = Rot([P.sb([128, 8, BS], BF16, f"wg{i}", st) for i in range(5)])
        xbr = Rot([P.sb([128, G + 3], F32, f"xbuf{i}", st) for i in range(4)])

        def tmp(nm, n=2):
            return Rot([P.sb([128, G], F32, f"{nm}{i}", st) for i in range(n)])
        cvr, rr, ir, ar, a2r, ur, hsr, glr = [tmp(n, k) for n, k in (("cv", 7), ("r", 4), ("i", 4), ("a", 5), ("a2", 5),
                                                                      ("u", 4), ("hs", 4), ("gl", 3))]
        yT = P.sb([128, NB, G], BF16, "yT", st)
        owr = Rot([P.sb([128, NB, 128], BF16, f"ow{i}", st) for i in range(3)])
        psA, psO, psW = Rot(PS["A"]), Rot(PS["O"]), Rot(PS["W"])

        for step in range(2 * NG):
            main = step >= NG
            g = step % NG
            src = xin_d if main else xprev_d
            P.dma("sp", xg[:], xview(src, g), xg, src)
            emit_norm(P, C, xg, A, mod, hT, hTb, sqrot, PS["S"], rstd)
            sched = []

            def at(t, fn):
                sched.append((t, len(sched), fn))

            for n in range(NB):
                wx, xb_ps, xb, cv, pw = wxr.next(), psA.next(), xbr.next(), cvr.next(), psW.next()
                r, i_, a, a2, u, hs = rr.next(), ir.next(), ar.next(), a2r.next(), ur.next(), hsr.next()
                r_ps, i_ps = pw[0:BS, 0:G], pw[0:BS, G:2 * G]
                at(n - 3, lambda wx=wx, n=n: P.dma("pool", wx[:], W["in_w"][n], wx, W["in_w"]))

                def mm_x(wx=wx, xb_ps=xb_ps):
                    for k in range(8):
                        P.op("pe", "matmul", [wx, hTb], [xb_ps], xb_ps[0:BS, :], lhsT=wx[:, k, :], rhs=hTb[:, k, :],
                             start=(k == 0), stop=(k == 7), inc=(k == 7))
                at(n, mm_x)
                at(n + 1, lambda xb_ps=xb_ps, xb=xb: P.op("act", "activation", [xb_ps], [xb], out=xb[0:BS, 3:G + 3],
                                                          in_=xb_ps[0:BS, :], func=AF.Copy))
                at(n + 1, lambda xb=xb, n=n: P.op("dve", "tensor_copy", [hist], [xb], out=xb[0:BS, 0:3], in_=hist[0:BS, n, :]))
                at(n + 2, lambda xb=xb, n=n: P.op("dve", "tensor_copy", [xb], [hist], out=hist[0:BS, n, :], in_=xb[0:BS, G:G + 3]))
                at(n + 2, lambda xb=xb, cv=cv, n=n: P.op(
                    "act", "activation", [xb, cw, tabs["conv_b"]], [cv], out=cv[0:BS, :], in_=xb[0:BS, 3:G + 3],
                    func=AF.Identity, scale=cw[0:BS, n, 3:4], bias=tabs["conv_b"][0:BS, n:n + 1]))

                def conv3(xb=xb, cv=cv, n=n):
                    for k in range(3):
                        P.op("dve", "scalar_tensor_tensor", [xb, cw, cv], [cv], out=cv[0:BS, :], in0=xb[0:BS, k:k + G],
                             scalar=cw[0:BS, n, k:k + 1], in1=cv[0:BS, :], op0=ALU.mult, op1=ALU.add)
                at(n + 3, conv3)

                def gates(cv=cv, pw=pw, r_ps=r_ps, i_ps=i_ps, n=n):
                    P.op("pe", "matmul", [raw, cv], [pw], r_ps, lhsT=raw[0:BS, n, :], rhs=cv[0:BS, :], start=True, stop=True,
                         inc=False)
                    P.op("pe", "matmul", [riw, cv], [pw], i_ps, lhsT=riw[0:BS, n, :], rhs=cv[0:BS, :], start=True, stop=True)
                at(n + 4, gates)

                def sig(pw=pw, r_ps=r_ps, i_ps=i_ps, r=r, i_=i_, n=n):
                    P.op("act", "activation", [pw, tabs["ra_b"]], [r], out=r[0:BS, :], in_=r_ps, func=AF.Sigmoid,
                         bias=tabs["ra_b"][0:BS, n:n + 1], scale=1.0)
                    P.op("act", "activation", [pw, tabs["ri_b"]], [i_], out=i_[0:BS, :], in_=i_ps, func=AF.Sigmoid,
                         bias=tabs["ri_b"][0:BS, n:n + 1], scale=1.0)
                at(n + 5, sig)

                def exps(r=r, a=a, a2=a2, n=n):
                    P.op("act", "activation", [r, nls], [a], out=a[0:BS, :], in_=r[0:BS, :], func=AF.Exp,
                         scale=nls[0:BS, n:n + 1])
                    P.op("act", "activation", [r, nls2], [a2], out=a2[0:BS, :], in_=r[0:BS, :], func=AF.Exp,
                         scale=nls2[0:BS, n:n + 1])
                at(n + 6, exps)

                def pre_u(a2=a2, i_=i_, cv=cv, u=u):
                    P.op("dve", "tensor_scalar", [a2], [a2], out=a2[0:BS, :], in0=a2[0:BS, :], scalar1=1.0, scalar2=-1.0,
                         op0=ALU.min, op1=ALU.mult)
                    P.op("dve", "tensor_tensor", [i_, cv], [u], out=u[0:BS, :], in0=i_[0:BS, :], in1=cv[0:BS, :], op=ALU.mult)
                at(n + 7, pre_u)
                at(n + 8, lambda a2=a2: P.op("act", "activation", [a2], [a2], out=a2[0:BS, :], in_=a2[0:BS, :], func=AF.Sqrt,
                                             bias=1.0, scale=1.0))

                def scan(u=u, a2=a2, a=a, hs=hs, n=n):
                    P.op("dve", "tensor_tensor", [u, a2], [u], out=u[0:BS, :], in0=u[0:BS, :], in1=a2[0:BS, :], op=ALU.mult)
                    P.op("dve", "tensor_tensor_scan", [a, u, carry], [hs], out=hs[0:BS, :], data0=a[0:BS, :],
                         data1=u[0:BS, :], initial=carry[0:BS, n:n + 1], op0=ALU.mult, op1=ALU.add)
                    P.op("dve", "tensor_copy", [hs], [carry], out=carry[0:BS, n:n + 1], in_=hs[0:BS, G - 1:G])
                at(n + 9, scan)
                if main:
                    wg, gb_ps, gl = wgr.next(), psO.next(), glr.next()
                    at(n + 4, lambda wg=wg, n=n: P.dma("pool", wg[:], W["in_w"][NB + n], wg, W["in_w"]))

                    def mm_g(wg=wg, gb_ps=gb_ps):
                        for k in range(8):
                            P.op("pe", "matmul", [wg, hTb], [gb_ps], gb_ps[0:BS, :], lhsT=wg[:, k, :], rhs=hTb[:, k, :],
                                 start=(k == 0), stop=(k == 7), inc=(k == 7))
                    at(n + 8, mm_g)
                    at(n + 9, lambda gb_ps=gb_ps, gl=gl: P.op("act", "activation", [gb_ps], [gl], out=gl[0:BS, :],
                                                              in_=gb_ps[0:BS, :], func=AF.Gelu))
                    at(n + 11, lambda hs=hs, gl=gl, n=n: P.op("pool", "tensor_tensor", [hs, gl], [yT], out=yT[0:BS, n, :],
                                                               in0=hs[0:BS, :], in1=gl[0:BS, :], op=ALU.mult))
            sched.sort(key=lambda x: (x[0], x[1]))
            for _, _, fn in sched:
                fn()
            if not main and g == NG - 1:
                P.op("dve", "tensor_scalar", [hist, flag], [hist], out=hist[0:BS, :, :], in0=hist[0:BS, :, :],
                     scalar1=flag[0:BS, 0:1], scalar2=None, op0=ALU.mult)
                P.op("dve", "tensor_scalar", [carry, flag], [carry], out=carry[0:BS, :], in0=carry[0:BS, :],
                     scalar1=flag[0:BS, 0:1], scalar2=None, op0=ALU.mult)
            if main:
                for dc in range(8):
                    ow = owr.next()
                    P.dma("pool", ow[0:BS, :, :], W["out_w"][dc], ow, W["out_w"])
                    o_ps = psO.next()
                    for n in range(NB):
                        P.op("pe", "matmul", [ow, yT], [o_ps], o_ps[:], lhsT=ow[0:BS, n, :], rhs=yT[0:BS, n, :],
                             start=(n == 0), stop=(n == NB - 1), inc=(n == NB - 1))
                    P.op("dve", "scalar_tensor_tensor", [o_ps, mod, xg], [xg], out=xg[:, dc, :], in0=o_ps[:],
                         scalar=mod[:, 16 + dc:17 + dc], in1=xg[:, dc, :], op0=ALU.mult, op1=ALU.add)
                P.dma("sp", xview(xout_d, g), xg[:], xout_d, xg)
        P.flush()
        P.recycle()


def emit_peer(P, C, PS, W, xin_d, xout_d, pfx):
    nc = P.nc
    with ExitStack() as st:
        A, mod = emit_mod(P, PS["A"][0], W["c"], W["ffn_g"], W["ffn_mod_w"], W["ffn_mod_b"], pfx + "pm", st)
        wf_d = P.dram(pfx + "wf", [128, 8, 2048], F32)
        with ExitStack() as s2:
            qbr = Rot([P.sb([128, D], F32, f"qb{i}", s2) for i in range(2)])
            skr = Rot([P.sb([128, 128], F32, f"sk{i}", s2) for i in range(2)])
            wfr = Rot([P.sb([128, 8, 128], F32, f"wfs{i}", s2) for i in range(2)])
            psW = Rot(PS["W"])
            for blk in range(16):
                qb, sk, wfs, pw = qbr.next(), skr.next(), wfr.next(), psW.next()
                P.dma("sp", qb[:], W["qwT"][blk], qb, W["qwT"])
                P.dma("sp", sk[:], W["skT"][blk], sk, W["skT"])
                for dc in range(8):
                    P.op("pe", "matmul", [qb, sk], [pw], pw[:, dc * 128:(dc + 1) * 128],
                         lhsT=qb[:, dc * 128:(dc + 1) * 128], rhs=sk[:], start=True, stop=True, inc=(dc == 7))
                P.op("act", "activation", [pw], [wfs], out=wfs[:].rearrange("p a b -> p (a b)"), in_=pw[:], func=AF.Copy)
                P.dma("sp", wf_d[:, :, blk * 128:(blk + 1) * 128], wfs[:], wf_d, wfs)
            P.flush()

        identf = P.sb([128, 128], F32, "identf", st)
        ident = P.sb([128, 128], BF16, "ident", st)
        P.dma("sp", identf[:], W["ident"][:], identf, W["ident"])
        P.op("dve", "tensor_copy", [identf], [ident], out=ident[:], in_=identf[:])

        xg = P.sb([128, 8, G], F32, "xg", st)
        hT = P.sb([128, 8, G], F32, "hT", st)
        hTb = P.sb([128, 8, G], BF16, "hTb", st)
        sqrot = Rot([P.sb([128, G], F32, f"sq{i}", st) for i in range(2)])
        rstd = P.sb([128, G], F32, "rstd", st)
        s_sb = [P.sb([128, 16, 128], F32, f"s_sb{i}", st) for i in range(4)]
        wfpr = Rot([P.sb([128, 8, 128], F32, f"wfp{i}", st) for i in range(2)])
        vtop = P.sb([128, 16, 16], F32, "vtop", st)
        tmp128 = P.sb([128, 128], F32, "tmp128", st)
        candr = Rot([P.sb([128, 256], F32, f"cand{i}", st) for i in range(2)])
        ctmp = P.sb([128, 256], F32, "ctmp", st)
        ctmp2 = P.sb([128, 256], F32, "ctmp2", st)
        ttop = P.sb([128, 8, 24], F32, "ttop", st)
        etop = P.sb([128, 16, 16], F32, "etop", st)
        mneg = P.sb([128, 8], F32, "mneg", st)
        Zs = P.sb([128, 8], F32, "Zs", st)
        tmid = P.sb([128, 8], F32, "tmid", st)
        thr = [P.sb([128, 8], F32, f"thr{i}", st) for i in range(4)]
        Er = Rot([P.sb([128, 8, 128], F32, f"E{i}", st) for i in range(3)])
        Mr = Rot([P.sb([128, 8, 128], BF16, f"M{i}", st) for i in range(3)])
        Kr = Rot([P.sb([128, 8, 128], BF16, f"K{i}", st) for i in range(3)])
        dg = P.sb([128, 4, 8, 128], BF16, "dg", st)
        Wtr = Rot([P.sb([128, 8, 128], BF16, f"Wt{i}", st) for i in range(2)])
        wTs = [P.sb([128, 8, G], BF16, f"wT{i}", st) for i in range(2)]
        Ur = Rot([P.sb([128, 8, 128], BF16, f"U{i}", st) for i in range(3)])
        NV = 16
        Vr = [P.sb([128, D], BF16, f"V{i}", st) for i in range(NV)]
        WAr = [P.sb([128, G], BF16, f"WA{i}", st) for i in range(NV)]
        Gr = Rot([P.sb([128, G], BF16, f"G{i}", st) for i in range(2)])
        otr = Rot([P.sb([128, G], F32, f"ot{i}", st) for i in range(2)])
        psA, psO = Rot(PS["A"]), Rot(PS["O"])
        wacc, wtp = PS["W"][0], PS["W"][1]
        ev = 0
        ACT_HEADS = ()
        DVE_HEADS = (0, 2, 4, 5, 7)

        for g in range(NG):
            P.dma("sp", xg[:], xview(xin_d, g), xg, xin_d)
            emit_norm(P, C, xg, A, mod, hT, hTb, sqrot, PS["S"], rstd)
            for ns in range(16):
                wfp = wfpr.next()
                P.dma("sp", wfp[:], wf_d[:, :, ns * 128:(ns + 1) * 128], wfp, wf_d)
                for tt in range(4):
                    ps = psA.next()
                    for k in range(8):
                        P.op("pe", "matmul", [hT, wfp], [ps], ps[:, 0:128], lhsT=hT[:, k, tt * 128:(tt + 1) * 128],
                             rhs=wfp[:, k, :], start=(k == 0), stop=(k == 7), inc=(k == 7))
                    dst = s_sb[tt][:, ns, :]
                    if ev % 2 == 0:
                        P.op("act", "activation", [ps], [s_sb[tt]], out=dst, in_=ps[:, 0:128], func=AF.Copy)
                    else:
                        P.op("dve", "tensor_copy", [ps], [s_sb[tt]], out=dst, in_=ps[:, 0:128])
                    ev += 1
            for tt in range(4):
                s_ = s_sb[tt]
                for blk in range(16):
                    P.op("dve", "max", [s_], [vtop], out=vtop[:, blk, 0:8], in_=s_[:, blk, :])
                    P.op("dve", "match_replace", [vtop, s_], [tmp128], out=tmp128[:], in_to_replace=vtop[:, blk, 0:8],
                         in_values=s_[:, blk, :], imm_value=-1e30)
                    P.op("dve", "max", [tmp128], [vtop], out=vtop[:, blk, 8:16], in_=tmp128[:])
                P.op("dve", "scalar_tensor_tensor", [vtop], [mneg], out=mneg[:], in0=vtop[:, 0:16:2, 0], scalar=-1.0,
                     in1=vtop[:, 1:16:2, 0], op0=ALU.mult, op1=ALU.subtract)
                for h in range(8):
                    P.op("act", "activation", [vtop, mneg], [etop], out=etop[:, 2 * h, :], in_=vtop[:, 2 * h, :],
                         func=AF.Exp, bias=mneg[:, h:h + 1], scale=1.0)
                    P.op("act", "activation", [s_, mneg], [s_], out=s_[:, 2 * h, :], in_=s_[:, 2 * h, :], func=AF.Exp,
                         bias=mneg[:, h:h + 1], scale=1.0)
                P.op("act", "activation", [vtop], [etop], out=etop[:, 1:16:2, :], in_=vtop[:, 1:16:2, :], func=AF.Exp)
                P.op("act", "activation", [s_], [s_], out=s_[:, 1:16:2, :], in_=s_[:, 1:16:2, :], func=AF.Exp)
                for h in range(8):
                    in0 = etop[:, 2 * h, :].unsqueeze(2).to_broadcast([128, 16, 16])
                    in1 = etop[:, 2 * h + 1, :].unsqueeze(1).to_broadcast([128, 16, 16])
                    cand = candr.next()
                    P.op("dve", "tensor_tensor", [etop], [cand],
                         out=cand[:].rearrange("p (a b) -> p a b", a=16), in0=in0, in1=in1, op=ALU.mult)
                    P.op("dve", "max", [cand], [ttop], out=ttop[:, h, 0:8], in_=cand[:])
                    P.op("dve", "match_replace", [ttop, cand], [ctmp], out=ctmp[:], in_to_replace=ttop[:, h, 0:8],
                         in_values=cand[:], imm_value=-1.0)
                    P.op("dve", "max", [ctmp], [ttop], out=ttop[:, h, 8:16], in_=ctmp[:])
                P.op("dve", "tensor_reduce", [ttop], [Zs], out=Zs[:], in_=ttop[:, :, 0:16], axis=AX.X, op=ALU.add)
                P.op("dve", "reciprocal", [Zs], [Zs], out=Zs[:], in_=Zs[:])
                P.op("dve", "scalar_tensor_tensor", [ttop, Zs], [thr[tt]], out=thr[tt][:], in0=ttop[:, :, 15],
                     scalar=1.0 - 2e-6, in1=Zs[:], op0=ALU.mult, op1=ALU.mult)
                for h in range(8):
                    P.op("pool", "tensor_scalar", [s_, Zs], [s_], out=s_[:, 2 * h, :], in0=s_[:, 2 * h, :],
                         scalar1=Zs[:, h:h + 1], scalar2=None, op0=ALU.mult)
                    P.op("pool", "tensor_scalar", [ident, thr[tt]], [dg], out=dg[:, tt, h, :], in0=ident[:],
                         scalar1=thr[tt][:, h:h + 1], scalar2=None, op0=ALU.mult)

            sched = []

            def at(t, fn):
                sched.append((t, len(sched), fn))

            NIC = 16
            for step in range(NIC + 2):
                for tt in range(4):
                    n0 = (step * 4 + tt) * 8
                    if step < NIC:
                        ic = step
                        s_ = s_sb[tt]
                        for h in range(8):
                            E, M = Er.next(), Mr.next()
                            in0 = s_[:, 2 * h, ic * 8:(ic + 1) * 8].unsqueeze(2).to_broadcast([128, 8, 128])
                            in1 = s_[:, 2 * h + 1, :].unsqueeze(1).to_broadcast([128, 8, 128])
                            eng = "dve" if h in DVE_HEADS else "pool"
                            at(n0 + h - 2, lambda eng=eng, s_=s_, E=E, in0=in0, in1=in1: P.op(
                                eng, "tensor_tensor", [s_], [E], out=E[:], in0=in0, in1=in1, op=ALU.mult))
                            Kb = Kr.next()
                            at(n0 + h - 1, lambda E=E, M=M, tt=tt, h=h: P.op(
                                "dve", "tensor_scalar", [E, thr[tt]], [M], out=M[:], in0=E[:], scalar1=thr[tt][:, h:h + 1],
                                scalar2=0.0, op0=ALU.subtract, op1=ALU.max))
                            at(n0 + h, lambda M=M, Kb=Kb: P.op("act", "activation", [M], [Kb], out=Kb[:], in_=M[:],
                                                                func=AF.Sign))

                            def acc(M=M, Kb=Kb, h=h, tt=tt):
                                for hb in range(2):
                                    P.op("pe", "matmul", [ident, M], [wacc], wacc[:, hb * 512:(hb + 1) * 512], lhsT=ident[:],
                                         rhs=M[:, hb * 4:(hb + 1) * 4, :].rearrange("p a b -> p (a b)"), start=(h == 0),
                                         stop=False, inc=False)
                                for hb in range(2):
                                    P.op("pe", "matmul", [dg, Kb], [wacc], wacc[:, hb * 512:(hb + 1) * 512],
                                         lhsT=dg[:, tt, h, :], rhs=Kb[:, hb * 4:(hb + 1) * 4, :].rearrange("p a b -> p (a b)"),
                                         start=False, stop=(h == 7), inc=(hb == 1))
                            at(n0 + h + 1, acc)
                        Wt = Wtr.next()
                        at(n0 + 9, lambda Wt=Wt: P.op("act", "activation", [wacc], [Wt],
                                                     out=Wt[:].rearrange("p a b -> p (a b)"), in_=wacc[:], func=AF.Copy))

                        def tr(Wt=Wt):
                            for i in range(8):
                                P.op("pe", "matmul", [Wt, ident], [wtp], wtp[:, i * 128:(i + 1) * 128], lhsT=Wt[:, i, :],
                                     rhs=ident[:], start=True, stop=True, inc=(i == 7))
                        at(n0 + 11, tr)
                        at(n0 + 12, lambda ic=ic, tt=tt: P.op(
                            "act", "activation", [wtp], [wTs[ic % 2]], out=wTs[ic % 2][:, :, tt * 128:(tt + 1) * 128],
                            in_=wtp[:].rearrange("p (a b) -> p a b", a=8), func=AF.Copy))
                    if 1 <= step <= NIC:
                        ic = step - 1
                        for q_, i in enumerate((2 * tt, 2 * tt + 1)):
                            e = ic * 8 + i
                            Vc, Uc, a_ps, Gt, WA = Vr[e % NV], Ur.next(), psA.next(), Gr.next(), WAr[e % NV]
                            ta = n0 + 4 * q_ + 3
                            at(max(step * 32, ta - 12), lambda Vc=Vc, e=e: P.dma("pool", Vc[:], W["V"][e * 128:(e + 1) * 128, :], Vc, W["V"]))
                            at(ta - 6, lambda Uc=Uc, e=e: P.dma("pool", Uc[:], W["UT"][e], Uc, W["UT"]))

                            def amm(Uc=Uc, a_ps=a_ps):
                                for k in range(8):
                                    P.op("pe", "matmul", [Uc, hTb], [a_ps], a_ps[:], lhsT=Uc[:, k, :], rhs=hTb[:, k, :],
                                         start=(k == 0), stop=(k == 7), inc=(k == 7))
                            at(ta, amm)
                            at(ta + 1, lambda a_ps=a_ps, Gt=Gt: P.op("act", "activation", [a_ps], [Gt], out=Gt[:],
                                                                      in_=a_ps[:], func=AF.Gelu))
                            at(ta + 2, lambda Gt=Gt, WA=WA, ic=ic, i=i: P.op(
                                "dve", "tensor_tensor", [Gt, wTs[ic % 2]], [WA], out=WA[:], in0=Gt[:],
                                in1=wTs[ic % 2][:, i, :], op=ALU.mult))
                    if 2 <= step <= NIC + 1:
                        ic = step - 2
                        for q_, dc in enumerate((2 * tt, 2 * tt + 1)):
                            o_ps, ot = psO.next(), otr.next()
                            tv = n0 + 4 * q_ + 2

                            def vmm(ic=ic, dc=dc, o_ps=o_ps):
                                for i in range(8):
                                    e = ic * 8 + i
                                    P.op("pe", "matmul", [Vr[e % NV], WAr[e % NV]], [o_ps], o_ps[:],
                                         lhsT=Vr[e % NV][:, dc * 128:(dc + 1) * 128], rhs=WAr[e % NV][:], start=(i == 0),
                                         stop=(i == 7), inc=(i == 7))
                            at(tv, vmm)
                            at(tv + 1, lambda o_ps=o_ps, ot=ot, dc=dc: P.op(
                                "act", "activation", [o_ps, mod], [ot], out=ot[:], in_=o_ps[:], func=AF.Copy,
                                scale=mod[:, 16 + dc:17 + dc]))
                            at(tv + 3, lambda ot=ot, dc=dc: P.op("pool", "tensor_tensor", [ot, xg], [xg], out=xg[:, dc, :],
                                                                 in0=ot[:], in1=xg[:, dc, :], op=ALU.add))
            sched.sort(key=lambda x: (x[0], x[1]))
            for _, _, fn in sched:
                fn()
            P.dma("sp", xview(xout_d, g), xg[:], xout_d, xg)
        P.flush()
        P.recycle()


def alloc_psum(P):
    PS = {}
    PS["A"] = [P.ps([128, 512], F32, f"psA{i}") for i in range(2)]
    PS["O"] = [P.ps([128, 512], F32, f"psO{i}") for i in range(2)]
    PS["W"] = [P.ps([128, 1024], F32, f"psW{i}") for i in range(2)]
    PS["S"] = PS["O"][0]
    return PS


SHAPES = {"c": [128, 8], "flag": [128, 1], "ident": [128, 128],
          "mix_g": [128, 8], "mix_mod_w": [128, 8, 3072], "mix_mod_b": [128, 24],
          "in_w": [2 * NB, 128, 8, BS], "conv_w": [BS, NB, 4], "conv_b": [BS, NB],
          "ra_w": [NB, BS, BS], "ra_b": [BS, NB], "ri_w": [NB, BS, BS], "ri_b": [BS, NB],
          "lam": [BS, NB], "out_w": [8, BS, NB, 128],
          "ffn_g": [128, 8], "ffn_mod_w": [128, 8, 3072], "ffn_mod_b": [128, 24],
          "qwT": [16, 128, D], "skT": [16, 128, 128], "UT": [128, 128, 8, 128], "V": [NE, D]}


class WMap(dict):
    GLOBAL = ("c", "flag", "ident", "bd", "maskneg")

    def __init__(self, nc, prefix="", shared=None):
        super().__init__()
        self.nc = nc
        self.prefix = prefix
        self.shared = shared if shared is not None else {}

    def __missing__(self, k):
        if k in self.GLOBAL:
            if k not in self.shared:
                self.shared[k] = din(self.nc, k, SHAPES[k])
            b = self.shared[k]
        else:
            b = din(self.nc, self.prefix + k, SHAPES[k])
        self[k] = b
        return b

    def names(self):
        return [(k if k in self.GLOBAL else self.prefix + k) for k in self.keys()]


def build_l0(do_lru=True, do_peer=True):
    nc = bass.Bass("TRN2", target_bir_lowering=False)
    W = WMap(nc)
    xin = din(nc, "xT", [D, T])
    xprev = din(nc, "xTp", [D, T]) if do_lru else None
    xout = dout(nc, "xo", [D, T])
    with ExitStack() as st:
        P = Prog(nc, st)
        PS = alloc_psum(P)
        C = emit_consts(P)
        if do_lru and do_peer:
            xmid = P.dram("xmid", [D, T], F32)
        else:
            xmid = xout
        if do_lru:
            emit_lru(P, C, PS, W, xin, xprev, xmid)
        if do_peer:
            emit_peer(P, C, PS, W, xmid if do_lru else xin, xout, "l0")
        P.flush([xout])
    nc.used_inputs = ["xT"] + (["xTp"] if do_lru else []) + W.names()
    return nc


def col8(v):
    return np.ascontiguousarray(v.reshape(-1, 128).T)


def blk16(v):
    return np.ascontiguousarray(v.reshape(NB, BS).T)


def lay_mod(g, w, b):
    return col8(g), np.ascontiguousarray(w.reshape(128, 8, 3072)), col8(b)


def lay_peer(q_w, sk1, sk2, u, v):
    qwT = np.ascontiguousarray(q_w.T.reshape(16, 128, D))
    skT = np.empty((16, 128, 128), np.float32)
    skT[0::2] = sk1.transpose(0, 2, 1)
    skT[1::2] = sk2.transpose(0, 2, 1)
    UT = np.ascontiguousarray(u.reshape(128, 128, 8, 128).transpose(0, 3, 2, 1))
    return qwT, skT, UT, np.ascontiguousarray(v)


def lay_l0(inp):
    d = {}
    d["mix_g"], d["mix_mod_w"], d["mix_mod_b"] = lay_mod(inp["l0_mix_norm_g"], inp["l0_mix_mod_w"], inp["l0_mix_mod_b"])
    d["ffn_g"], d["ffn_mod_w"], d["ffn_mod_b"] = lay_mod(inp["l0_ffn_norm_g"], inp["l0_ffn_mod_w"], inp["l0_ffn_mod_b"])
    iw = inp["l0_lru_in_w"].reshape(8, 128, 2 * NB, BS)
    d["in_w"] = np.ascontiguousarray(iw.transpose(2, 1, 0, 3))
    d["conv_w"] = np.ascontiguousarray(inp["l0_lru_conv_w"].reshape(4, NB, BS).transpose(2, 1, 0))
    d["conv_b"] = blk16(inp["l0_lru_conv_b"])
    d["ra_w"] = np.ascontiguousarray(inp["l0_lru_ra_w"])
    d["ri_w"] = np.ascontiguousarray(inp["l0_lru_ri_w"])
    d["ra_b"] = blk16(inp["l0_lru_ra_b"])
    d["ri_b"] = blk16(inp["l0_lru_ri_b"])
    d["lam"] = blk16(inp["l0_lru_lambda"])
    ow = inp["l0_lru_out_w"].reshape(NB, BS, 8, 128)
    d["out_w"] = np.ascontiguousarray(ow.transpose(2, 1, 0, 3))
    d["qwT"], d["skT"], d["UT"], d["V"] = lay_peer(inp["l0_peer_q_w"], inp["l0_peer_subkey1"], inp["l0_peer_subkey2"],
                                                  inp["l0_peer_u"], inp["l0_peer_v"])
    d["ident"] = np.eye(128, dtype=np.float32)
    return d


def core_maps_l0(inp, shared, cores):
    maps = []
    x = inp["x"]
    for c in cores:
        b, hf = c // 2, c % 2
        m = dict(shared)
        m["xT"] = np.ascontiguousarray(x[b, hf * T:(hf + 1) * T, :].T)
        m["xTp"] = np.ascontiguousarray(x[b, 0:T, :].T) if hf else np.zeros((D, T), np.float32)
        m["flag"] = np.full((128, 1), float(hf), np.float32)
        m["c"] = np.ascontiguousarray(inp["c"][b].reshape(128, 8))
        maps.append(m)
    return maps


SHAPES.update({"wqk": [16, 128, 8, 128], "wv": [2, 128, 8, 512], "wf": [128, 8, 16], "wog": [8, 128, 8, 128],
               "fb": [16, 1], "gq": [128, 1], "gk": [128, 1], "bd": [128, 128],
               "fox_out_w": [8, 128, 8, 128], "maskneg": [4, 128, 512]})
S4 = 4096
PAIRS = [[0, 1], [2, 3], [4, 5], [6, 7]]


def emit_fox_b(P, C, PS, W, xin, qk_o, v_o, lf_o, og_o):
    with ExitStack() as st:
        A, mod = emit_mod(P, PS["A"][0], W["c"], W["mix_g"], W["mix_mod_w"], W["mix_mod_b"], "fm", st)
        bd = P.sb([128, 128], F32, "bd", st)
        P.dma("sp", bd[:], W["bd"][:], bd, W["bd"])
        gq = P.sb([128, 1], F32, "gq", st)
        gk = P.sb([128, 1], F32, "gk", st)
        nfb = P.sb([16, 1], F32, "nfb", st)
        P.dma("sp", gq[:], W["gq"][:], gq, W["gq"])
        P.dma("sp", gk[:], W["gk"][:], gk, W["gk"])
        P.dma("sp", nfb[:], W["fb"][:], nfb, W["fb"])
        P.op("dve", "tensor_scalar", [nfb], [nfb], out=nfb[:], in0=nfb[:], scalar1=-1.0, scalar2=None, op0=ALU.mult)
        xg = P.sb([128, 8, G], F32, "xg", st)
        hT = P.sb([128, 8, G], F32, "hT", st)
        hTb = P.sb([128, 8, G], BF16, "hTb", st)
        sqrot = Rot([P.sb([128, G], F32, f"sq{i}", st) for i in range(2)])
        rstd = P.sb([128, G], F32, "rstd", st)
        wr = Rot([P.sb([128, 8, 128], BF16, f"w{i}", st) for i in range(4)])
        wvr = Rot([P.sb([128, 8, 512], BF16, f"wv{i}", st) for i in range(2)])
        wfs = P.sb([128, 8, 16], F32, "wfs", st)
        P.dma("sp", wfs[:], W["wf"][:], wfs, W["wf"])
        q2r = Rot([P.sb([128, G], F32, f"q2{i}", st) for i in range(2)])
        rsr = Rot([P.sb([128, G], F32, f"rs{i}", st) for i in range(2)])
        qnr = Rot([P.sb([128, G], F32, f"qn{i}", st) for i in range(3)])
        vsr = Rot([P.sb([128, 512], F32, f"vs{i}", st) for i in range(3)])
        lfr = Rot([P.sb([16, G], F32, f"lf{i}", st) for i in range(2)])
        psA, psO, psW = Rot(PS["A"]), Rot(PS["O"]), Rot(PS["W"])
        for g in range(NG):
            gs = slice(g * G, (g + 1) * G)
            P.dma("sp", xg[:], xview(xin, g), xg, xin)
            emit_norm(P, C, xg, A, mod, hT, hTb, sqrot, PS["S"], rstd)
            for j in range(16):
                w = wr.next()
                P.dma("pool", w[:], W["wqk"][j], w, W["wqk"])
                ps = psA.next()
                for k in range(8):
                    P.op("pe", "matmul", [w, hTb], [ps], ps[:], lhsT=w[:, k, :], rhs=hTb[:, k, :], start=(k == 0),
                         stop=(k == 7), inc=(k == 7))
                q2, rs, qn = q2r.next(), rsr.next(), qnr.next()
                P.op("act", "activation", [ps], [q2], out=q2[:], in_=ps[:], func=AF.Square)
                pw = psW.next()
                P.op("pe", "matmul", [bd, q2], [pw], pw[:, 0:G], lhsT=bd[:], rhs=q2[:], start=True, stop=True)
                P.op("dve", "tensor_scalar", [pw], [rs], out=rs[:], in0=pw[:, 0:G], scalar1=1.0 / 64, scalar2=1e-6,
                     op0=ALU.mult, op1=ALU.add)
                P.op("act", "activation", [rs], [rs], out=rs[:], in_=rs[:], func=AF.Sqrt)
                P.op("dve", "reciprocal", [rs], [rs], out=rs[:], in_=rs[:])
                gcol = gq if j < 8 else gk
                P.op("dve", "scalar_tensor_tensor", [ps, gcol, rs], [qn], out=qn[:], in0=ps[:], scalar=gcol[:, 0:1],
                     in1=rs[:], op0=ALU.mult, op1=ALU.mult)
                P.dma("sp", qk_o[j // 2][(j % 2) * 128:(j % 2 + 1) * 128, gs], qn[:], qk_o[j // 2], qn)
            for n2 in range(2):
                wv = wvr.next()
                P.dma("pool", wv[:], W["wv"][n2], wv, W["wv"])
                for tt in range(4):
                    ps = psO.next()
                    for k in range(8):
                        P.op("pe", "matmul", [wv, hTb], [ps], ps[:], lhsT=hTb[:, k, tt * 128:(tt + 1) * 128],
                             rhs=wv[:, k, :], start=(k == 0), stop=(k == 7), inc=(k == 7))
                    vs = vsr.next()
                    P.op("act", "activation", [ps], [vs], out=vs[:], in_=ps[:], func=AF.Copy)
                    P.dma("sp", v_o[g][tt * 128:(tt + 1) * 128, n2 * 512:(n2 + 1) * 512], vs[:], v_o[g], vs)
            ps = psA.next()
            for k in range(8):
                P.op("pe", "matmul", [wfs, hT], [ps], ps[0:16, :], lhsT=wfs[:, k, :], rhs=hT[:, k, :], start=(k == 0),
                     stop=(k == 7), inc=(k == 7))
            lf = lfr.next()
            P.op("act", "activation", [ps, nfb], [lf], out=lf[:], in_=ps[0:16, :], func=AF.Exp, scale=-1.0,
                 bias=nfb[:, 0:1])
            P.op("act", "activation", [lf], [lf], out=lf[:], in_=lf[:], func=AF.Ln, bias=1.0, scale=1.0)
            P.op("dve", "tensor_scalar", [lf], [lf], out=lf[:], in0=lf[:], scalar1=-1.0, scalar2=None, op0=ALU.mult)
            P.dma("sp", lf_o[:, gs], lf[:], lf_o, lf)
            for j in range(8):
                w = wr.next()
                P.dma("pool", w[:], W["wog"][j], w, W["wog"])
                ps = psA.next()
                for k in range(8):
                    P.op("pe", "matmul", [w, hTb], [ps], ps[:], lhsT=w[:, k, :], rhs=hTb[:, k, :], start=(k == 0),
                         stop=(k == 7), inc=(k == 7))
                qn = qnr.next()
                P.op("act", "activation", [ps], [qn], out=qn[:], in_=ps[:], func=AF.Sigmoid)
                P.dma("sp", og_o[j * 128:(j + 1) * 128, gs], qn[:], og_o, qn)
        P.flush()
        P.recycle()


def emit_blend(P, dst_ap, dst, c0_ap, c1_ap, srcbuf, t0, t1, flag, nflag, np_):
    P.dma("sp", t0[0:np_], c0_ap, t0, srcbuf)
    P.dma("act", t1[0:np_], c1_ap, t1, srcbuf)
    P.op("pool", "tensor_scalar", [t1, flag], [t1], out=t1[0:np_], in0=t1[0:np_], scalar1=flag[0:np_, 0:1], scalar2=None,
         op0=ALU.mult)
    P.op("dve", "scalar_tensor_tensor", [t0, nflag, t1], [dst], out=dst_ap, in0=t0[0:np_], scalar=nflag[0:np_, 0:1],
         in1=t1[0:np_], op0=ALU.mult, op1=ALU.add)


def emit_blend2(P, dst_ap, dst, c0_ap, b0, c1_ap, b1, t0, t1, flag, nflag, np_):
    P.dma("sp", t0[0:np_], c0_ap, t0, b0)
    P.dma("act", t1[0:np_], c1_ap, t1, b1)
    P.op("pool", "tensor_scalar", [t1, flag], [t1], out=t1[0:np_], in0=t1[0:np_], scalar1=flag[0:np_, 0:1], scalar2=0.0,
         op0=ALU.mult, op1=ALU.add)
    P.op("dve", "scalar_tensor_tensor", [t0, nflag, t1], [dst], out=dst_ap, in0=t0[0:np_], scalar=nflag[0:np_, 0:1],
         in1=t1[0:np_], op0=ALU.mult, op1=ALU.add)


def emit_fox_c(P, C, PS, W, qk_g, v_g, lf_g, o_s):
    with ExitStack() as st:
        flag = P.sb([128, 1], F32, "flag", st)
        nflag = P.sb([128, 1], F32, "nflag", st)
        P.dma("sp", flag[:], W["flag"][:], flag, W["flag"])
        P.op("dve", "tensor_scalar", [flag], [nflag], out=nflag[:], in0=flag[:], scalar1=-1.0, scalar2=1.0,
             op0=ALU.mult, op1=ALU.add)
        masks = P.sb([128, 4, 512], F32, "masks", st)
        P.dma("sp", masks[:], W["maskneg"].t.rearrange("j p t -> p j t"), masks, W["maskneg"])
        t0 = P.sb([128, T], F32, "bt0", st)
        t1 = P.sb([128, T], F32, "bt1", st)
        rows = {nm: (P.sb([8, S4], BF16, nm + "hi", st), P.sb([8, S4], BF16, nm + "lo", st)) for nm in ("cs", "ncs")}
        sp_ = ExitStack()
        ones8 = P.sb([8, S4], F32, "ones8", sp_)
        P.op("pool", "memset", [], [ones8], ones8[:], 1.0)
        lfa = P.sb([8, S4], F32, "lfa", sp_)
        lfb = P.sb([8, S4], F32, "lfb", sp_)
        for r in range(2):
            rs_ = slice(r * T, (r + 1) * T)
            emit_blend(P, lfa[:, rs_], lfa, lf_g[r * 16:r * 16 + 8, :], lf_g[r * 16 + 8:r * 16 + 16, :], lf_g, t0, t1,
                       flag, nflag, 8)
        P.op("dve", "tensor_tensor_scan", [ones8, lfa], [lfb], out=lfb[:], data0=ones8[:], data1=lfa[:], initial=0.0,
             op0=ALU.mult, op1=ALU.add)
        P.op("dve", "tensor_scalar", [lfb], [lfa], out=lfa[:], in0=lfb[:], scalar1=8.0, scalar2=0.0, op0=ALU.mult, op1=ALU.add)
        P.op("dve", "tensor_scalar", [lfb], [lfb], out=lfb[:], in0=lfb[:], scalar1=-8.0, scalar2=0.0, op0=ALU.mult, op1=ALU.add)
        cs8, ncs8 = lfa, lfb
        for nm, src in (("cs", cs8), ("ncs", ncs8)):
            hi, lo = rows[nm]
            P.op("act", "activation", [src], [hi], out=hi[:], in_=src[:], func=AF.Copy)
            P.op("dve", "tensor_tensor", [src, hi], [lo], out=lo[:], in0=src[:], in1=hi[:], op=ALU.subtract)
        P.flush()
        sp_.close()
        qar = Rot([(P.sb([128, S4], BF16, f"qah{i}", st), P.sb([128, S4], BF16, f"qal{i}", st)) for i in range(2)])
        kar = Rot([(P.sb([128, S4], BF16, f"kah{i}", st), P.sb([128, S4], BF16, f"kal{i}", st)) for i in range(2)])
        bl = P.sb([128, T], F32, "bl", st)
        for pair in qar.bufs + kar.bufs:
            for xx in pair:
                P.op("pool", "memset", [], [xx], xx[64:128, :], 0.0)
        vf = P.sb([128, 16, 64], F32, "vf", st)
        var_ = Rot([P.sb([128, 32, 65], BF16, f"va{i}", st) for i in range(2)])
        ptr = Rot([P.sb([128, 512], BF16, f"pt{i}", st) for i in range(4)])
        tmr = Rot([P.sb([128, 512], F32, f"tm{i}", st) for i in range(2)])
        rden = P.sb([128, 512], F32, "rden", st)
        bcs = P.sb([64, 512], F32, "bcs", st)
        osr = Rot([P.sb([64, 512], F32, f"os{i}", st) for i in range(2)])
        psS = Rot([PS["A"][0], PS["A"][1], PS["W"][0]])
        psBC = PS["W"][1]
        psO = Rot(PS["O"])
        v3 = [vg.t.rearrange("(r kt p) c -> p r kt c", r=2, p=128) for vg in v_g]

        def load_head(h, qa, ka, va):
            for r in range(2):
                rs_ = slice(r * T, (r + 1) * T)

                def cand(base_j, hg):
                    j = base_j + hg * 4 + h // 2
                    off = r * 256 + (j % 2) * 128 + (h % 2) * 64
                    return qk_g[j // 2], qk_g[j // 2][off:off + 64, :]
                for base_j, (xh, xl) in ((0, qa), (8, ka)):
                    (b0_, a0_), (b1_, a1_) = cand(base_j, 0), cand(base_j, 1)
                    emit_blend2(P, bl[0:64, :], bl, a0_, b0_, a1_, b1_, t0, t1, flag, nflag, 64)
                    P.op("act", "activation", [bl], [xh], out=xh[0:64, rs_], in_=bl[0:64, :], func=AF.Copy)
                    P.op("dve", "tensor_tensor", [bl, xh], [xl], out=xl[0:64, rs_], in0=bl[0:64, :], in1=xh[0:64, rs_],
                         op=ALU.subtract)
                    yield
                c0, c1 = h * 64, 512 + h * 64
                t1v = t1[:, 0:1024].rearrange("p (a b) -> p a b", a=16)
                for i in range(4):
                    P.dma("sp", vf[:, i * 4:(i + 1) * 4, :], v3[i][:, r, :, c0:c0 + 64], vf, v_g[i])
                    P.dma("act", t1v[:, i * 4:(i + 1) * 4, :], v3[i][:, r, :, c1:c1 + 64], t1, v_g[i])
                P.op("pool", "tensor_scalar", [t1, flag], [t1], out=t1[:, 0:1024], in0=t1[:, 0:1024], scalar1=flag[:, 0:1],
                     scalar2=0.0, op0=ALU.mult, op1=ALU.add)
                P.op("dve", "scalar_tensor_tensor", [vf, nflag, t1], [va], out=va[:, r * 16:(r + 1) * 16, 0:64],
                     in0=vf[:], scalar=nflag[:, 0:1], in1=t1v, op0=ALU.mult, op1=ALU.add)
                yield
            for xx, val in ((qa[0], 1.0), (qa[1], 0.0), (ka[0], 1.0), (ka[1], 0.0)):
                P.op("pool", "memset", [], [xx], xx[64:66, :], val)
            P.dma("sp", qa[0][64:65, :], rows["cs"][0][h:h + 1, :], qa[0], rows["cs"][0])
            P.dma("sp", qa[1][64:65, :], rows["cs"][1][h:h + 1, :], qa[1], rows["cs"][1])
            P.dma("sp", ka[0][65:66, :], rows["ncs"][0][h:h + 1, :], ka[0], rows["ncs"][0])
            P.dma("sp", ka[1][65:66, :], rows["ncs"][1][h:h + 1, :], ka[1], rows["ncs"][1])
            P.op("pool", "memset", [], [va], va[:, :, 64:65], 1.0)
            yield

        bufs = [(qar.next(), kar.next(), var_.next()) for _ in range(8)]
        for _ in load_head(0, *bufs[0]):
            pass
        for h in range(8):
            qa, ka, va = bufs[h]
            nxt = load_head(h + 1, *bufs[h + 1]) if h + 1 < 8 else iter(())
            items = [(qc, kt) for qc in range(8) for kt in range(4 * qc + 4)]
            LAG = 2
            pend = []
            o_cur = {}

            def qk(qc, kt):
                qs = slice(qc * 512, (qc + 1) * 512)
                ps = psS.next()
                terms = ((ka[0], qa[0]), (ka[0], qa[1]), (ka[1], qa[0]))
                for ti, (kk, qq) in enumerate(terms):
                    P.op("pe", "matmul", [kk, qq], [ps], ps[:, 0:512], lhsT=kk[:, kt * 128:(kt + 1) * 128], rhs=qq[:, qs],
                         start=(ti == 0), stop=(ti == 2), inc=(ti == 2))
                pt = ptr.next()
                if kt >= 4 * qc:
                    tm = tmr.next()
                    P.op("dve", "tensor_tensor", [ps, masks], [tm], out=tm[:], in0=ps[:, 0:512], in1=masks[:, kt - 4 * qc, :],
                         op=ALU.add)
                    P.op("act", "activation", [tm], [pt], out=pt[:], in_=tm[:], func=AF.Exp, scale=0.125)
                else:
                    P.op("act", "activation", [ps], [pt], out=pt[:], in_=ps[:, 0:512], func=AF.Exp, scale=0.125)
                return pt

            def pv(qc, kt, pt):
                nk = 4 * qc + 4
                if kt == 0:
                    o_cur[qc] = psO.next()
                o_ps = o_cur[qc]
                P.op("pe", "matmul", [va, pt], [o_ps], o_ps[0:65, :], lhsT=va[:, kt, :], rhs=pt[:], start=(kt == 0),
                     stop=(kt == nk - 1), inc=True)
                if kt == nk - 1:
                    qs = slice(qc * 512, (qc + 1) * 512)
                    P.op("dve", "reciprocal", [o_ps], [rden], out=rden[64:65, :], in_=o_ps[64:65, :])
                    P.op("pe", "matmul", [C["ones"], rden], [psBC], psBC[0:64, 0:512], lhsT=C["ones"][64:65, 0:64],
                         rhs=rden[64:65, :], start=True, stop=True)
                    P.op("act", "activation", [psBC], [bcs], out=bcs[:], in_=psBC[0:64, 0:512], func=AF.Copy)
                    os_ = osr.next()
                    P.op("dve", "tensor_tensor", [o_ps, bcs], [os_], out=os_[:], in0=o_ps[0:64, :], in1=bcs[:], op=ALU.mult)
                    P.dma("sp", o_s[h // 2][(h % 2) * 64:(h % 2 + 1) * 64, qs], os_[:], o_s[h // 2], os_)
                    next(nxt, None)

            for (qc, kt) in items:
                pend.append((qc, kt, qk(qc, kt)))
                if len(pend) > LAG:
                    pv(*pend.pop(0))
            while pend:
                pv(*pend.pop(0))
            for _ in nxt:
                pass
        P.flush()
        P.recycle()


def emit_fox_d(P, C, PS, W, xin, o_g, og_i, xmid):
    with ExitStack() as s1:
        A, mod = emit_mod(P, PS["A"][0], W["c"], W["mix_g"], W["mix_mod_w"], W["mix_mod_b"], "dm", s1)
        flag = P.sb([128, 1], F32, "flag", s1)
        nflag = P.sb([128, 1], F32, "nflag", s1)
        P.dma("sp", flag[:], W["flag"][:], flag, W["flag"])
        P.op("dve", "tensor_scalar", [flag], [nflag], out=nflag[:], in0=flag[:], scalar1=-1.0, scalar2=1.0,
             op0=ALU.mult, op1=ALU.add)
        xg = P.sb([128, 8, G], F32, "xg", s1)
        og = P.sb([128, 8, G], F32, "og", s1)
        ogb = P.sb([128, 8, G], BF16, "ogb", s1)
        ot0 = P.sb([128, 8, G], F32, "ot0", s1)
        ot1 = P.sb([128, 8, G], F32, "ot1", s1)
        owr = Rot([P.sb([128, 8, 128], BF16, f"ow{i}", s1) for i in range(3)])
        psO = Rot(PS["O"])
        for g in range(NG):
            P.dma("sp", xg[:], xview(xin, g), xg, xin)
            P.dma("sp", og[:], xview(og_i, g), og, og_i)
            for k in range(8):
                og_k = o_g[k % 4][(k // 4) * 128:(k // 4 + 1) * 128, :]
                P.dma("sp", ot0[:, k, :], og_k[:, g * G:(g + 1) * G], ot0, o_g[k % 4])
                P.dma("act", ot1[:, k, :], og_k[:, T + g * G:T + (g + 1) * G], ot1, o_g[k % 4])
            P.op("pool", "tensor_scalar", [ot1, flag], [ot1], out=ot1[:], in0=ot1[:], scalar1=flag[:, 0:1], scalar2=0.0,
                 op0=ALU.mult, op1=ALU.add)
            P.op("dve", "scalar_tensor_tensor", [ot0, nflag, ot1], [ot0], out=ot0[:], in0=ot0[:], scalar=nflag[:, 0:1],
                 in1=ot1[:], op0=ALU.mult, op1=ALU.add)
            P.op("dve", "tensor_tensor", [og, ot0], [ogb], out=ogb[:], in0=og[:], in1=ot0[:], op=ALU.mult)
            for dc in range(8):
                ow = owr.next()
                P.dma("pool", ow[:], W["fox_out_w"][dc], ow, W["fox_out_w"])
                o_ps = psO.next()
                for k in range(8):
                    P.op("pe", "matmul", [ow, ogb], [o_ps], o_ps[:], lhsT=ow[:, k, :], rhs=ogb[:, k, :], start=(k == 0),
                         stop=(k == 7), inc=(k == 7))
                P.op("dve", "scalar_tensor_tensor", [o_ps, mod, xg], [xg], out=xg[:, dc, :], in0=o_ps[:],
                     scalar=mod[:, 16 + dc:17 + dc], in1=xg[:, dc, :], op0=ALU.mult, op1=ALU.add)
            P.dma("sp", xview(xmid, g), xg[:], xmid, xg)
        P.flush()
        P.recycle()


def build_fused(ncores=NCORES, do_l0=True, do_peer1=True):
    nc = bass.Bass("TRN2", target_bir_lowering=False)
    PAIRS = [[2 * i, 2 * i + 1] for i in range(ncores // 2)]
    shared = {}
    W0 = WMap(nc, "l0_", shared)
    W1 = WMap(nc, "l1_", shared)
    xin = din(nc, "xT", [D, T])
    xprev = din(nc, "xTp", [D, T])
    xout = dout(nc, "xo", [D, T])
    with ExitStack() as st:
        P = Prog(nc, st)
        PS = alloc_psum(P)
        C = emit_consts(P)
        xmid0 = P.dram("xmid0", [D, T])
        x1 = P.dram("x1", [D, T])
        qk_s = [P.dram(f"qk_s{i}", [256, T]) for i in range(8)]
        qk_g = [P.dram(f"qk_g{i}", [512, T]) for i in range(8)]
        v_s = [P.dram(f"v_s{i}", [512, D]) for i in range(4)]
        v_g = [P.dram(f"v_g{i}", [1024, D]) for i in range(4)]
        lf_s = P.dram("lf_s", [16, T])
        lf_g = P.dram("lf_g", [2 * 16, T])
        ogs = P.dram("ogs", [D, T])
        o_s = [P.dram(f"o_s{i}", [128, S4]) for i in range(4)]
        o_g = [P.dram(f"o_g{i}", [256, S4]) for i in range(4)]
        xmid1 = P.dram("xmid1", [D, T])
        if do_l0:
            emit_lru(P, C, PS, W0, xin, xprev, xmid0)
            emit_peer(P, C, PS, W0, xmid0, x1, "l0")
        else:
            x1 = xin
        emit_fox_b(P, C, PS, W1, x1, qk_s, v_s, lf_s, ogs)
        P.coll("AllGather", PAIRS, lf_s, lf_g)
        for a, b_ in zip(qk_s + v_s, qk_g + v_g):
            P.coll("AllGather", PAIRS, a, b_)
        emit_fox_c(P, C, PS, W1, qk_g, v_g, lf_g, o_s)
        for a, b_ in zip(o_s, o_g):
            P.coll("AllGather", PAIRS, a, b_)
        if do_peer1:
            emit_fox_d(P, C, PS, W1, x1, o_g, ogs, xmid1)
            emit_peer(P, C, PS, W1, xmid1, xout, "l1")
        else:
            emit_fox_d(P, C, PS, W1, x1, o_g, ogs, xout)
        P.flush([xout])
    nc.used_inputs = ["xT", "xTp"] + W0.names() + [n for n in W1.names() if n not in W0.names()]
    return nc


def lay_fox(inp):
    d = {}
    d["mix_g"], d["mix_mod_w"], d["mix_mod_b"] = lay_mod(inp["l1_mix_norm_g"], inp["l1_mix_mod_w"], inp["l1_mix_mod_b"])
    iw = inp["l1_fox_in_w"]

    def blocks(cols, nblk, w):
        return np.ascontiguousarray(cols.reshape(8, 128, nblk, w).transpose(2, 1, 0, 3))
    d["wqk"] = blocks(iw[:, 0:2048], 16, 128)
    d["wv"] = blocks(iw[:, 2048:3072], 2, 512)
    d["wf"] = np.ascontiguousarray(iw[:, 3072:3088].reshape(8, 128, 16).transpose(1, 0, 2))
    d["wog"] = blocks(iw[:, 3088:4112], 8, 128)
    d["fb"] = np.ascontiguousarray(inp["l1_fox_f_b"].reshape(16, 1))
    d["gq"] = np.ascontiguousarray(np.tile(inp["l1_fox_q_norm_g"], 2).reshape(128, 1))
    d["gk"] = np.ascontiguousarray(np.tile(inp["l1_fox_k_norm_g"], 2).reshape(128, 1))
    d["fox_out_w"] = blocks(inp["l1_fox_out_w"], 8, 128)
    return d


def lay_l1peer(inp):
    d = {}
    d["ffn_g"], d["ffn_mod_w"], d["ffn_mod_b"] = lay_mod(inp["l1_ffn_norm_g"], inp["l1_ffn_mod_w"], inp["l1_ffn_mod_b"])
    d["qwT"], d["skT"], d["UT"], d["V"] = lay_peer(inp["l1_peer_q_w"], inp["l1_peer_subkey1"], inp["l1_peer_subkey2"],
                                                  inp["l1_peer_u"], inp["l1_peer_v"])
    return d


def maskneg():
    m = np.zeros((4, 128, 512), np.float32)
    tk = np.arange(128)[:, None]
    tq = np.arange(512)[None, :]
    for j in range(4):
        m[j] = np.where(tk + j * 128 > tq, -240000.0, 0.0)
    return m


def block_diag_ones():
    bd = np.zeros((128, 128), np.float32)
    bd[0:64, 0:64] = 1.0
    bd[64:128, 64:128] = 1.0
    return bd


_CACHE = {}


def kernel(**inp):
    inp = {k: np.asarray(v, dtype=np.float32) for k, v in inp.items()}
    if "nc" not in _CACHE:
        _CACHE["nc"] = build_fused()
    nc = _CACHE["nc"]
    sh = {}
    l0 = lay_l0(inp)
    l0.pop("ident")
    for k, v in l0.items():
        sh["l0_" + k] = v
    for k, v in {**lay_fox(inp), **lay_l1peer(inp)}.items():
        sh["l1_" + k] = v
    sh["ident"] = np.eye(128, dtype=np.float32)
    sh["bd"] = block_diag_ones()
    sh["maskneg"] = maskneg()
    maps = []
    x = inp["x"]
    for c in range(NCORES):
        b, hf = c // 2, c % 2
        m = dict(sh)
        m["xT"] = np.ascontiguousarray(x[b, hf * T:(hf + 1) * T, :].T)
        m["xTp"] = np.ascontiguousarray(x[b, 0:T, :].T) if hf else np.zeros((D, T), np.float32)
        m["flag"] = np.full((128, 1), float(hf), np.float32)
        m["c"] = np.ascontiguousarray(inp["c"][b].reshape(128, 8))
        maps.append({k: v for k, v in m.items() if k in nc.used_inputs})
    res = run_bass_kernel_spmd(nc, maps, core_ids=list(range(NCORES)))
    out = np.empty((4, S4, D), np.float32)
    for c in range(NCORES):
        b, hf = c // 2, c % 2
        out[b, hf * T:(hf + 1) * T, :] = res.results[c]["xo"].T
    return out
```
